# Optimizing a Trainium2 kernel written in Bass

```python
import math
import jax, jax.numpy as jnp
from jax import lax
import numpy as np

D_MODEL = 1024
BATCH = 8
SEQ = 2048
DEPTH = 4
DEC_BATCH = 2
DEC_SEQ = 8192
PAST_LEN = 128

N_MEM = 256
EPS = 1e-6
NEG_INF = -1e30

ATT_GROUPS = ((128, 1), (512, 4), (2048, 16))
ATT_HEADS_PER_GROUP = 4
ATT_HEADS = ATT_HEADS_PER_GROUP * len(ATT_GROUPS)
ATT_HEAD_DIM = 64
ATT_WIDTH = ATT_HEADS * ATT_HEAD_DIM

MLSTM_HEADS = 4
MLSTM_HEAD_DIM = 192
MLSTM_WIDTH = MLSTM_HEADS * MLSTM_HEAD_DIM
MLSTM_CHUNK = 64

XATT_HEADS = 4
XATT_HEAD_DIM = 192
XATT_WIDTH = XATT_HEADS * XATT_HEAD_DIM

N_BRANCH = 3
BRANCH_WIDTH = 768
CONV_W = 3
FFN_DIM = 2816

IN_SIZES = (ATT_WIDTH, ATT_WIDTH, ATT_WIDTH,
            MLSTM_WIDTH, MLSTM_WIDTH, MLSTM_WIDTH, MLSTM_WIDTH, 4 * MLSTM_HEADS,
            XATT_WIDTH, N_BRANCH * D_MODEL)
IN_DIM = sum(IN_SIZES)

kernel_name = "hybrid_dilated_mlstm_memory_encoder"


def rmsnorm(x, g):
    xf = x.astype(jnp.float32)
    y = xf * lax.rsqrt(jnp.mean(xf * xf, axis=-1, keepdims=True) + EPS)
    return (y * g.astype(jnp.float32)).astype(x.dtype)


def dwconv(x, w, b):
    C = x.shape[-1]
    y = lax.conv_general_dilated(
        x, w[:, None, :].astype(x.dtype), window_strides=(1,),
        padding=((CONV_W // 2, CONV_W // 2),),
        dimension_numbers=("NWC", "WIO", "NWC"), feature_group_count=C)
    return y + b.astype(x.dtype)


def alibi_slopes(n):
    return jnp.exp2(-8.0 * jnp.arange(1, n + 1, dtype=jnp.float32) / n)


def dilated_group_attention(q, k, v, dil, radius, slopes):
    B, S, H, Dh = q.shape
    R = radius
    Lc = S // dil
    nb = -(-Lc // R)
    Lp = nb * R

    def to_classes(t):
        t = t.reshape(B, Lc, dil, H, Dh).transpose(0, 2, 1, 3, 4)
        return jnp.pad(t, ((0, 0), (0, 0), (0, Lp - Lc), (0, 0), (0, 0)))

    def neighbours(t):
        tp = jnp.pad(t, ((0, 0), (0, 0), (R, R), (0, 0), (0, 0))).reshape(B, dil, nb + 2, R, H, Dh)
        return jnp.concatenate([tp[:, :, :-2], tp[:, :, 1:-1], tp[:, :, 2:]], axis=3)

    qb = to_classes(q).reshape(B, dil, nb, R, H, Dh)
    kb = neighbours(to_classes(k))
    vb = neighbours(to_classes(v))

    s = jnp.einsum("bcnqhd,bcnkhd->bcnhqk", qb, kb,
                   preferred_element_type=jnp.float32) * (Dh ** -0.5)
    iq = jnp.arange(R)
    jk = jnp.arange(3 * R)
    blk = jnp.arange(nb)
    delta = jk[None, :] - R - iq[:, None]
    uk = (blk[:, None] - 1) * R + jk[None, :]
    valid = (jnp.abs(delta) <= R)[None] & ((uk >= 0) & (uk < Lc))[:, None, :]
    bias = -slopes[:, None, None] * (dil * jnp.abs(delta)).astype(jnp.float32)[None]
    s = jnp.where(valid[None, None, :, None], s + bias[None, None, None], NEG_INF)
    lse = jax.nn.logsumexp(s, axis=-1)
    p = jnp.exp(s - lse[..., None]).astype(v.dtype)
    o = jnp.einsum("bcnhqk,bcnkhd->bcnqhd", p, vb)
    o = o.reshape(B, dil, Lp, H, Dh)[:, :, :Lc].transpose(0, 2, 1, 3, 4).reshape(B, S, H, Dh)
    lse = lse.transpose(0, 1, 2, 4, 3).reshape(B, dil, Lp, H)[:, :, :Lc]
    lse = lse.transpose(0, 2, 1, 3).reshape(B, S, H)
    return o, lse


def mlstm_chunkwise(q, k, v, log_i, log_f):
    B, S, H, Dh = q.shape
    L = MLSTM_CHUNK
    nc = S // L
    f32 = jnp.float32

    def chunks(t):
        t = t.astype(f32)
        return t.reshape((B, nc, L) + t.shape[2:]).swapaxes(0, 1)

    causal = jnp.tril(jnp.ones((L, L), dtype=bool))

    def step(carry, inp):
        C, n, m = carry
        qj, kj, vj, li, lf = inp
        b = jnp.cumsum(lf, axis=1).swapaxes(1, 2)
        li = li.swapaxes(1, 2)
        dmat = jnp.where(causal, b[..., :, None] - b[..., None, :] + li[..., None, :], -jnp.inf)
        inter = b + m[..., None]
        m_row = jnp.maximum(inter, jnp.max(dmat, axis=-1))
        w = jnp.exp(dmat - m_row[..., None])
        a = jnp.exp(inter - m_row)
        sqk = jnp.einsum("blhd,bshd->bhls", qj, kj) * w
        num = jnp.einsum("bhls,bshd->bhld", sqk, vj) + a[..., None] * jnp.einsum("blhd,bhde->bhle", qj, C)
        den = jnp.sum(sqk, axis=-1) + a * jnp.einsum("blhd,bhd->bhl", qj, n)
        hj = num / jnp.maximum(jnp.abs(den), jnp.exp(-m_row))[..., None]
        b_last = b[..., -1]
        g = b_last[..., None] - b + li
        m_new = jnp.maximum(b_last + m, jnp.max(g, axis=-1))
        kw = kj * jnp.exp(g - m_new[..., None]).swapaxes(1, 2)[..., None]
        decay = jnp.exp(b_last + m - m_new)
        C = decay[..., None, None] * C + jnp.einsum("bshd,bshe->bhde", kw, vj)
        n = decay[..., None] * n + jnp.sum(kw, axis=1)
        return (C, n, m_new), hj.swapaxes(1, 2)

    init = (jnp.zeros((B, H, Dh, Dh), f32), jnp.zeros((B, H, Dh), f32), jnp.zeros((B, H), f32))
    xs = (chunks(q) * (Dh ** -0.5), chunks(k), chunks(v), chunks(log_i), chunks(log_f))
    _, hs = lax.scan(step, init, xs)
    return hs.swapaxes(0, 1).reshape(B, S, H, Dh)


def token_mixer(h, mem_n, w_in, mlstm_conv_w, mlstm_conv_b, mlstm_gate_b, att_q_g, att_k_g,
                xatt_q_g, xatt_k_g, w_mem_kv, mlstm_h_g, w_branch, w_out):
    B, S, _ = h.shape
    points = [int(p) for p in np.cumsum(IN_SIZES)[:-1]]
    aq, ak, av, mq, mk, mv, mo, mif, xq, gpre = jnp.split(h @ w_in, points, axis=-1)

    aq = rmsnorm(aq.reshape(B, S, ATT_HEADS, ATT_HEAD_DIM), att_q_g)
    ak = rmsnorm(ak.reshape(B, S, ATT_HEADS, ATT_HEAD_DIM), att_k_g)
    av = av.reshape(B, S, ATT_HEADS, ATT_HEAD_DIM)
    slopes = alibi_slopes(ATT_HEADS)
    outs, lses = [], []
    for gi, (win, dil) in enumerate(ATT_GROUPS):
        sl = slice(gi * ATT_HEADS_PER_GROUP, (gi + 1) * ATT_HEADS_PER_GROUP)
        o, lse = dilated_group_attention(aq[:, :, sl], ak[:, :, sl], av[:, :, sl],
                                         dil, win // (2 * dil), slopes[sl])
        outs.append(o)
        lses.append(lse)
    alpha = jax.nn.softmax(jnp.stack(lses, axis=0), axis=0)
    att = jnp.concatenate([o * alpha[gi][..., None].astype(o.dtype) for gi, o in enumerate(outs)],
                          axis=2).reshape(B, S, ATT_WIDTH)

    qk = jax.nn.silu(dwconv(jnp.concatenate([mq, mk], axis=-1), mlstm_conv_w, mlstm_conv_b))
    mq, mk = jnp.split(qk, 2, axis=-1)
    gates = (mif + mlstm_gate_b).astype(jnp.float32).reshape(B, S, 4, MLSTM_HEADS)
    li_f, lf_f = gates[:, :, 0], jax.nn.log_sigmoid(gates[:, :, 1])
    li_b, lf_b = gates[:, :, 2], jax.nn.log_sigmoid(gates[:, :, 3])
    q4 = mq.reshape(B, S, MLSTM_HEADS, MLSTM_HEAD_DIM)
    k4 = mk.reshape(B, S, MLSTM_HEADS, MLSTM_HEAD_DIM)
    v4 = mv.reshape(B, S, MLSTM_HEADS, MLSTM_HEAD_DIM)
    flip = lambda t: jnp.flip(t, axis=1)
    h_fwd = mlstm_chunkwise(q4, k4, v4, li_f, lf_f)
    h_bwd = flip(mlstm_chunkwise(flip(q4), flip(k4), flip(v4), flip(li_b), flip(lf_b)))
    hm = rmsnorm(h_fwd + h_bwd, mlstm_h_g.reshape(MLSTM_HEADS, MLSTM_HEAD_DIM))
    hm = hm.reshape(B, S, MLSTM_WIDTH).astype(h.dtype) * jax.nn.sigmoid(mo)

    M = mem_n.shape[1]
    mkv = mem_n @ w_mem_kv
    xk, xv = jnp.split(mkv, 2, axis=-1)
    xq = rmsnorm(xq.reshape(B, S, XATT_HEADS, XATT_HEAD_DIM), xatt_q_g)
    xk = rmsnorm(xk.reshape(B, M, XATT_HEADS, XATT_HEAD_DIM), xatt_k_g)
    xv = xv.reshape(B, M, XATT_HEADS, XATT_HEAD_DIM)
    s = jnp.einsum("bshd,bmhd->bhsm", xq, xk, preferred_element_type=jnp.float32) * (XATT_HEAD_DIM ** -0.5)
    p = jax.nn.softmax(s, axis=-1).astype(xv.dtype)
    xo = jnp.einsum("bhsm,bmhd->bshd", p, xv).reshape(B, S, XATT_WIDTH)

    gate = jax.nn.sigmoid(gpre.reshape(B, S, N_BRANCH, D_MODEL))
    br = jnp.stack([att.astype(h.dtype), hm, xo], axis=2)
    proj = jnp.einsum("bsnc,ncd->bsnd", br, w_branch)
    merged = jnp.sum(gate * proj, axis=2)
    return merged @ w_out


def conv_ffn(h, w_up, ffn_conv_w, ffn_conv_b, w_down):
    u = dwconv(h @ w_up, ffn_conv_w, ffn_conv_b)
    a, val = jnp.split(u, 2, axis=-1)
    return (jax.nn.gelu(a) * val) @ w_down


def setup_inputs(seed: int = 0) -> dict:
    key = jax.random.key(seed)
    ks = jax.random.split(key, 23)
    f32 = jnp.float32

    def nrm(k, shape, scale):
        return jax.random.normal(k, shape, f32) * scale

    def gain(k, shape):
        return 1.0 + 0.02 * jax.random.normal(k, shape, f32)

    fbias = jnp.linspace(3.0, 6.0, MLSTM_HEADS, dtype=f32)
    gate_base = jnp.concatenate([jnp.zeros((MLSTM_HEADS,), f32), fbias,
                                 jnp.zeros((MLSTM_HEADS,), f32), fbias])
    return {
        "x_prompt": nrm(ks[0], (BATCH, SEQ, D_MODEL), 1.0),
        "x_sample": nrm(ks[1], (DEC_BATCH, DEC_SEQ, D_MODEL), 1.0),
        "mem_prompt": nrm(ks[2], (BATCH, N_MEM, D_MODEL), 1.0),
        "mem_sample": nrm(ks[3], (DEC_BATCH, N_MEM, D_MODEL), 1.0),
        "norm_mix_g": gain(ks[4], (DEPTH, D_MODEL)),
        "norm_mem_g": gain(ks[5], (DEPTH, D_MODEL)),
        "w_in": nrm(ks[6], (DEPTH, D_MODEL, IN_DIM), D_MODEL ** -0.5),
        "mlstm_conv_w": nrm(ks[7], (DEPTH, CONV_W, 2 * MLSTM_WIDTH), CONV_W ** -0.5),
        "mlstm_conv_b": nrm(ks[8], (DEPTH, 2 * MLSTM_WIDTH), 0.02),
        "mlstm_gate_b": gate_base[None] + nrm(ks[9], (DEPTH, 4 * MLSTM_HEADS), 0.1),
        "att_q_g": gain(ks[10], (DEPTH, ATT_HEAD_DIM)),
        "att_k_g": gain(ks[11], (DEPTH, ATT_HEAD_DIM)),
        "xatt_q_g": gain(ks[12], (DEPTH, XATT_HEAD_DIM)),
        "xatt_k_g": gain(ks[13], (DEPTH, XATT_HEAD_DIM)),
        "w_mem_kv": nrm(ks[14], (DEPTH, D_MODEL, 2 * XATT_WIDTH), D_MODEL ** -0.5),
        "mlstm_h_g": gain(ks[15], (DEPTH, MLSTM_WIDTH)),
        "w_branch": nrm(ks[16], (DEPTH, N_BRANCH, BRANCH_WIDTH, D_MODEL), BRANCH_WIDTH ** -0.5),
        "w_out": nrm(ks[17], (DEPTH, D_MODEL, D_MODEL), D_MODEL ** -0.5),
        "norm_ffn_g": gain(ks[18], (DEPTH, D_MODEL)),
        "w_up": nrm(ks[19], (DEPTH, D_MODEL, 2 * FFN_DIM), D_MODEL ** -0.5),
        "ffn_conv_w": nrm(ks[20], (DEPTH, CONV_W, 2 * FFN_DIM), CONV_W ** -0.5),
        "ffn_conv_b": nrm(ks[21], (DEPTH, 2 * FFN_DIM), 0.02),
        "w_down": nrm(ks[22], (DEPTH, FFN_DIM, D_MODEL), FFN_DIM ** -0.5),
    }


def reference(x_prompt, x_sample, mem_prompt, mem_sample, norm_mix_g, norm_mem_g, w_in,
              mlstm_conv_w, mlstm_conv_b, mlstm_gate_b, att_q_g, att_k_g, xatt_q_g, xatt_k_g,
              w_mem_kv, mlstm_h_g, w_branch, w_out, norm_ffn_g, w_up, ffn_conv_w, ffn_conv_b,
              w_down):
    def trunk(x, mem):
        for l in range(DEPTH):
            h = rmsnorm(x, norm_mix_g[l])
            mem_n = rmsnorm(mem, norm_mem_g[l])
            x = x + token_mixer(h, mem_n, w_in[l], mlstm_conv_w[l], mlstm_conv_b[l], mlstm_gate_b[l],
                                att_q_g[l], att_k_g[l], xatt_q_g[l], xatt_k_g[l], w_mem_kv[l],
                                mlstm_h_g[l], w_branch[l], w_out[l]).astype(x.dtype)
            h = rmsnorm(x, norm_ffn_g[l])
            x = x + conv_ffn(h, w_up[l], ffn_conv_w[l], ffn_conv_b[l], w_down[l]).astype(x.dtype)
        return x

    y_prompt = trunk(x_prompt, mem_prompt)
    y_sample = trunk(x_sample, mem_sample)
    return (y_prompt, y_sample)
```

```python
import bisect
import math
import os
from contextlib import ExitStack

import numpy as np
import concourse.bass as bass
import concourse.mybir as mybir
from concourse.bass_utils import run_bass_kernel_spmd

F32 = mybir.dt.float32
BF16 = mybir.dt.bfloat16
AF = mybir.ActivationFunctionType
ALU = mybir.AluOpType
AX = mybir.AxisListType

D = 1024
NL = 4
NMEM = 256
IN_DIM = 9232
FFN = 2816
EPS = 1e-6
C_AQ, C_AK, C_AV, C_MQ, C_MK, C_MV, C_MO, C_MIF, C_XQ, C_G = 0, 768, 1536, 2304, 3072, 3840, 4608, 5376, 5392, 6160
PAD = 64
DILS = (1, 4, 16)


class _Eng:
    def __init__(self, name, obj, sem):
        self.name, self.obj, self.sem = name, obj, sem
        self.seq = 0
        self.sig_seqs = []
        self.waited = {}


class Prog:
    def __init__(self, nc, es, n_sp=6, n_pool=4):
        self.nc = nc
        self.E = {}
        for name, obj in (("pe", nc.tensor), ("act", nc.scalar), ("dve", nc.vector),
                          ("pool", nc.gpsimd), ("sp", nc.sync)):
            sem = es.enter_context(nc.semaphore("s_" + name))
            self.E[name] = _Eng(name, obj, sem)
        self.dsems = []
        self.dq = {}
        for q, n in (("sp", n_sp), ("pool", n_pool)):
            lst = []
            for i in range(n):
                self.dsems.append(es.enter_context(nc.semaphore("d_%s%d" % (q, i))))
                lst.append([len(self.dsems) - 1, 0])
            self.dq[q] = lst
        self.dq_next = {"sp": 0, "pool": 0}
        self.lastw = {}
        self.readers = {}
        self.n_ops = 0

    def _wait(self, e, tok):
        if tok[0] == "c":
            src = self.E[tok[1]]
            i = bisect.bisect_left(src.sig_seqs, tok[2])
            assert i < len(src.sig_seqs), "dependency on an op with no later signal on %s" % src.name
            cnt, key, sem = i + 1, src.name, src.sem
        else:
            key, sem, cnt = ("d", tok[1]), self.dsems[tok[1]], tok[2]
        if e.waited.get(key, 0) >= cnt:
            return
        e.obj.wait_ge(sem, cnt)
        e.waited[key] = cnt

    def _deps(self, reads, writes):
        deps = []
        for r in reads:
            t = self.lastw.get(r)
            if t is not None:
                deps.append((t, "raw"))
        for w in writes:
            t = self.lastw.get(w)
            if t is not None:
                deps.append((t, "waw"))
            rd = self.readers.get(w)
            if rd:
                for t in rd.values():
                    deps.append((t, "war"))
        return deps

    def _commit(self, tok, reads, writes):
        key = tok[1] if tok[0] == "c" else tok
        for r in reads:
            self.readers.setdefault(r, {})[key] = tok
        for w in writes:
            self.lastw[w] = tok
            self.readers[w] = {}

    def op(self, eng, fn, reads=(), writes=(), sig=True):
        e = self.E[eng]
        for tok, kind in self._deps(reads, writes):
            if tok[0] == "c" and tok[1] == eng:
                if eng == "pe" or (kind != "raw" and eng != "pool"):
                    continue
            self._wait(e, tok)
        ins = fn(e.obj)
        e.seq += 1
        tok = ("c", eng, e.seq)
        if sig:
            ins.then_inc(e.sem, 1)
            e.sig_seqs.append(e.seq)
        self._commit(tok, reads, writes)
        self.n_ops += 1
        return tok

    def dma(self, q, out, in_, reads=(), writes=()):
        e = self.E[q]
        for tok, kind in self._deps(reads, writes):
            self._wait(e, tok)
        pool = self.dq[q]
        i = self.dq_next[q]
        self.dq_next[q] = (i + 1) % len(pool)
        ent = pool[i]
        if ent[1] > 0:
            self._wait(e, ("d", ent[0], ent[1]))
        ent[1] += 16
        e.obj.dma_start(out=out, in_=in_).then_inc(self.dsems[ent[0]], 16)
        tok = ("d", ent[0], ent[1])
        self._commit(tok, reads, writes)
        self.n_ops += 1
        return tok

    def barrier(self):
        for e in self.E.values():
            for src in self.E.values():
                if (src is e and e.name != "pool") or not src.sig_seqs:
                    continue
                cnt = len(src.sig_seqs)
                if e.waited.get(src.name, 0) < cnt:
                    e.obj.wait_ge(src.sem, cnt)
                    e.waited[src.name] = cnt
            for lst in self.dq.values():
                for semi, val in lst:
                    if val > 0 and e.waited.get(("d", semi), 0) < val:
                        e.obj.wait_ge(self.dsems[semi], val)
                        e.waited[("d", semi)] = val
        self.lastw.clear()
        self.readers.clear()


def ssl(start, n, step):
    return slice(start, start + (n - 1) * step + 1, step)


def alibi_slopes():
    return [2.0 ** (-8.0 * (h + 1) / 12.0) for h in range(12)]


def host_consts():
    kk = np.arange(128)[:, None]
    qq = np.arange(128)[None, :]
    eb = np.zeros((128, 12, 3, 128), np.float32)
    sl = alibi_slopes()
    for h in range(12):
        dil = DILS[h // 4]
        for j in range(3):
            delta = 128 * (j - 1) + kk - qq
            val = np.exp(-sl[h] * dil * np.abs(delta).astype(np.float64))
            eb[:, h, j, :] = np.where(np.abs(delta) <= 64, val, 0.0)
    cd = np.zeros((128, 5, 128), np.float32)
    cd[:, 0, :] = (kk <= qq)
    cd[:, 1, :] = (kk >= qq)
    cd[:, 2, :] = np.eye(128)
    cd[:, 3, :] = 1.0
    bd = np.zeros((128, 128), np.float32)
    bd[:64, :64] = 1.0
    bd[64:, 64:] = 1.0
    cd[:, 4, :] = bd
    return eb.reshape(128, -1), cd.reshape(128, -1)


def pack_params(inp):
    L = NL
    f = lambda k: np.asarray(inp[k], np.float32)
    cA = np.zeros((128, L, 8 * 3 + 2 + 3 * 44 + 44), np.float32)
    for l in range(L):
        o = 0
        for key in ("norm_mix_g", "norm_ffn_g", "norm_mem_g"):
            cA[:, l, o:o + 8] = f(key)[l].reshape(8, 128).T
            o += 8
        cA[:, l, o] = np.tile(f("att_q_g")[l], 2); o += 1
        cA[:, l, o] = np.tile(f("att_k_g")[l], 2); o += 1
        cA[:, l, o:o + 132] = f("ffn_conv_w")[l].reshape(3, 44, 128).transpose(2, 0, 1).reshape(128, 132); o += 132
        cA[:, l, o:o + 44] = f("ffn_conv_b")[l].reshape(44, 128).T; o += 44
    cB = np.zeros((128, L, 2 + 2 + 48 + 16), np.float32)
    for l in range(L):
        cB[:96, l, 0:2] = f("xatt_q_g")[l].reshape(2, 96).T
        cB[:96, l, 2:4] = f("xatt_k_g")[l].reshape(2, 96).T
        cB[:96, l, 4:52] = f("mlstm_conv_w")[l].reshape(3, 16, 96).transpose(2, 0, 1).reshape(96, 48)
        cB[:96, l, 52:68] = f("mlstm_conv_b")[l].reshape(16, 96).T
    cC = np.zeros((128, L, 16 + 768), np.float32)
    for l in range(L):
        cC[:, l, 0:16] = f("mlstm_gate_b")[l][None, :]
        cC[:, l, 16:] = f("mlstm_h_g")[l][None, :]
    return cA.reshape(128, -1), cB.reshape(128, -1), cC.reshape(128, -1)


def pack_cm(inp):
    f = lambda k: np.asarray(inp[k], np.float32)
    cM = np.zeros((128, NL, 48), np.float32)
    for l in range(NL):
        cM[:, l, 0:36] = f("mlstm_conv_w")[l].reshape(3, 12, 128).transpose(2, 0, 1).reshape(128, 36)
        cM[:, l, 36:48] = f("mlstm_conv_b")[l].reshape(12, 128).T
    return cM.reshape(128, -1)


NA = 8 * 3 + 2 + 132 + 44
NB = 68
NCC = 16 + 768


def build(seq_lens, depth=NL, dbg=(), upto=None):
    nc = bass.Bass("TRN2", target_bir_lowering=False)
    Smax = max(seq_lens)
    nseq = len(seq_lens)
    dt_in = lambda name, shape: nc.dram_tensor(name, list(shape), F32, kind="ExternalInput").ap()
    xs = [dt_in("x%d" % i, (seq_lens[i], D)) for i in range(nseq)]
    mems = [dt_in("mem%d" % i, (NMEM, D)) for i in range(nseq)]
    ys = [nc.dram_tensor("y%d" % i, [seq_lens[i], D], F32, kind="ExternalOutput").ap() for i in range(nseq)]
    w_in = dt_in("w_in", (NL, D, IN_DIM))
    w_kv = dt_in("w_mem_kv", (NL, D, 1536))
    w_br = dt_in("w_branch", (NL, 2304, D))
    w_out = dt_in("w_out", (NL, D, D))
    w_up = dt_in("w_up", (NL, D, 2 * FFN))
    w_dn = dt_in("w_down", (NL, FFN, D))
    cA_d = dt_in("cA", (128, NL * NA))
    cB_d = dt_in("cB", (128, NL * NB))
    cC_d = dt_in("cC", (128, NL * NCC))
    eb_d = dt_in("cEB", (128, 12 * 3 * 128))
    cM_d = dt_in("cM", (128, NL * 48))
    cd_d = dt_in("cD", (128, 5 * 128))

    def scratch(name, shape, dt):
        kind = "ExternalOutput" if name in dbg else "Internal"
        return nc.dram_tensor(name, list(shape), dt, kind=kind).ap()

    SP = Smax + 2 * PAD
    xT = scratch("xT", (D, Smax), F32)
    hT = scratch("hT", (D, SP), BF16)
    qaT = scratch("qaT", (768, Smax), BF16)
    kaT = scratch("kaT", (768, Smax), BF16)
    va = scratch("va", (Smax, 768), BF16)
    attT = scratch("attT", (768, Smax), BF16)
    mqT = scratch("mqT", (768, Smax), BF16)
    mkT = scratch("mkT", (768, Smax), BF16)
    mv = scratch("mv", (Smax, 768), BF16)
    mo = scratch("mo", (Smax, 768), F32)
    gts = scratch("gts", (Smax, 16), F32)
    hb = scratch("hb", (Smax, 768), F32)
    hmT = scratch("hmT", (768, Smax), BF16)
    xoT = scratch("xoT", (768, Smax), BF16)
    gT = scratch("gT", (FFN, Smax), BF16)

    es = ExitStack()
    with es:
        P = Prog(nc, es)
        uid = [0]

        def sb(st, name, shape, dt):
            uid[0] += 1
            return st.enter_context(nc.sbuf_tensor("sb%d_%s" % (uid[0], name), list(shape), dt))
        cA = sb(es, "cA", (128, NL, NA), F32)
        cB = sb(es, "cB", (128, NL, NB), F32)
        cM = sb(es, "cM", (128, NL, 48), F32)
        cD = sb(es, "cDs", (128, 5, 128), F32)
        cDb = sb(es, "cDb", (128, 5, 128), BF16)
        zer = sb(es, "zer", (128, 8, PAD), BF16)
        NPS = 6
        PS = [es.enter_context(nc.psum_tensor("ps%d" % i, [128, 512], F32)) for i in range(NPS)]
        PSB = es.enter_context(nc.psum_tensor("psb", [128, 1024], BF16))
        PSB2 = es.enter_context(nc.psum_tensor("psb2", [128, 1024], BF16))
        TRIF, TRIB, IDF, ONEF = cD[:, 0, :], cD[:, 1, :], cD[:, 2, :], cD[:, 3, :]
        IDB, ONEB, BDB = cDb[:, 2, :], cDb[:, 3, :], cDb[:, 4, :]
        P.dma("sp", cA[:].rearrange("p l n -> p (l n)"), cA_d, writes=["cA"])
        P.dma("sp", cB[:].rearrange("p l n -> p (l n)"), cB_d, writes=["cB"])
        P.dma("sp", cM[:].rearrange("p l n -> p (l n)"), cM_d, writes=["cM"])
        P.dma("sp", cD[:].rearrange("p l n -> p (l n)"), cd_d, writes=["cD"])
        P.dma("pool", cDb[:].rearrange("p l n -> p (l n)"), cd_d, writes=["cDb"])
        P.op("dve", lambda e: e.memset(zer[:], 0.0), writes=["zer"])
        epsT = sb(es, "epsT", (128, 4), F32)
        P.op("dve", lambda e: e.memset(epsT[:, 0:1], EPS), writes=["eps"])
        P.op("dve", lambda e: e.memset(epsT[:, 1:2], 1.0), writes=["eps1"])
        P.op("dve", lambda e: e.memset(epsT[:, 2:3], math.log(192.0 ** -0.5)), writes=["eps2"])
        EPSC = epsT[:, 0:1]
        ONEC = epsT[:, 1:2]
        LNC = epsT[:, 2:3]
        P.barrier()

        psn = [0]

        def run_tiles(n, ld, body):
            if n:
                ld(0)
            for it in range(n):
                if it + 1 < n:
                    ld(it + 1)
                body(it)

        def bank():
            psn[0] = (psn[0] + 1) % NPS
            return psn[0]

        def fused_norm(nb, xtile, xkeys, T, lg, goff, t0, b):
            nsq, nrs, nho = nb
            P.op("act", lambda e: e.activation(out=nsq[:, :, 0:T], in_=xtile, func=AF.Square),
                 reads=xkeys, writes=["nsq"])
            bk = bank()
            for c in range(8):
                P.op("pe", lambda e, c=c: e.matmul(PS[bk][:, 0:T], ONEB, nsq[:, c, 0:T], start=(c == 0), stop=(c == 7)),
                     reads=["nsq", "cDb"], writes=[("ps", bk)], sig=(c == 7))
            P.op("act", lambda e: e.activation(out=nrs[:, 0:T], in_=PS[bk][:, 0:T], func=AF.Ln, scale=1.0 / D, bias=EPSC),
                 reads=[("ps", bk)], writes=["nrs"])
            P.op("act", lambda e: e.activation(out=nrs[:, 0:T], in_=nrs[:, 0:T], func=AF.Exp, scale=-0.5), reads=["nrs"], writes=["nrs"])
            for c in range(8):
                P.op("dve", lambda e, c=c: e.scalar_tensor_tensor(
                    nho[b][:, c, 0:T], xtile[:, c, :], cA[:, lg, goff + c:goff + c + 1], nrs[:, 0:T], ALU.mult, ALU.mult),
                    reads=xkeys + ["nrs", "cA"], writes=[("nho", b, c)])
            P.dma("sp", hT[:, PAD + t0:PAD + t0 + T].rearrange("(c p) t -> p c t", p=128), nho[b][:, :, 0:T],
                  reads=[("nho", b, c) for c in range(8)])

        def norm_bufs(st, T):
            return (sb(st, "fnsq", (128, 8, T), BF16), sb(st, "fnrs", (128, T), F32),
                    [sb(st, "fnho%d" % i, (128, 8, T), BF16) for i in range(2)])

        def load_w(st, name, src, kp, nk, ncols, c0=0):
            t = sb(st, name, (kp, nk, ncols), BF16)
            v = src.rearrange("(k p) n -> p k n", p=kp)
            step = max(1, 4096 // ncols)
            for k0 in range(0, nk, step):
                k1 = min(nk, k0 + step)
                P.dma("pool", t[:, k0:k1, :], v[:, k0:k1, c0:c0 + ncols], writes=[(name, k0)])
            return t, [(name, k0) for k0 in range(0, nk, step)]

        def phase_in(si, S):
            with ExitStack() as st:
                xin = [sb(st, "xin%d" % i, (128, 4, D), F32) for i in range(2)]
                xo = [sb(st, "xo%d" % i, (128, 8, 512), F32) for i in range(2)]
                P.dma("sp", hT[:, 0:PAD].rearrange("(c p) t -> p c t", p=128), zer[:], reads=["zer"])
                P.dma("sp", hT[:, PAD + S:PAD + S + PAD].rearrange("(c p) t -> p c t", p=128), zer[:], reads=["zer"])
                nt = S // 512

                def ld(it):
                    P.dma("sp", xin[it % 2][:], xs[si][it * 512:it * 512 + 512, :].rearrange("(j p) f -> p j f", p=128),
                          writes=[("xin", it % 2)])

                def body(it):
                    b = it % 2
                    t0 = it * 512
                    for c in range(8):
                        bk = bank()
                        for j in range(4):
                            P.op("pe", lambda e, bk=bk, j=j, c=c, b=b: e.transpose(
                                PS[bk][:, j * 128:(j + 1) * 128], xin[b][:, j, c * 128:(c + 1) * 128], IDF),
                                reads=[("xin", b), "cD"], writes=[("ps", bk)], sig=(j == 3))
                        eng = "act" if c % 2 else "dve"
                        if eng == "act":
                            P.op("act", lambda e, bk=bk, c=c, b=b: e.copy(xo[b][:, c, :], PS[bk][:]),
                                 reads=[("ps", bk)], writes=[("xo", b, c)])
                        else:
                            P.op("dve", lambda e, bk=bk, c=c, b=b: e.tensor_copy(xo[b][:, c, :], PS[bk][:]),
                                 reads=[("ps", bk)], writes=[("xo", b, c)])
                    P.dma("sp", xT[:, t0:t0 + 512].rearrange("(c p) t -> p c t", p=128), xo[b][:],
                          reads=[("xo", b, c) for c in range(8)])
                run_tiles(nt, ld, body)
                P.barrier()

        def phase_out(si, S):
            with ExitStack() as st:
                xi = [sb(st, "xi%d" % i, (128, 8, 512), F32) for i in range(2)]
                yo = [sb(st, "yo%d" % i, (128, 4, D), F32) for i in range(2)]
                def ld(it):
                    P.dma("sp", xi[it % 2][:], xT[:, it * 512:it * 512 + 512].rearrange("(c p) t -> p c t", p=128),
                          writes=[("xi", it % 2)])

                def body(it):
                    b = it % 2
                    t0 = it * 512
                    for j in range(4):
                        for half in range(2):
                            bk = bank()
                            for cc in range(4):
                                c = half * 4 + cc
                                P.op("pe", lambda e, bk=bk, j=j, c=c, cc=cc, b=b: e.transpose(
                                    PS[bk][:, cc * 128:(cc + 1) * 128], xi[b][:, c, j * 128:(j + 1) * 128], IDF),
                                    reads=[("xi", b), "cD"], writes=[("ps", bk)], sig=(cc == 3))
                            if half:
                                P.op("act", lambda e, bk=bk, j=j, b=b: e.copy(yo[b][:, j, 512:1024], PS[bk][:]),
                                     reads=[("ps", bk)], writes=[("yo", b, j, 1)])
                            else:
                                P.op("dve", lambda e, bk=bk, j=j, b=b: e.tensor_copy(yo[b][:, j, 0:512], PS[bk][:]),
                                     reads=[("ps", bk)], writes=[("yo", b, j, 0)])
                    P.dma("sp", ys[si][t0:t0 + 512, :].rearrange("(j p) f -> p j f", p=128), yo[b][:],
                          reads=[("yo", b, j, h) for j in range(4) for h in range(2)])
                run_tiles(S // 512, ld, body)
                P.barrier()

        def phase_norm(l, S, goff):
            with ExitStack() as st:
                xi = [sb(st, "nxi%d" % i, (128, 8, 512), F32) for i in range(2)]
                sq = sb(st, "nsq", (128, 8, 512), BF16)
                rs = sb(st, "nrs", (128, 512), F32)
                ho = [sb(st, "nho%d" % i, (128, 8, 512), BF16) for i in range(2)]
                def ld(it):
                    P.dma("sp", xi[it % 2][:], xT[:, it * 512:it * 512 + 512].rearrange("(c p) t -> p c t", p=128),
                          writes=[("xi", it % 2)])

                def body(it):
                    b = it % 2
                    t0 = it * 512
                    P.op("act", lambda e, b=b: e.activation(out=sq[:], in_=xi[b][:], func=AF.Square),
                         reads=[("xi", b)], writes=["sq"])
                    bk = bank()
                    for c in range(8):
                        P.op("pe", lambda e, bk=bk, c=c: e.matmul(PS[bk][:], ONEB, sq[:, c, :], start=(c == 0), stop=(c == 7)),
                             reads=["sq", "cDb"], writes=[("ps", bk)], sig=(c == 7))
                    P.op("act", lambda e, bk=bk: e.activation(out=rs[:], in_=PS[bk][:], func=AF.Ln, scale=1.0 / D, bias=EPSC),
                         reads=[("ps", bk)], writes=["rs"])
                    P.op("act", lambda e: e.activation(out=rs[:], in_=rs[:], func=AF.Exp, scale=-0.5), reads=["rs"], writes=["rs"])
                    for c in range(8):
                        P.op("dve", lambda e, c=c, b=b: e.scalar_tensor_tensor(
                            ho[b][:, c, :], xi[b][:, c, :], cA[:, l, goff + c:goff + c + 1], rs[:], ALU.mult, ALU.mult),
                            reads=[("xi", b), "rs", "cA"], writes=[("ho", b, c)])
                    P.dma("sp", hT[:, PAD + t0:PAD + t0 + 512].rearrange("(c p) t -> p c t", p=128), ho[b][:],
                          reads=[("ho", b, c) for c in range(8)])
                run_tiles(S // 512, ld, body)
                P.barrier()

        def phase_a1(l, S):
            with ExitStack() as st:
                W, wk = load_w(st, "wA", w_in[l], 128, 8, 2304, C_AQ)
                hi = [sb(st, "ahi%d" % i, (128, 8, 512), BF16) for i in range(2)]
                sq = [sb(st, "asq%d" % i, (128, 512), BF16) for i in range(2)]
                rs = [sb(st, "ars%d" % i, (128, 512), F32) for i in range(2)]
                qo = [sb(st, "aqo%d" % i, (128, 12, 512), BF16) for i in range(2)]
                vo = [sb(st, "avo%d" % i, (128, 4, 768), BF16) for i in range(2)]

                def ld(it):
                    P.dma("sp", hi[it % 2][:], hT[:, PAD + it * 512:PAD + it * 512 + 512].rearrange("(c p) t -> p c t", p=128),
                          writes=[("hi", it % 2)])

                def group(b, oc):
                    bk = bank()
                    for k in range(8):
                        P.op("pe", lambda e, bk=bk, k=k, oc=oc, b=b: e.matmul(
                            PS[bk][:], W[:, k, oc * 128:(oc + 1) * 128], hi[b][:, k, :], start=(k == 0), stop=(k == 7)),
                            reads=[("hi", b)] + wk, writes=[("ps", bk)], sig=(k == 7))
                    s = oc % 2
                    P.op("act", lambda e, bk=bk, s=s: e.activation(out=sq[s][:], in_=PS[bk][:], func=AF.Square),
                         reads=[("ps", bk)], writes=[("sq", s)])
                    return bk

                def norm(b, oc, bk):
                    s = oc % 2
                    bk2 = bank()
                    P.op("pe", lambda e, bk2=bk2, s=s: e.matmul(PS[bk2][:], BDB, sq[s][:], start=True, stop=True),
                         reads=[("sq", s), "cDb"], writes=[("ps", bk2)])
                    P.op("act", lambda e, bk2=bk2, s=s: e.activation(out=rs[s][:], in_=PS[bk2][:], func=AF.Ln, scale=1.0 / 64, bias=EPSC),
                         reads=[("ps", bk2)], writes=[("rs", s)])
                    P.op("act", lambda e, s=s: e.activation(out=rs[s][:], in_=rs[s][:], func=AF.Exp, scale=-0.5), reads=[("rs", s)], writes=[("rs", s)])
                    gcol = 24 + (0 if oc < 6 else 1)
                    P.op("dve", lambda e, bk=bk, s=s, oc=oc, b=b, gcol=gcol: e.scalar_tensor_tensor(
                        qo[b][:, oc, :], PS[bk][:], cA[:, l, gcol:gcol + 1], rs[s][:], ALU.mult, ALU.mult),
                        reads=[("ps", bk), ("rs", s), "cA"], writes=[("qo", b, oc)])

                def body(it):
                    b = it % 2
                    t0 = it * 512
                    prev = None
                    for oc in range(12):
                        bk = group(b, oc)
                        if prev is not None:
                            norm(b, prev[0], prev[1])
                        prev = (oc, bk)
                    vgroups = [(j, n0, nn) for j in range(4) for (n0, nn) in ((0, 512), (512, 256))]
                    for gi, (j, n0, nn) in enumerate(vgroups):
                        bk = bank()
                        for k in range(8):
                            P.op("pe", lambda e, bk=bk, k=k, j=j, n0=n0, nn=nn, b=b: e.matmul(
                                PS[bk][:, 0:nn], hi[b][:, k, j * 128:(j + 1) * 128], W[:, k, 1536 + n0:1536 + n0 + nn],
                                start=(k == 0), stop=(k == 7)),
                                reads=[("hi", b)] + wk, writes=[("ps", bk)], sig=(k == 7))
                        P.op("act", lambda e, bk=bk, j=j, n0=n0, nn=nn, b=b: e.copy(vo[b][:, j, n0:n0 + nn], PS[bk][:, 0:nn]),
                             reads=[("ps", bk)], writes=[("vo", b, j, n0)])
                        if gi == 0:
                            norm(b, prev[0], prev[1])
                            P.dma("sp", qaT[:, t0:t0 + 512].rearrange("(c p) t -> p c t", p=128), qo[b][:, 0:6, :],
                                  reads=[("qo", b, oc) for oc in range(6)])
                            P.dma("sp", kaT[:, t0:t0 + 512].rearrange("(c p) t -> p c t", p=128), qo[b][:, 6:12, :],
                                  reads=[("qo", b, oc) for oc in range(6, 12)])
                    P.dma("sp", va[t0:t0 + 512, :].rearrange("(j p) f -> p j f", p=128), vo[b][:],
                          reads=[("vo", b, j, n0) for j in range(4) for n0 in (0, 512)])
                run_tiles(S // 512, ld, body)
                P.barrier()

        def phase_a2(l, S):
            with ExitStack() as st:
                EBs = [sb(st, "EB%d" % i, (128, 3, 384), F32) for i in range(2)]
                ebv = eb_d.rearrange("p (g q n) -> p g q n", g=3, q=4)
                qs = sb(st, "a2q", (128, 6, 2048), BF16)
                kw = [2048 + 256 * d for d in DILS]
                ks = [sb(st, "a2k%d" % g, (128, 2, kw[g]), BF16) for g in range(3)]
                ntl = [2048 // d // 128 + 2 for d in DILS]
                vs = [sb(st, "a2v%d" % g, (128, ntl[g], DILS[g], 256), BF16) for g in range(3)]
                OD = sb(st, "a2od", (64, 2, 3, 2048), F32)
                DS = sb(st, "a2ds", (64, 2048), F32)
                AO = sb(st, "a2ao", (64, 2, 2048), BF16)
                Ee = [sb(st, "a2e%d" % i, (128, 384), F32) for i in range(3)]
                Pp = [sb(st, "a2p%d" % i, (128, 384), BF16) for i in range(3)]
                cnt = {"un": 0, "ao": 0, "eb": 0}

                def stage_a(U):
                    bk = bank()
                    U["bk"] = bk
                    for jj, ksl in enumerate(U["ksl"]):
                        P.op("pe", lambda e, bk=bk, jj=jj, ksl=ksl, qsl=U["qsl"]: e.matmul(
                            PS[bk][:, jj * 128:(jj + 1) * 128], ksl, qsl, start=True, stop=True),
                            reads=["qs", ("ks", U["g"])], writes=[("ps", bk)], sig=(jj == U["nb"] - 1))

                def stage_b1(U):
                    u, nb_, bk = U["u"], U["nb"], U["bk"]
                    P.op("act", lambda e: e.activation(
                        out=Ee[u][:, 0:128 * nb_], in_=PS[bk][:, 0:128 * nb_], func=AF.Exp, scale=0.125),
                        reads=[("ps", bk)], writes=[("Ee", u)])

                def stage_b2(U):
                    u, nb_, bk = U["u"], U["nb"], U["bk"]
                    eng = "dve"
                    EB, g, jlo = U["EB"], U["g"], U["jlo"]
                    P.op(eng, lambda e: e.tensor_tensor(
                        Pp[u][:, 0:128 * nb_], Ee[u][:, 0:128 * nb_], EB[:, g, jlo * 128:(jlo + nb_) * 128], ALU.mult),
                        reads=[("Ee", u), ("EB", U["ebi"])], writes=[("Pp", u)])

                def stage_c1(U):
                    u, nb_, g = U["u"], U["nb"], U["g"]
                    bk2 = bank()
                    U["bk2"] = bk2
                    for jj, (vsl, vkey) in enumerate(U["vsl"]):
                        P.op("pe", lambda e, jj=jj, vsl=vsl: e.matmul(
                            PS[bk2][0:64, 0:128], vsl, Pp[u][:, jj * 128:(jj + 1) * 128],
                            start=(jj == 0), stop=(jj == nb_ - 1)),
                            reads=[("Pp", u), vkey], writes=[("ps", bk2)], sig=False)
                    for jj in range(nb_):
                        P.op("pe", lambda e, jj=jj: e.matmul(
                            PS[bk2][0:64, 128:256], ONEB[:, 0:64], Pp[u][:, jj * 128:(jj + 1) * 128],
                            start=(jj == 0), stop=(jj == nb_ - 1)),
                            reads=[("Pp", u), "cDb"], writes=[("ps", bk2)], sig=(jj == nb_ - 1))

                def stage_c2(U):
                    osl, bk2, g = U["osl"], U["bk2"], U["g"]
                    P.op("act", lambda e: e.copy(osl, PS[bk2][0:64, 0:256].rearrange("p (a b) -> p a b", a=2)),
                         reads=[("ps", bk2)], writes=[("OD", g)])

                for sbi in range(S // 2048):
                    T0 = sbi * 2048
                    P.dma("sp", qs[:], qaT[:, T0:T0 + 2048].rearrange("(c p) t -> p c t", p=128), writes=["qs"])
                    K0s, mt0s = [], []
                    for g in range(3):
                        d = DILS[g]
                        K0 = max(0, T0 - 128 * d)
                        K1 = min(S, T0 + 2048 + 128 * d)
                        K0s.append(K0)
                        P.dma("sp", ks[g][:, :, 0:K1 - K0],
                              kaT[256 * g:256 * g + 256, K0:K1].rearrange("(c p) t -> p c t", p=128), writes=[("ks", g)])
                        m_lo = max(0, T0 // d // 128 - 1)
                        m_hi = min(S // d // 128, (T0 + 2048) // d // 128 + 1)
                        mt0s.append(m_lo)
                        for mm in range(m_lo, m_hi):
                            P.dma("sp", vs[g][:, mm - m_lo, :, :],
                                  va[mm * 128 * d:(mm + 1) * 128 * d, 256 * g:256 * g + 256].rearrange("(i r) f -> i r f", r=d),
                                  writes=[("vs", g, mm - m_lo)])
                    for hh in range(4):
                        ebi = cnt["eb"] % 2
                        cnt["eb"] += 1
                        EB = EBs[ebi]
                        P.dma("sp", EB[:], ebv[:, :, hh, :], writes=[("EB", ebi)])
                        units = []
                        for g in range(3):
                            if os.environ.get("A2G") and str(g) not in os.environ["A2G"]:
                                continue
                            d = DILS[g]
                            h = 4 * g + hh
                            c = h // 2
                            pb = 64 * (h % 2)
                            Lt = S // d // 128
                            for r in range(d):
                                for j in range(2048 // d // 128):
                                    m = T0 // d // 128 + j
                                    tiles = [mm for mm in (m - 1, m, m + 1) if 0 <= mm < Lt]
                                    U = {"g": g, "nb": len(tiles), "jlo": tiles[0] - (m - 1), "EB": EB, "ebi": ebi,
                                         "u": cnt["un"] % 3, "n": cnt["un"]}
                                    cnt["un"] += 1
                                    U["qsl"] = qs[pb:pb + 64, c, ssl(r + 128 * j * d, 128, d)]
                                    U["ksl"] = [ks[g][pb:pb + 64, c - 2 * g, ssl(mm * 128 * d + r - K0s[g], 128, d)] for mm in tiles]
                                    U["vsl"] = [(vs[g][:, mm - mt0s[g], r, 64 * hh:64 * hh + 64], ("vs", g, mm - mt0s[g])) for mm in tiles]
                                    U["osl"] = OD[:, :, g, ssl(r + 128 * j * d, 128, d)]
                                    units.append(U)
                        NU = len(units)
                        stage_a(units[0])
                        if NU > 1:
                            stage_a(units[1])
                        stage_b1(units[0])
                        stage_b2(units[0])
                        for i, U in enumerate(units):
                            if i + 2 < NU:
                                stage_a(units[i + 2])
                            if i + 1 < NU:
                                stage_b1(units[i + 1])
                            stage_c1(U)
                            if i + 1 < NU:
                                stage_b2(units[i + 1])
                            stage_c2(U)
                        P.op("dve", lambda e: e.tensor_tensor(DS[:], OD[:, 1, 0, :], OD[:, 1, 1, :], ALU.add),
                             reads=[("OD", 0), ("OD", 1)], writes=["DS"])
                        P.op("dve", lambda e: e.tensor_tensor(DS[:], DS[:], OD[:, 1, 2, :], ALU.add),
                             reads=["DS", ("OD", 2)], writes=["DS"])
                        P.op("act", lambda e: e.activation(out=DS[:], in_=DS[:], func=AF.Ln), reads=["DS"], writes=["DS"])
                        P.op("act", lambda e: e.activation(out=DS[:], in_=DS[:], func=AF.Exp, scale=-1.0), reads=["DS"], writes=["DS"])
                        for g in range(3):
                            ai = cnt["ao"] % 2
                            cnt["ao"] += 1
                            P.op("dve", lambda e, g=g, ai=ai: e.tensor_tensor(AO[:, ai, :], OD[:, 0, g, :], DS[:], ALU.mult),
                                 reads=["DS", ("OD", g)], writes=[("AO", ai)])
                            P.dma("sp", attT[64 * (4 * g + hh):64 * (4 * g + hh) + 64, T0:T0 + 2048], AO[:, ai, :],
                                  reads=[("AO", ai)])
                P.barrier()

        def phase_m1a(l, S):
            with ExitStack() as st:
                W, wk = load_w(st, "wMa", w_in[l], 128, 8, 1536, C_MQ)
                hi = [sb(st, "mhi%d" % i, (128, 8, 512), BF16) for i in range(2)]
                tmp = [sb(st, "mtmp%d" % i, (128, 512), F32) for i in range(4)]
                qo = [sb(st, "mqo%d" % i, (128, 12, 512), BF16) for i in range(2)]
                tl = [(t, min(510, S - t)) for t in range(0, S, 510)]

                def ld(it):
                    t0, ntok = tl[it]
                    P.dma("sp", hi[it % 2][:, :, 0:ntok + 2],
                          hT[:, PAD + t0 - 1:PAD + t0 + 1 + ntok].rearrange("(c p) t -> p c t", p=128), writes=[("hi", it % 2)])

                def body(it):
                    t0, ntok = tl[it]
                    ncol = ntok + 2
                    b = it % 2
                    pend = None
                    for oc in range(12):
                        bk = bank()
                        for k in range(8):
                            P.op("pe", lambda e, bk=bk, k=k, oc=oc: e.matmul(
                                PS[bk][:, 0:ncol], W[:, k, oc * 128:(oc + 1) * 128], hi[b][:, k, 0:ncol],
                                start=(k == 0), stop=(k == 7)),
                                reads=[("hi", b)] + wk, writes=[("ps", bk)], sig=(k == 7))
                        u = oc % 4
                        w0, w1, w2 = (cM[:, l, jj * 12 + oc:jj * 12 + oc + 1] for jj in range(3))
                        bb = cM[:, l, 36 + oc:37 + oc]
                        P.op("act", lambda e, bk=bk, u=u, w1=w1, bb=bb: e.activation(
                            out=tmp[u][:, 0:ntok], in_=PS[bk][:, 1:1 + ntok], func=AF.Identity, scale=w1, bias=bb),
                            reads=[("ps", bk), "cM"], writes=[("tmp", u)])
                        if pend is not None:
                            pend()
                        P.op("dve", lambda e, bk=bk, u=u, w0=w0: e.scalar_tensor_tensor(
                            tmp[u][:, 0:ntok], PS[bk][:, 0:ntok], w0, tmp[u][:, 0:ntok], ALU.mult, ALU.add),
                            reads=[("ps", bk), ("tmp", u), "cM"], writes=[("tmp", u)])
                        P.op("dve", lambda e, bk=bk, u=u, w2=w2: e.scalar_tensor_tensor(
                            tmp[u][:, 0:ntok], PS[bk][:, 2:2 + ntok], w2, tmp[u][:, 0:ntok], ALU.mult, ALU.add),
                            reads=[("ps", bk), ("tmp", u), "cM"], writes=[("tmp", u)])

                        def pend(u=u, oc=oc):
                            P.op("act", lambda e: e.activation(out=qo[b][:, oc, 0:ntok], in_=tmp[u][:, 0:ntok], func=AF.Silu),
                                 reads=[("tmp", u)], writes=[("qo", b, oc)])
                    pend()
                    P.dma("sp", mqT[:, t0:t0 + ntok].rearrange("(c p) t -> p c t", p=128), qo[b][:, 0:6, 0:ntok],
                          reads=[("qo", b, oc) for oc in range(6)])
                    P.dma("sp", mkT[:, t0:t0 + ntok].rearrange("(c p) t -> p c t", p=128), qo[b][:, 6:12, 0:ntok],
                          reads=[("qo", b, oc) for oc in range(6, 12)])
                run_tiles(len(tl), ld, body)
                P.barrier()

        def phase_m1b(l, S):
            with ExitStack() as st:
                W, wk = load_w(st, "wMb", w_in[l], 128, 8, 1552, C_MV)
                cCl = sb(st, "cCl", (128, 16), F32)
                P.dma("sp", cCl[:], cC_d[:, l * NCC:l * NCC + 16], writes=["cCl"])
                hi = [sb(st, "bhi%d" % i, (128, 8, 512), BF16) for i in range(2)]
                vo = [sb(st, "bvo%d" % i, (128, 4, 768), BF16) for i in range(2)]
                oo = [sb(st, "boo%d" % i, (128, 4, 768), F32) for i in range(2)]
                go = [sb(st, "bgo%d" % i, (128, 4, 16), F32) for i in range(2)]
                gt = sb(st, "bgt", (128, 4, 8), F32)
                def ld(it):
                    P.dma("sp", hi[it % 2][:], hT[:, PAD + it * 512:PAD + it * 512 + 512].rearrange("(c p) t -> p c t", p=128),
                          writes=[("hi", it % 2)])

                def body(it):
                    b = it % 2
                    t0 = it * 512
                    bkg = bank()
                    for j in range(4):
                        for k in range(8):
                            P.op("pe", lambda e, k=k, j=j, b=b, bkg=bkg: e.matmul(
                                PS[bkg][:, j * 16:(j + 1) * 16], hi[b][:, k, j * 128:(j + 1) * 128], W[:, k, 1536:1552],
                                start=(k == 0), stop=(k == 7)),
                                reads=[("hi", b)] + wk, writes=[("ps", bkg)], sig=(k == 7))
                    for j in range(4):
                        P.op("dve", lambda e, j=j, b=b, bkg=bkg: e.tensor_tensor(
                            go[b][:, j, :], PS[bkg][:, j * 16:(j + 1) * 16], cCl[:], ALU.add),
                            reads=[("ps", bkg), "cCl"], writes=[("go", b)])
                    for half in range(2):
                        src = go[b][:, :, 8 * half + 4:8 * half + 8]
                        P.op("act", lambda e, src=src, half=half: e.activation(out=gt[:, :, 4 * half:4 * half + 4], in_=src, func=AF.Exp, scale=-1.0),
                             reads=[("go", b)], writes=["gt"])
                    P.op("act", lambda e: e.activation(out=gt[:], in_=gt[:], func=AF.Ln, bias=ONEC, scale=1.0),
                         reads=["gt", "eps1"], writes=["gt"])
                    for half in range(2):
                        dst = go[b][:, :, 8 * half + 4:8 * half + 8]
                        P.op("dve", lambda e, dst=dst, half=half: e.tensor_scalar(dst, gt[:, :, 4 * half:4 * half + 4], -1.0, None, ALU.mult),
                             reads=["gt"], writes=[("go", b)])
                    P.dma("sp", gts[t0:t0 + 512, :].rearrange("(j p) f -> p j f", p=128), go[b][:], reads=[("go", b)])
                    for j in range(4):
                        for gi, (n0, nn) in enumerate(((0, 512), (512, 256), (768, 512), (1280, 256))):
                            bk = bank()
                            for k in range(8):
                                P.op("pe", lambda e, bk=bk, k=k, j=j, n0=n0, nn=nn, b=b: e.matmul(
                                    PS[bk][:, 0:nn], hi[b][:, k, j * 128:(j + 1) * 128], W[:, k, n0:n0 + nn],
                                    start=(k == 0), stop=(k == 7)),
                                    reads=[("hi", b)] + wk, writes=[("ps", bk)], sig=(k == 7))
                            if gi < 2:
                                if gi == 0:
                                    P.op("dve", lambda e, bk=bk, j=j, n0=n0, nn=nn, b=b: e.tensor_copy(vo[b][:, j, n0:n0 + nn], PS[bk][:, 0:nn]),
                                         reads=[("ps", bk)], writes=[("vo", b, j, gi)])
                                else:
                                    P.op("act", lambda e, bk=bk, j=j, n0=n0, nn=nn, b=b: e.copy(vo[b][:, j, n0:n0 + nn], PS[bk][:, 0:nn]),
                                         reads=[("ps", bk)], writes=[("vo", b, j, gi)])
                            else:
                                P.op("act", lambda e, bk=bk, j=j, n0=n0, nn=nn, b=b: e.activation(
                                    out=oo[b][:, j, n0 - 768:n0 - 768 + nn], in_=PS[bk][:, 0:nn], func=AF.Sigmoid),
                                    reads=[("ps", bk)], writes=[("oo", b, j, gi)])
                    P.dma("sp", mv[t0:t0 + 512, :].rearrange("(j p) f -> p j f", p=128), vo[b][:],
                          reads=[("vo", b, j, gi) for j in range(4) for gi in range(2)])
                    P.dma("sp", mo[t0:t0 + 512, :].rearrange("(j p) f -> p j f", p=128), oo[b][:],
                          reads=[("oo", b, j, gi) for j in range(4) for gi in (2, 3)])
                run_tiles(S // 512, ld, body)
                P.barrier()

        def phase_scan(l, S, fwd):
            with ExitStack() as st:
                gofs = 0 if fwd else 8
                MASK = TRIF if fwd else TRIB
                qg = [sb(st, "sq%d" % i, (96, 8, 512), BF16) for i in range(2)]
                kg = [sb(st, "sk%d" % i, (96, 8, 512), BF16) for i in range(2)]
                vg = [sb(st, "sv%d" % i, (128, 4, 4, 194), BF16) for i in range(2)]
                gg = [sb(st, "sg%d" % i, (128, 4, 16), F32) for i in range(2)]
                for i in range(2):
                    P.op("pool", lambda e, i=i: e.memset(vg[i][:, :, :, 192:194], 1.0), writes=[("vg1", i)])
                Cs = sb(st, "sC", (96, 4, 2, 194), F32)
                Cb = sb(st, "sCb", (96, 4, 2, 194), BF16)
                P.op("dve", lambda e: e.memset(Cs[:], 0.0), writes=[("C", h) for h in range(4)])
                P.op("pool", lambda e: e.memset(Cb[:], 0.0), writes=[("Cb", h) for h in range(4)])
                sm = [sb(st, "ssm%d" % i, (128, 20), F32) for i in range(3)]
                smb = [sb(st, "ssmb%d" % i, (128, 8), F32) for i in range(3)]
                kgs = [sb(st, "skgs%d" % i, (128, 768), BF16) for i in range(2)]
                smt = [sb(st, "ssmt%d" % i, (128, 128), BF16) for i in range(4)]
                t1 = [sb(st, "st1%d" % i, (128, 4), F32) for i in range(2)]
                ho = [sb(st, "sho%d" % i, (128, 4, 768), F32) for i in range(2)]
                if fwd:
                    hbg = [sb(st, "shb%d" % i, (128, 4, 768), F32) for i in range(2)]
                    mog = [sb(st, "smo%d" % i, (128, 4, 768), F32) for i in range(2)]
                    MHG = sb(st, "smhg", (128, 768), F32)
                    P.dma("sp", MHG[:], cC_d[:, l * NCC + 16:l * NCC + 16 + 768], writes=["MHG"])
                    ss = [sb(st, "sss%d" % i, (128, 4), F32) for i in range(2)]
                    hbf = [sb(st, "shbf%d" % i, (128, 768), BF16) for i in range(2)]
                    hmo = [sb(st, "shmo%d" % i, (128, 6, 512), BF16) for i in range(2)]
                ng = S // 512
                order = list(range(ng)) if fwd else list(range(ng - 1, -1, -1))

                def ld(gi_):
                    b = gi_ % 2
                    T0 = order[gi_] * 512
                    P.dma("sp", qg[b][:], mqT[:, T0:T0 + 512].rearrange("(c p) t -> p c t", p=96), writes=[("qg", b)])
                    P.dma("sp", kg[b][:], mkT[:, T0:T0 + 512].rearrange("(c p) t -> p c t", p=96), writes=[("kg", b)])
                    for j in range(4):
                        P.dma("sp", vg[b][:, j, :, 0:192], mv[T0 + j * 128:T0 + (j + 1) * 128, :].rearrange("p (h f) -> p h f", h=4),
                              writes=[("vg", b, j)])
                    P.dma("sp", gg[b][:], gts[T0:T0 + 512, :].rearrange("(j p) f -> p j f", p=128), writes=[("gg", b)])
                    if fwd:
                        P.dma("sp", hbg[b][:], hb[T0:T0 + 512, :].rearrange("(j p) f -> p j f", p=128), writes=[("hbg", b)])
                        P.dma("sp", mog[b][:], mo[T0:T0 + 512, :].rearrange("(j p) f -> p j f", p=128), writes=[("mog", b)])

                def chunk_s1(b, j, cn):
                    cs = slice(j * 128, (j + 1) * 128)
                    u = cn % 3
                    kb = cn % 2
                    smu = sm[u]
                    bkA = bank()
                    lf = gg[b][:, j, gofs + 4:gofs + 8]
                    li = gg[b][:, j, gofs:gofs + 4]
                    P.op("pe", lambda e: e.matmul(PS[bkA][:, 0:4], MASK, lf, start=True, stop=True),
                         reads=[("gg", b), "cD"], writes=[("ps", bkA)], sig=False)
                    P.op("pe", lambda e: e.matmul(PS[bkA][:, 4:8], ONEF, lf, start=True, stop=True),
                         reads=[("gg", b), "cD"], writes=[("ps", bkA)])
                    smb_ = smb[u]
                    P.op("act", lambda e: e.copy(smb_[:], PS[bkA][:, 0:8]), reads=[("ps", bkA)], writes=[("smb", u)])
                    P.op("dve", lambda e: e.tensor_tensor(smu[:, 0:4], li, smb_[:, 0:4], ALU.subtract),
                         reads=[("smb", u), ("gg", b)], writes=[("sm", u, 0)])
                    P.op("act", lambda e: e.activation(out=smu[:, 8:12], in_=smb_[:, 0:4], func=AF.Exp, bias=LNC, scale=1.0),
                         reads=[("smb", u), "eps2"], writes=[("sm", u, 2)])
                    P.op("act", lambda e: e.activation(out=smu[:, 12:16], in_=smb_[:, 4:8], func=AF.Exp),
                         reads=[("smb", u)], writes=[("sm", u, 3)])
                    P.op("act", lambda e: e.activation(out=smu[:, 4:8], in_=smu[:, 0:4], func=AF.Exp),
                         reads=[("sm", u, 0)], writes=[("sm", u, 1)])
                    P.op("dve", lambda e: e.tensor_tensor(smu[:, 16:20], smu[:, 4:8], smu[:, 12:16], ALU.mult),
                         reads=[("sm", u, 1), ("sm", u, 3)], writes=[("sm", u, 4)])
                    for c in range(8):
                        P.op("pe", lambda e, c=c: e.transpose(PSB[:, c * 96:(c + 1) * 96], kg[b][:, c, cs], IDB[0:96, 0:96]),
                             reads=[("kg", b), "cDb"], writes=["psb"], sig=(c == 7))
                    for h in range(4):
                        P.op("act", lambda e, h=h: e.mul(kgs[kb][:, h * 192:(h + 1) * 192], PSB[:, h * 192:(h + 1) * 192], smu[:, 16 + h:17 + h]),
                             reads=["psb", ("sm", u, 4)], writes=[("kgs", kb, h)])

                def chunk_rest(b, j, cn):
                    cs = slice(j * 128, (j + 1) * 128)
                    u = cn % 3
                    kb = cn % 2
                    smu = sm[u]
                    tt = t1[cn % 2]
                    bS = []
                    for h in range(4):
                        bkS = bank()
                        bS.append(bkS)
                        for cc in range(2):
                            P.op("pe", lambda e, bkS=bkS, cc=cc, h=h: e.matmul(
                                PS[bkS][:, 0:128], kg[b][:, 2 * h + cc, cs], qg[b][:, 2 * h + cc, cs], start=(cc == 0), stop=(cc == 1)),
                                reads=[("kg", b), ("qg", b)], writes=[("ps", bkS)], sig=(cc == 1))
                    for h in range(4):
                        P.op("dve", lambda e, h=h: e.scalar_tensor_tensor(
                            smt[h][:], PS[bS[h]][:, 0:128], smu[:, 4 + h:5 + h], MASK, ALU.mult, ALU.mult),
                            reads=[("ps", bS[h]), ("sm", u, 1), "cD"], writes=[("smt", h)])
                    bN = []
                    for h in range(4):
                        bkN = bank()
                        bN.append(bkN)
                        P.op("pe", lambda e, bkN=bkN, h=h: e.matmul(
                            PS[bkN][:, 0:193], smt[h][:], vg[b][:, j, h, 0:193], start=True, stop=False),
                            reads=[("smt", h), ("vg", b, j), ("vg1", b)], writes=[("ps", bkN)], sig=False)
                        for cc in range(2):
                            P.op("pe", lambda e, bkN=bkN, cc=cc, h=h: e.matmul(
                                PS[bkN][:, 0:193], qg[b][:, 2 * h + cc, cs], Cb[:, h, cc, 0:193], start=False, stop=(cc == 1)),
                                reads=[("qg", b), ("Cb", h)], writes=[("ps", bkN)], sig=(cc == 1))
                    for h in range(4):
                        P.op("act", lambda e, h=h: e.activation(
                            out=tt[:, h:h + 1], in_=PS[bN[h]][:, 192:193], func=AF.Abs, scale=smu[:, 8 + h:9 + h]),
                            reads=[("ps", bN[h]), ("sm", u, 2)], writes=[("t1", cn % 2)])
                    P.op("dve", lambda e: e.tensor_scalar_max(tt[:], tt[:], 1.0), reads=[("t1", cn % 2)], writes=[("t1", cn % 2)])
                    P.op("dve", lambda e: e.reciprocal(tt[:], tt[:]), reads=[("t1", cn % 2)], writes=[("t1", cn % 2)])
                    P.op("dve", lambda e: e.tensor_tensor(tt[:], tt[:], smu[:, 8:12], ALU.mult),
                         reads=[("t1", cn % 2), ("sm", u, 2)], writes=[("t1", cn % 2)])
                    for h in range(4):
                        P.op("act", lambda e, h=h: e.mul(ho[b][:, j, h * 192:(h + 1) * 192], PS[bN[h]][:, 0:192], tt[:, h:h + 1]),
                             reads=[("ps", bN[h]), ("t1", cn % 2)], writes=[("ho", b, j, h)])
                    for h in range(4):
                        bkC = bank()
                        for cc in range(2):
                            P.op("pe", lambda e, bkC=bkC, cc=cc, h=h: e.matmul(
                                PS[bkC][0:96, cc * 193:(cc + 1) * 193], kgs[kb][:, h * 192 + cc * 96:h * 192 + (cc + 1) * 96], vg[b][:, j, h, 0:193],
                                start=True, stop=True),
                                reads=[("kgs", kb, h), ("vg", b, j), ("vg1", b)], writes=[("ps", bkC)], sig=(cc == 1))
                        Cv = Cs[:, h, :, 0:193]
                        P.op("dve", lambda e, bkC=bkC, Cv=Cv, h=h: e.scalar_tensor_tensor(
                            Cv, Cv, smu[0:96, 12 + h:13 + h], PS[bkC][0:96, 0:386].rearrange("p (a b) -> p a b", a=2), ALU.mult, ALU.add),
                            reads=[("ps", bkC), ("C", h), ("sm", u, 3)], writes=[("C", h)])
                        P.op("act", lambda e, h=h: e.copy(Cb[:, h, :, :], Cs[:, h, :, :]),
                             reads=[("C", h)], writes=[("Cb", h)])
                    if fwd:
                        hv = ho[b][:, j, :]
                        si_ = cn % 2
                        hk = [("ho", b, j, h) for h in range(4)]
                        P.op("dve", lambda e: e.tensor_tensor(hv, hv, hbg[b][:, j, :], ALU.add),
                             reads=hk + [("hbg", b)], writes=hk)
                        for h in range(4):
                            P.op("act", lambda e, h=h: e.activation(
                                out=hbf[si_][:, h * 192:(h + 1) * 192], in_=hv[:, h * 192:(h + 1) * 192], func=AF.Square, accum_out=ss[si_][:, h:h + 1]),
                                reads=[("ho", b, j, h)], writes=[("ss", si_), ("hbf", si_)])
                        P.op("act", lambda e: e.activation(out=ss[si_][:], in_=ss[si_][:], func=AF.Ln, scale=1.0 / 192, bias=EPSC),
                             reads=[("ss", si_), "eps"], writes=[("ss", si_)])
                        P.op("act", lambda e: e.activation(out=ss[si_][:], in_=ss[si_][:], func=AF.Exp, scale=-0.5), reads=[("ss", si_)], writes=[("ss", si_)])
                        for h in range(4):
                            P.op("dve", lambda e, h=h: e.scalar_tensor_tensor(
                                hv[:, h * 192:(h + 1) * 192], hv[:, h * 192:(h + 1) * 192], ss[si_][:, h:h + 1],
                                MHG[:, h * 192:(h + 1) * 192], ALU.mult, ALU.mult),
                                reads=[("ho", b, j, h), ("ss", si_), "MHG"], writes=[("ho", b, j, h)])
                        P.op("dve", lambda e: e.tensor_tensor(hbf[si_][:], hv, mog[b][:, j, :], ALU.mult),
                             reads=hk + [("mog", b)], writes=[("hbf", si_)])
                        for c in range(6):
                            P.op("pe", lambda e, c=c: e.transpose(PSB2[:, c * 128:(c + 1) * 128], hbf[si_][:, c * 128:(c + 1) * 128], IDB),
                                 reads=[("hbf", si_), "cDb"], writes=["psb2"], sig=(c == 5))
                        P.op("act", lambda e: e.copy(hmo[b][:, :, cs], PSB2[:, 0:768].rearrange("p (a b) -> p a b", a=6)),
                             reads=["psb2"], writes=[("hmo", b, j)])

                def store(gi_):
                    b = gi_ % 2
                    T0 = order[gi_] * 512
                    if fwd:
                        P.dma("sp", hmT[:, T0:T0 + 512].rearrange("(c p) t -> p c t", p=128), hmo[b][:],
                              reads=[("hmo", b, j) for j in range(4)])
                    else:
                        P.dma("sp", hb[T0:T0 + 512, :].rearrange("(j p) f -> p j f", p=128), ho[b][:],
                              reads=[("ho", b, j, h) for j in range(4) for h in range(4)])

                chunks = []
                for gi_ in range(ng):
                    for ji, j in enumerate(range(4) if fwd else range(3, -1, -1)):
                        chunks.append((gi_, gi_ % 2, j, gi_ * 4 + ji, ji))
                ld(0)
                chunk_s1(chunks[0][1], chunks[0][2], chunks[0][3])
                for i, (gi_, b, j, cn, ji) in enumerate(chunks):
                    if ji == 0 and gi_ + 1 < ng:
                        ld(gi_ + 1)
                    if i + 1 < len(chunks):
                        nx = chunks[i + 1]
                        chunk_s1(nx[1], nx[2], nx[3])
                    chunk_rest(b, j, cn)
                    if ji == 3:
                        store(gi_)
                P.barrier()

        def phase_x1(l, S, si):
            with ExitStack() as st:
                Wkv, wkk = load_w(st, "wKV", w_kv[l], 128, 8, 1536, 0)
                Wq, wqk = load_w(st, "wXQ", w_in[l], 128, 8, 768, C_XQ)
                xkT = sb(st, "xkT", (96, 8, 256), BF16)
                xv = sb(st, "xv", (128, 2, 768), BF16)
                with ExitStack() as st2:
                    mt = sb(st2, "xmt", (128, 2, D), F32)
                    msq = sb(st2, "xmsq", (128, 2, D), BF16)
                    mss = sb(st2, "xmss", (128, 2), F32)
                    memT = sb(st2, "xmemT", (128, 8, 256), BF16)
                    ksq = sb(st2, "xksq", (96, 2, 256), BF16)
                    krs = sb(st2, "xkrs", (96, 256), F32)
                    P.dma("sp", mt[:], mems[si].rearrange("(j p) f -> p j f", p=128), writes=["mt"])
                    for j in range(2):
                        P.op("act", lambda e, j=j: e.activation(out=msq[:, j, :], in_=mt[:, j, :], func=AF.Square, accum_out=mss[:, j:j + 1]),
                             reads=["mt"], writes=["mss", ("msq", j)])
                    P.op("act", lambda e: e.activation(out=mss[:], in_=mss[:], func=AF.Sqrt, scale=1.0 / D, bias=EPSC),
                         reads=["mss", "eps"], writes=["mss"])
                    P.op("dve", lambda e: e.reciprocal(mss[:], mss[:]), reads=["mss"], writes=["mss"])
                    for j in range(2):
                        P.op("dve", lambda e, j=j: e.tensor_scalar(mt[:, j, :], mt[:, j, :], mss[:, j:j + 1], None, ALU.mult),
                             reads=["mt", "mss"], writes=[("mtn", j)])
                    for c in range(8):
                        bk = bank()
                        for j in range(2):
                            P.op("pe", lambda e, bk=bk, j=j, c=c: e.transpose(PS[bk][:, j * 128:(j + 1) * 128], mt[:, j, c * 128:(c + 1) * 128], IDF),
                                 reads=[("mtn", j), "cD"], writes=[("ps", bk)], sig=(j == 1))
                        P.op("dve", lambda e, bk=bk, c=c: e.tensor_scalar(memT[:, c, :], PS[bk][:, 0:256], cA[:, l, 16 + c:17 + c], None, ALU.mult),
                             reads=[("ps", bk), "cA"], writes=[("memT", c)])
                    mTk = [("memT", c) for c in range(8)]
                    for h in range(4):
                        bq = []
                        for cc in range(2):
                            bk = bank()
                            bq.append(bk)
                            for k in range(8):
                                P.op("pe", lambda e, bk=bk, k=k, h=h, cc=cc: e.matmul(
                                    PS[bk][0:96, 0:256], Wkv[:, k, (2 * h + cc) * 96:(2 * h + cc + 1) * 96], memT[:, k, :],
                                    start=(k == 0), stop=(k == 7)), reads=mTk + wkk, writes=[("ps", bk)], sig=(k == 7))
                            P.op("act", lambda e, bk=bk, cc=cc: e.activation(out=ksq[:, cc, :], in_=PS[bk][0:96, 0:256], func=AF.Square),
                                 reads=[("ps", bk)], writes=[("ksq", cc)])
                        bk2 = bank()
                        for cc in range(2):
                            P.op("pe", lambda e, bk2=bk2, cc=cc: e.matmul(PS[bk2][0:96, 0:256], ONEB[0:96, 0:96], ksq[:, cc, :], start=(cc == 0), stop=(cc == 1)),
                                 reads=[("ksq", cc), "cDb"], writes=[("ps", bk2)], sig=(cc == 1))
                        P.op("act", lambda e, bk2=bk2: e.activation(out=krs[:], in_=PS[bk2][0:96, 0:256], func=AF.Sqrt, scale=1.0 / 192, bias=EPSC[0:96, :]),
                             reads=[("ps", bk2), "eps"], writes=["krs"])
                        P.op("dve", lambda e: e.reciprocal(krs[:], krs[:]), reads=["krs"], writes=["krs"])
                        for cc in range(2):
                            P.op("dve", lambda e, cc=cc, h=h, bq=bq: e.scalar_tensor_tensor(
                                xkT[:, 2 * h + cc, :], PS[bq[cc]][0:96, 0:256], cB[0:96, l, 2 + cc:3 + cc], krs[:], ALU.mult, ALU.mult),
                                reads=[("ps", bq[cc]), "krs", "cB"], writes=[("xkT", h)])
                    for mc in range(2):
                        for gi, (n0, nn) in enumerate(((0, 512), (512, 256))):
                            bk = bank()
                            for k in range(8):
                                P.op("pe", lambda e, bk=bk, k=k, mc=mc, n0=n0, nn=nn: e.matmul(
                                    PS[bk][:, 0:nn], memT[:, k, mc * 128:(mc + 1) * 128], Wkv[:, k, 768 + n0:768 + n0 + nn],
                                    start=(k == 0), stop=(k == 7)), reads=mTk + wkk, writes=[("ps", bk)], sig=(k == 7))
                            P.op("act", lambda e, bk=bk, mc=mc, n0=n0, nn=nn: e.copy(xv[:, mc, n0:n0 + nn], PS[bk][:, 0:nn]),
                                 reads=[("ps", bk)], writes=[("xv", mc, gi)])
                    P.barrier()
                hi = [sb(st, "xhi%d" % i, (128, 8, 512), BF16) for i in range(2)]
                qsq = sb(st, "xqsq", (96, 2, 512), BF16)
                qrs = sb(st, "xqrs", (96, 512), F32)
                xq = [sb(st, "xxq%d" % i, (96, 2, 512), BF16) for i in range(2)]
                pT = [sb(st, "xpT%d" % i, (128, 2, 512), BF16) for i in range(2)]
                rD = [sb(st, "xrD%d" % i, (96, 512), F32) for i in range(2)]
                xo = [sb(st, "xxo%d" % i, (96, 8, 512), BF16) for i in range(2)]
                def ld(it):
                    P.dma("sp", hi[it % 2][:], hT[:, PAD + it * 512:PAD + it * 512 + 512].rearrange("(c p) t -> p c t", p=128), writes=[("hi", it % 2)])

                def body(it):
                    b = it % 2
                    t0 = it * 512
                    for h in range(4):
                        r = h % 2
                        bq = []
                        for cc in range(2):
                            bk = bank()
                            bq.append(bk)
                            for k in range(8):
                                P.op("pe", lambda e, bk=bk, k=k, h=h, cc=cc, b=b: e.matmul(
                                    PS[bk][0:96, :], Wq[:, k, (2 * h + cc) * 96:(2 * h + cc + 1) * 96], hi[b][:, k, :],
                                    start=(k == 0), stop=(k == 7)), reads=[("hi", b)] + wqk, writes=[("ps", bk)], sig=(k == 7))
                            P.op("act", lambda e, bk=bk, cc=cc: e.activation(out=qsq[:, cc, :], in_=PS[bk][0:96, :], func=AF.Square),
                                 reads=[("ps", bk)], writes=[("qsq", cc)])
                        bk2 = bank()
                        for cc in range(2):
                            P.op("pe", lambda e, bk2=bk2, cc=cc: e.matmul(PS[bk2][0:96, :], ONEB[0:96, 0:96], qsq[:, cc, :], start=(cc == 0), stop=(cc == 1)),
                                 reads=[("qsq", cc), "cDb"], writes=[("ps", bk2)], sig=(cc == 1))
                        P.op("act", lambda e, bk2=bk2: e.activation(out=qrs[:], in_=PS[bk2][0:96, :], func=AF.Ln, scale=1.0 / 192, bias=EPSC[0:96, :]),
                             reads=[("ps", bk2), "eps"], writes=["qrs"])
                        P.op("act", lambda e: e.activation(out=qrs[:], in_=qrs[:], func=AF.Exp, scale=-0.5), reads=["qrs"], writes=["qrs"])
                        for cc in range(2):
                            P.op("dve", lambda e, cc=cc, r=r, bq=bq: e.scalar_tensor_tensor(
                                xq[r][:, cc, :], PS[bq[cc]][0:96, :], cB[0:96, l, cc:cc + 1], qrs[:], ALU.mult, ALU.mult),
                                reads=[("ps", bq[cc]), "qrs", "cB"], writes=[("xq", r)])
                        for mc in range(2):
                            bk = bank()
                            for cc in range(2):
                                P.op("pe", lambda e, bk=bk, cc=cc, mc=mc, h=h, r=r: e.matmul(
                                    PS[bk][:, :], xkT[:, 2 * h + cc, mc * 128:(mc + 1) * 128], xq[r][:, cc, :], start=(cc == 0), stop=(cc == 1)),
                                    reads=[("xq", r), ("xkT", h)], writes=[("ps", bk)], sig=(cc == 1))
                            P.op("act", lambda e, bk=bk, mc=mc, r=r: e.activation(out=pT[r][:, mc, :], in_=PS[bk][:, :], func=AF.Exp, scale=192.0 ** -0.5),
                                 reads=[("ps", bk)], writes=[("pT", r, mc)])
                        bo = []
                        for cc in range(2):
                            bk = bank()
                            bo.append(bk)
                            for mc in range(2):
                                P.op("pe", lambda e, bk=bk, cc=cc, mc=mc, h=h, r=r: e.matmul(
                                    PS[bk][0:96, :], xv[:, mc, h * 192 + cc * 96:h * 192 + (cc + 1) * 96], pT[r][:, mc, :], start=(mc == 0), stop=(mc == 1)),
                                    reads=[("pT", r, mc), ("xv", mc, 0), ("xv", mc, 1)], writes=[("ps", bk)], sig=(mc == 1))
                        bkD = bank()
                        for mc in range(2):
                            P.op("pe", lambda e, bkD=bkD, mc=mc, r=r: e.matmul(PS[bkD][0:96, :], ONEB[:, 0:96], pT[r][:, mc, :], start=(mc == 0), stop=(mc == 1)),
                                 reads=[("pT", r, mc), "cDb"], writes=[("ps", bkD)], sig=(mc == 1))
                        P.op("act", lambda e, bkD=bkD, r=r: e.activation(out=rD[r][:], in_=PS[bkD][0:96, :], func=AF.Ln), reads=[("ps", bkD)], writes=[("rD", r)])
                        P.op("act", lambda e, r=r: e.activation(out=rD[r][:], in_=rD[r][:], func=AF.Exp, scale=-1.0), reads=[("rD", r)], writes=[("rD", r)])
                        for cc in range(2):
                            P.op("dve", lambda e, cc=cc, h=h, b=b, r=r, bo=bo: e.tensor_tensor(xo[b][:, 2 * h + cc, :], PS[bo[cc]][0:96, :], rD[r][:], ALU.mult),
                                 reads=[("ps", bo[cc]), ("rD", r)], writes=[("xo", b, h, cc)])
                    P.dma("sp", xoT[:, t0:t0 + 512].rearrange("(c p) t -> p c t", p=96), xo[b][:],
                          reads=[("xo", b, h, cc) for h in range(4) for cc in range(2)])
                run_tiles(S // 512, ld, body)
                P.barrier()

        def phase_g1(l, S):
            T = 256
            with ExitStack() as st:
                Wg, wgk = load_w(st, "wG", w_in[l], 128, 8, 3072, C_G)
                Wa, wak = load_w(st, "wBa", w_br[l][0:768, :], 128, 6, D)
                Wm, wmk = load_w(st, "wBm", w_br[l][768:1536, :], 128, 6, D)
                Wx, wxk = load_w(st, "wBx", w_br[l][1536:2304, :], 96, 8, D)
                Wo, wok = load_w(st, "wO", w_out[l], 128, 8, D)
                hi = [sb(st, "ghi%d" % i, (128, 8, T), BF16) for i in range(2)]
                ai = [sb(st, "gai%d" % i, (128, 6, T), BF16) for i in range(2)]
                mi = [sb(st, "gmi%d" % i, (128, 6, T), BF16) for i in range(2)]
                xi_ = [sb(st, "gxi%d" % i, (96, 8, T), BF16) for i in range(2)]
                xt = [sb(st, "gxt%d" % i, (128, 8, T), F32) for i in range(2)]
                mg = sb(st, "gmg", (128, 8, T), BF16)
                s01 = [sb(st, "gs01%d" % i, (128, 2 * T), F32) for i in range(2)]
                s2 = [sb(st, "gs2%d" % i, (128, T), F32) for i in range(2)]
                mm = [sb(st, "gmm%d" % i, (128, 3, T), F32) for i in range(2)]
                nb = norm_bufs(st, T)
                def ld(it):
                    b = it % 2
                    t0 = it * T
                    P.dma("sp", hi[b][:], hT[:, PAD + t0:PAD + t0 + T].rearrange("(c p) t -> p c t", p=128), writes=[("hi", b)])
                    P.dma("sp", ai[b][:], attT[:, t0:t0 + T].rearrange("(c p) t -> p c t", p=128), writes=[("ai", b)])
                    P.dma("sp", mi[b][:], hmT[:, t0:t0 + T].rearrange("(c p) t -> p c t", p=128), writes=[("mi", b)])
                    P.dma("sp", xi_[b][:], xoT[:, t0:t0 + T].rearrange("(c p) t -> p c t", p=96), writes=[("xi", b)])
                    P.dma("sp", xt[b][:], xT[:, t0:t0 + T].rearrange("(c p) t -> p c t", p=128), writes=[("xt", b)])

                def body(it):
                    b = it % 2
                    t0 = it * T
                    for oc in range(8):
                        u = oc % 2
                        ocs = slice(oc * 128, (oc + 1) * 128)
                        bA, bB, bC, bD = bank(), bank(), bank(), bank()
                        slots = [(bA, 0), (bA, T), (bB, 0)]
                        for br in range(3):
                            bk, c0 = slots[br]
                            for k in range(8):
                                P.op("pe", lambda e, bk=bk, c0=c0, k=k, br=br, b=b, ocs=ocs: e.matmul(
                                    PS[bk][:, c0:c0 + T], Wg[:, k, br * D + ocs.start:br * D + ocs.stop], hi[b][:, k, :],
                                    start=(k == 0), stop=(k == 7)), reads=[("hi", b)] + wgk, writes=[("ps", bk)], sig=(k == 7))
                        for k in range(6):
                            P.op("pe", lambda e, k=k, b=b, ocs=ocs, bC=bC: e.matmul(PS[bC][:, 0:T], Wa[:, k, ocs], ai[b][:, k, :], start=(k == 0), stop=(k == 5)),
                                 reads=[("ai", b)] + wak, writes=[("ps", bC)], sig=(k == 5))
                        for k in range(6):
                            P.op("pe", lambda e, k=k, b=b, ocs=ocs, bC=bC: e.matmul(PS[bC][:, T:2 * T], Wm[:, k, ocs], mi[b][:, k, :], start=(k == 0), stop=(k == 5)),
                                 reads=[("mi", b)] + wmk, writes=[("ps", bC)], sig=(k == 5))
                        for k in range(8):
                            P.op("pe", lambda e, k=k, b=b, ocs=ocs, bD=bD: e.matmul(PS[bD][0:128, 0:T], Wx[:, k, ocs], xi_[b][:, k, :], start=(k == 0), stop=(k == 7)),
                                 reads=[("xi", b)] + wxk, writes=[("ps", bD)], sig=(k == 7))
                        P.op("act", lambda e, u=u, bA=bA: e.activation(out=s01[u][:], in_=PS[bA][:, :], func=AF.Sigmoid),
                             reads=[("ps", bA)], writes=[("s01", u)])
                        P.op("act", lambda e, u=u, bB=bB: e.activation(out=s2[u][:], in_=PS[bB][:, 0:T], func=AF.Sigmoid),
                             reads=[("ps", bB)], writes=[("s2", u)])
                        P.op("dve", lambda e, u=u, bC=bC: e.tensor_tensor(mm[u][:, 0, :], PS[bC][:, 0:T], s01[u][:, 0:T], ALU.mult),
                             reads=[("ps", bC), ("s01", u)], writes=[("mm", u, 0)])
                        P.op("dve", lambda e, u=u, bC=bC: e.tensor_tensor(mm[u][:, 1, :], PS[bC][:, T:2 * T], s01[u][:, T:2 * T], ALU.mult),
                             reads=[("ps", bC), ("s01", u)], writes=[("mm", u, 1)])
                        P.op("dve", lambda e, u=u, bD=bD: e.tensor_tensor(mm[u][:, 2, :], PS[bD][:, 0:T], s2[u][:], ALU.mult),
                             reads=[("ps", bD), ("s2", u)], writes=[("mm", u, 2)])
                        P.op("dve", lambda e, u=u: e.tensor_tensor(mm[u][:, 0, :], mm[u][:, 0, :], mm[u][:, 1, :], ALU.add),
                             reads=[("mm", u, 0), ("mm", u, 1)], writes=[("mm", u, 0)])
                        P.op("dve", lambda e, u=u, oc=oc: e.tensor_tensor(mg[:, oc, :], mm[u][:, 0, :], mm[u][:, 2, :], ALU.add),
                             reads=[("mm", u, 0), ("mm", u, 2)], writes=[("mg", oc)])
                    for oc in range(8):
                        bk = bank()
                        for k in range(8):
                            P.op("pe", lambda e, bk=bk, k=k, oc=oc: e.matmul(PS[bk][:, 0:T], Wo[:, k, oc * 128:(oc + 1) * 128], mg[:, k, :], start=(k == 0), stop=(k == 7)),
                                 reads=[("mg", kk) for kk in range(8)] + wok, writes=[("ps", bk)], sig=(k == 7))
                        P.op("dve", lambda e, bk=bk, oc=oc, b=b: e.tensor_tensor(xt[b][:, oc, :], PS[bk][:, 0:T], xt[b][:, oc, :], ALU.add),
                             reads=[("ps", bk), ("xt", b)], writes=[("xn", b, oc)])
                    P.dma("sp", xT[:, t0:t0 + T].rearrange("(c p) t -> p c t", p=128), xt[b][:],
                          reads=[("xn", b, oc) for oc in range(8)] + [("xt", b)])
                    fused_norm(nb, xt[b][:], [("xn", b, oc) for oc in range(8)] + [("xt", b)], T, l, 8, t0, b)
                run_tiles(S // T, ld, body)
                P.barrier()

        def phase_f1(l, S):
            with ExitStack() as st:
                W, wk = load_w(st, "wUp", w_up[l], 128, 8, 2 * FFN, 0)
                hi = [sb(st, "fhi%d" % i, (128, 8, 512), BF16) for i in range(2)]
                ta = [sb(st, "fta%d" % i, (128, 512), F32) for i in range(3)]
                tv = [sb(st, "ftv%d" % i, (128, 512), F32) for i in range(3)]
                go = [sb(st, "fgo%d" % i, (128, 22, 512), BF16) for i in range(2)]
                tl = [(t, min(510, S - t)) for t in range(0, S, 510)]

                def ld(it):
                    t0, ntok = tl[it]
                    P.dma("sp", hi[it % 2][:, :, 0:ntok + 2],
                          hT[:, PAD + t0 - 1:PAD + t0 + 1 + ntok].rearrange("(c p) t -> p c t", p=128), writes=[("hi", it % 2)])

                def body(it):
                    t0, ntok = tl[it]
                    ncol = ntok + 2
                    b = it % 2
                    pend = None
                    for fc in range(22):
                        u = fc % 3
                        for part, (tt_, key) in enumerate(((ta[u], "ta"), (tv[u], "tv"))):
                            ch = part * 22 + fc
                            bk = bank()
                            for k in range(8):
                                P.op("pe", lambda e, bk=bk, k=k, ch=ch: e.matmul(
                                    PS[bk][:, 0:ncol], W[:, k, ch * 128:(ch + 1) * 128], hi[b][:, k, 0:ncol], start=(k == 0), stop=(k == 7)),
                                    reads=[("hi", b)] + wk, writes=[("ps", bk)], sig=(k == 7))
                            w0, w1, w2 = (cA[:, l, 26 + jj * 44 + ch:26 + jj * 44 + ch + 1] for jj in range(3))
                            bb = cA[:, l, 158 + ch:158 + ch + 1]
                            P.op("act", lambda e, bk=bk, tt_=tt_, w1=w1, bb=bb: e.activation(
                                out=tt_[:, 0:ntok], in_=PS[bk][:, 1:1 + ntok], func=AF.Identity, scale=w1, bias=bb),
                                reads=[("ps", bk), "cA"], writes=[(key, u)])
                            if part == 1 and pend is not None:
                                pend()
                            P.op("dve", lambda e, bk=bk, tt_=tt_, w0=w0: e.scalar_tensor_tensor(
                                tt_[:, 0:ntok], PS[bk][:, 0:ntok], w0, tt_[:, 0:ntok], ALU.mult, ALU.add),
                                reads=[("ps", bk), (key, u), "cA"], writes=[(key, u)])
                            P.op("dve", lambda e, bk=bk, tt_=tt_, w2=w2: e.scalar_tensor_tensor(
                                tt_[:, 0:ntok], PS[bk][:, 2:2 + ntok], w2, tt_[:, 0:ntok], ALU.mult, ALU.add),
                                reads=[("ps", bk), (key, u), "cA"], writes=[(key, u)])

                        def pend(u=u, fc=fc):
                            P.op("act", lambda e: e.activation(out=ta[u][:, 0:ntok], in_=ta[u][:, 0:ntok], func=AF.Gelu_apprx_tanh),
                                 reads=[("ta", u)], writes=[("ta", u)])
                            P.op("pool", lambda e: e.tensor_tensor(go[b][:, fc, 0:ntok], ta[u][:, 0:ntok], tv[u][:, 0:ntok], ALU.mult),
                                 reads=[("ta", u), ("tv", u)], writes=[("go", b, fc)])
                    pend()
                    P.dma("sp", gT[:, t0:t0 + ntok].rearrange("(c p) t -> p c t", p=128), go[b][:, :, 0:ntok],
                          reads=[("go", b, fc) for fc in range(22)])
                run_tiles(len(tl), ld, body)
                P.barrier()

        def phase_f2(l, S, fuse_next):
            with ExitStack() as st:
                W, wk = load_w(st, "wDn", w_dn[l], 128, 22, D, 0)
                gi = [sb(st, "dgi%d" % i, (128, 22, 512), BF16) for i in range(2)]
                xt = [sb(st, "dxt%d" % i, (128, 8, 512), F32) for i in range(2)]
                nb = norm_bufs(st, 512) if fuse_next else None
                def ld(it):
                    b = it % 2
                    t0 = it * 512
                    P.dma("sp", gi[b][:], gT[:, t0:t0 + 512].rearrange("(c p) t -> p c t", p=128), writes=[("gi", b)])
                    P.dma("sp", xt[b][:], xT[:, t0:t0 + 512].rearrange("(c p) t -> p c t", p=128), writes=[("xt", b)])

                def body(it):
                    b = it % 2
                    t0 = it * 512
                    for oc in range(8):
                        bk = bank()
                        for k in range(22):
                            P.op("pe", lambda e, bk=bk, k=k, oc=oc, b=b: e.matmul(PS[bk][:, :], W[:, k, oc * 128:(oc + 1) * 128], gi[b][:, k, :], start=(k == 0), stop=(k == 21)),
                                 reads=[("gi", b)] + wk, writes=[("ps", bk)], sig=(k == 21))
                        P.op("dve", lambda e, bk=bk, oc=oc, b=b: e.tensor_tensor(xt[b][:, oc, :], PS[bk][:, :], xt[b][:, oc, :], ALU.add),
                             reads=[("ps", bk), ("xt", b)], writes=[("xn", b, oc)])
                    P.dma("sp", xT[:, t0:t0 + 512].rearrange("(c p) t -> p c t", p=128), xt[b][:],
                          reads=[("xn", b, oc) for oc in range(8)] + [("xt", b)])
                    if fuse_next:
                        fused_norm(nb, xt[b][:], [("xn", b, oc) for oc in range(8)] + [("xt", b)], 512, l + 1, 0, t0, b)
                run_tiles(S // 512, ld, body)
                P.barrier()

        phases = []
        for si, S in enumerate(seq_lens):
            phases.append(("in", lambda si=si, S=S: phase_in(si, S)))
            for l in range(depth):
                if l == 0:
                    phases.append(("n1", lambda l=l, S=S: phase_norm(l, S, 0)))
                phases.append(("a1", lambda l=l, S=S: phase_a1(l, S)))
                phases.append(("a2", lambda l=l, S=S: phase_a2(l, S)))
                phases.append(("m1a", lambda l=l, S=S: phase_m1a(l, S)))
                phases.append(("m1b", lambda l=l, S=S: phase_m1b(l, S)))
                phases.append(("m2", lambda l=l, S=S: phase_scan(l, S, False)))
                phases.append(("m3", lambda l=l, S=S: phase_scan(l, S, True)))
                phases.append(("x1", lambda l=l, S=S, si=si: phase_x1(l, S, si)))
                phases.append(("g1", lambda l=l, S=S: phase_g1(l, S)))
                phases.append(("f1", lambda l=l, S=S: phase_f1(l, S)))
                phases.append(("f2", lambda l=l, S=S: phase_f2(l, S, l + 1 < depth)))
            phases.append(("out", lambda si=si, S=S: phase_out(si, S)))
        for name, fn in phases:
            fn()
            if upto is not None and name == upto:
                break
        P.barrier()
    return nc


def kernel(**inputs):
    f = lambda k: np.ascontiguousarray(np.asarray(inputs[k], dtype=np.float32))
    xp, xsm = f("x_prompt"), f("x_sample")
    mp, ms = f("mem_prompt"), f("mem_sample")
    nc = build([xp.shape[1], xsm.shape[1]], depth=NL)
    cA, cB, cC = pack_params(inputs)
    eb, cd = host_consts()
    common = {"w_in": f("w_in"), "w_mem_kv": f("w_mem_kv"),
              "w_branch": f("w_branch").reshape(NL, 2304, D), "w_out": f("w_out"),
              "w_up": f("w_up"), "w_down": f("w_down"),
              "cA": cA, "cB": cB, "cC": cC, "cEB": eb, "cD": cd, "cM": pack_cm(inputs)}
    n = 8
    nsm = xsm.shape[0]
    in_maps = []
    for c in range(n):
        m = dict(common)
        m["x0"] = xp[c]
        m["mem0"] = mp[c]
        m["x1"] = xsm[c % nsm]
        m["mem1"] = ms[c % nsm]
        in_maps.append(m)
    res = run_bass_kernel_spmd(nc, in_maps, core_ids=list(range(n)))
    y_prompt = np.stack([np.asarray(res.results[c]["y0"], dtype=np.float32) for c in range(n)])
    y_sample = np.stack([np.asarray(res.results[c]["y1"], dtype=np.float32) for c in range(nsm)])
    return (y_prompt, y_sample)
```

```python
import bisect
import math
import os
from contextlib import ExitStack

import numpy as np
import concourse.bass as bass
import concourse.mybir as mybir
from concourse.bass_utils import run_bass_kernel_spmd

F32 = mybir.dt.float32
BF16 = mybir.dt.bfloat16
AF = mybir.ActivationFunctionType
ALU = mybir.AluOpType
AX = mybir.AxisListType

D = 1024
NL = 4
NMEM = 256
IN_DIM = 9232
FFN = 2816
EPS = 1e-6
C_AQ, C_AK, C_AV, C_MQ, C_MK, C_MV, C_MO, C_MIF, C_XQ, C_G = 0, 768, 1536, 2304, 3072, 3840, 4608, 5376, 5392, 6160
PAD = 64
DILS = (1, 4, 16)


class _Eng:
    def __init__(self, name, obj, sem):
        self.name, self.obj, self.sem = name, obj, sem
        self.seq = 0
        self.sig_seqs = []
        self.waited = {}


class Prog:
    def __init__(self, nc, es, n_sp=6, n_pool=4):
        self.nc = nc
        self.E = {}
        for name, obj in (("pe", nc.tensor), ("act", nc.scalar), ("dve", nc.vector),
                          ("pool", nc.gpsimd), ("sp", nc.sync)):
            sem = es.enter_context(nc.semaphore("s_" + name))
            self.E[name] = _Eng(name, obj, sem)
        self.dsems = []
        self.dq = {}
        for q, n in (("sp", n_sp), ("pool", n_pool)):
            lst = []
            for i in range(n):
                self.dsems.append(es.enter_context(nc.semaphore("d_%s%d" % (q, i))))
                lst.append([len(self.dsems) - 1, 0])
            self.dq[q] = lst
        self.dq_next = {"sp": 0, "pool": 0}
        self.lastw = {}
        self.readers = {}
        self.n_ops = 0

    def _wait(self, e, tok):
        if tok[0] == "c":
            src = self.E[tok[1]]
            i = bisect.bisect_left(src.sig_seqs, tok[2])
            assert i < len(src.sig_seqs), "dependency on an op with no later signal on %s" % src.name
            cnt, key, sem = i + 1, src.name, src.sem
        else:
            key, sem, cnt = ("d", tok[1]), self.dsems[tok[1]], tok[2]
        if e.waited.get(key, 0) >= cnt:
            return
        e.obj.wait_ge(sem, cnt)
        e.waited[key] = cnt

    def _deps(self, reads, writes):
        deps = []
        for r in reads:
            t = self.lastw.get(r)
            if t is not None:
                deps.append((t, "raw"))
        for w in writes:
            t = self.lastw.get(w)
            if t is not None:
                deps.append((t, "waw"))
            rd = self.readers.get(w)
            if rd:
                for t in rd.values():
                    deps.append((t, "war"))
        return deps

    def _commit(self, tok, reads, writes):
        key = tok[1] if tok[0] == "c" else tok
        for r in reads:
            self.readers.setdefault(r, {})[key] = tok
        for w in writes:
            self.lastw[w] = tok
            self.readers[w] = {}

    def op(self, eng, fn, reads=(), writes=(), sig=True):
        e = self.E[eng]
        for tok, kind in self._deps(reads, writes):
            if tok[0] == "c" and tok[1] == eng:
                if eng == "pe" or (kind != "raw" and eng != "pool"):
                    continue
            self._wait(e, tok)
        ins = fn(e.obj)
        e.seq += 1
        tok = ("c", eng, e.seq)
        if sig:
            ins.then_inc(e.sem, 1)
            e.sig_seqs.append(e.seq)
        self._commit(tok, reads, writes)
        self.n_ops += 1
        return tok

    def dma(self, q, out, in_, reads=(), writes=()):
        e = self.E[q]
        for tok, kind in self._deps(reads, writes):
            self._wait(e, tok)
        pool = self.dq[q]
        i = self.dq_next[q]
        self.dq_next[q] = (i + 1) % len(pool)
        ent = pool[i]
        if ent[1] > 0:
            self._wait(e, ("d", ent[0], ent[1]))
        ent[1] += 16
        e.obj.dma_start(out=out, in_=in_).then_inc(self.dsems[ent[0]], 16)
        tok = ("d", ent[0], ent[1])
        self._commit(tok, reads, writes)
        self.n_ops += 1
        return tok

    def barrier(self):
        for e in self.E.values():
            for src in self.E.values():
                if (src is e and e.name != "pool") or not src.sig_seqs:
                    continue
                cnt = len(src.sig_seqs)
                if e.waited.get(src.name, 0) < cnt:
                    e.obj.wait_ge(src.sem, cnt)
                    e.waited[src.name] = cnt
            for lst in self.dq.values():
                for semi, val in lst:
                    if val > 0 and e.waited.get(("d", semi), 0) < val:
                        e.obj.wait_ge(self.dsems[semi], val)
                        e.waited[("d", semi)] = val
        self.lastw.clear()
        self.readers.clear()


def ssl(start, n, step):
    return slice(start, start + (n - 1) * step + 1, step)


def alibi_slopes():
    return [2.0 ** (-8.0 * (h + 1) / 12.0) for h in range(12)]


def host_consts():
    kk = np.arange(128)[:, None]
    qq = np.arange(128)[None, :]
    eb = np.zeros((128, 12, 3, 128), np.float32)
    sl = alibi_slopes()
    for h in range(12):
        dil = DILS[h // 4]
        for j in range(3):
            delta = 128 * (j - 1) + kk - qq
            val = np.exp(-sl[h] * dil * np.abs(delta).astype(np.float64))
            eb[:, h, j, :] = np.where(np.abs(delta) <= 64, val, 0.0)
    cd = np.zeros((128, 5, 128), np.float32)
    cd[:, 0, :] = (kk <= qq)
    cd[:, 1, :] = (kk >= qq)
    cd[:, 2, :] = np.eye(128)
    cd[:, 3, :] = 1.0
    bd = np.zeros((128, 128), np.float32)
    bd[:64, :64] = 1.0
    bd[64:, 64:] = 1.0
    cd[:, 4, :] = bd
    return eb.reshape(128, -1), cd.reshape(128, -1)


def pack_params(inp):
    L = NL
    f = lambda k: np.asarray(inp[k], np.float32)
    cA = np.zeros((128, L, 8 * 3 + 2 + 3 * 44 + 44), np.float32)
    for l in range(L):
        o = 0
        for key in ("norm_mix_g", "norm_ffn_g", "norm_mem_g"):
            cA[:, l, o:o + 8] = f(key)[l].reshape(8, 128).T
            o += 8
        cA[:, l, o] = np.tile(f("att_q_g")[l], 2); o += 1
        cA[:, l, o] = np.tile(f("att_k_g")[l], 2); o += 1
        cA[:, l, o:o + 132] = f("ffn_conv_w")[l].reshape(3, 44, 128).transpose(2, 0, 1).reshape(128, 132); o += 132
        cA[:, l, o:o + 44] = f("ffn_conv_b")[l].reshape(44, 128).T; o += 44
    cB = np.zeros((128, L, 2 + 2 + 48 + 16), np.float32)
    for l in range(L):
        cB[:96, l, 0:2] = f("xatt_q_g")[l].reshape(2, 96).T
        cB[:96, l, 2:4] = f("xatt_k_g")[l].reshape(2, 96).T
        cB[:96, l, 4:52] = f("mlstm_conv_w")[l].reshape(3, 16, 96).transpose(2, 0, 1).reshape(96, 48)
        cB[:96, l, 52:68] = f("mlstm_conv_b")[l].reshape(16, 96).T
    cC = np.zeros((128, L, 16 + 768), np.float32)
    for l in range(L):
        cC[:, l, 0:16] = f("mlstm_gate_b")[l][None, :]
        cC[:, l, 16:] = f("mlstm_h_g")[l][None, :]
    return cA.reshape(128, -1), cB.reshape(128, -1), cC.reshape(128, -1)


def pack_cm(inp):
    f = lambda k: np.asarray(inp[k], np.float32)
    cM = np.zeros((128, NL, 48), np.float32)
    for l in range(NL):
        cM[:, l, 0:36] = f("mlstm_conv_w")[l].reshape(3, 12, 128).transpose(2, 0, 1).reshape(128, 36)
        cM[:, l, 36:48] = f("mlstm_conv_b")[l].reshape(12, 128).T
    return cM.reshape(128, -1)


NA = 8 * 3 + 2 + 132 + 44
NB = 68
NCC = 16 + 768


def build(seq_lens, depth=NL, dbg=(), upto=None):
    nc = bass.Bass("TRN2", target_bir_lowering=False)
    Smax = max(seq_lens)
    nseq = len(seq_lens)
    dt_in = lambda name, shape: nc.dram_tensor(name, list(shape), F32, kind="ExternalInput").ap()
    xs = [dt_in("x%d" % i, (seq_lens[i], D)) for i in range(nseq)]
    mems = [dt_in("mem%d" % i, (NMEM, D)) for i in range(nseq)]
    ys = [nc.dram_tensor("y%d" % i, [seq_lens[i], D], F32, kind="ExternalOutput").ap() for i in range(nseq)]
    w_in = dt_in("w_in", (NL, D, IN_DIM))
    w_kv = dt_in("w_mem_kv", (NL, D, 1536))
    w_br = dt_in("w_branch", (NL, 2304, D))
    w_out = dt_in("w_out", (NL, D, D))
    w_up = dt_in("w_up", (NL, D, 2 * FFN))
    w_dn = dt_in("w_down", (NL, FFN, D))
    cA_d = dt_in("cA", (128, NL * NA))
    cB_d = dt_in("cB", (128, NL * NB))
    cC_d = dt_in("cC", (128, NL * NCC))
    eb_d = dt_in("cEB", (128, 12 * 3 * 128))
    cM_d = dt_in("cM", (128, NL * 48))
    cd_d = dt_in("cD", (128, 5 * 128))

    def scratch(name, shape, dt):
        kind = "ExternalOutput" if name in dbg else "Internal"
        return nc.dram_tensor(name, list(shape), dt, kind=kind).ap()

    SP = Smax + 2 * PAD
    xT = scratch("xT", (D, Smax), F32)
    hT = scratch("hT", (D, SP), BF16)
    qaT = scratch("qaT", (768, Smax), BF16)
    kaT = scratch("kaT", (768, Smax), BF16)
    va = scratch("va", (Smax, 768), BF16)
    attT = scratch("attT", (768, Smax), BF16)
    mqT = scratch("mqT", (768, Smax), BF16)
    mkT = scratch("mkT", (768, Smax), BF16)
    mv = scratch("mv", (Smax, 768), BF16)
    mo = scratch("mo", (Smax, 768), F32)
    gts = scratch("gts", (Smax, 16), F32)
    hb = scratch("hb", (Smax, 768), F32)
    hmT = scratch("hmT", (768, Smax), BF16)
    xoT = scratch("xoT", (768, Smax), BF16)
    gT = scratch("gT", (FFN, Smax), BF16)

    es = ExitStack()
    with es:
        P = Prog(nc, es)
        uid = [0]

        def sb(st, name, shape, dt):
            uid[0] += 1
            return st.enter_context(nc.sbuf_tensor("sb%d_%s" % (uid[0], name), list(shape), dt))
        cA = sb(es, "cA", (128, NL, NA), F32)
        cB = sb(es, "cB", (128, NL, NB), F32)
        cM = sb(es, "cM", (128, NL, 48), F32)
        cD = sb(es, "cDs", (128, 5, 128), F32)
        cDb = sb(es, "cDb", (128, 5, 128), BF16)
        zer = sb(es, "zer", (128, 8, PAD), BF16)
        NPS = 6
        PS = [es.enter_context(nc.psum_tensor("ps%d" % i, [128, 512], F32)) for i in range(NPS)]
        PSB = es.enter_context(nc.psum_tensor("psb", [128, 1024], BF16))
        PSB2 = es.enter_context(nc.psum_tensor("psb2", [128, 1024], BF16))
        TRIF, TRIB, IDF, ONEF = cD[:, 0, :], cD[:, 1, :], cD[:, 2, :], cD[:, 3, :]
        IDB, ONEB, BDB = cDb[:, 2, :], cDb[:, 3, :], cDb[:, 4, :]
        P.dma("sp", cA[:].rearrange("p l n -> p (l n)"), cA_d, writes=["cA"])
        P.dma("sp", cB[:].rearrange("p l n -> p (l n)"), cB_d, writes=["cB"])
        P.dma("sp", cM[:].rearrange("p l n -> p (l n)"), cM_d, writes=["cM"])
        P.dma("sp", cD[:].rearrange("p l n -> p (l n)"), cd_d, writes=["cD"])
        P.dma("pool", cDb[:].rearrange("p l n -> p (l n)"), cd_d, writes=["cDb"])
        P.op("dve", lambda e: e.memset(zer[:], 0.0), writes=["zer"])
        epsT = sb(es, "epsT", (128, 4), F32)
        P.op("dve", lambda e: e.memset(epsT[:, 0:1], EPS), writes=["eps"])
        P.op("dve", lambda e: e.memset(epsT[:, 1:2], 1.0), writes=["eps1"])
        P.op("dve", lambda e: e.memset(epsT[:, 2:3], math.log(192.0 ** -0.5)), writes=["eps2"])
        EPSC = epsT[:, 0:1]
        ONEC = epsT[:, 1:2]
        LNC = epsT[:, 2:3]
        P.barrier()

        psn = [0]

        def run_tiles(n, ld, body):
            if n:
                ld(0)
            for it in range(n):
                if it + 1 < n:
                    ld(it + 1)
                body(it)

        def bank():
            psn[0] = (psn[0] + 1) % NPS
            return psn[0]

        def fused_norm(nb, xtile, xkeys, T, lg, goff, t0, b):
            nsq, nrs, nho = nb
            P.op("act", lambda e: e.activation(out=nsq[:, :, 0:T], in_=xtile, func=AF.Square),
                 reads=xkeys, writes=["nsq"])
            bk = bank()
            for c in range(8):
                P.op("pe", lambda e, c=c: e.matmul(PS[bk][:, 0:T], ONEB, nsq[:, c, 0:T], start=(c == 0), stop=(c == 7)),
                     reads=["nsq", "cDb"], writes=[("ps", bk)], sig=(c == 7))
            P.op("act", lambda e: e.activation(out=nrs[:, 0:T], in_=PS[bk][:, 0:T], func=AF.Ln, scale=1.0 / D, bias=EPSC),
                 reads=[("ps", bk)], writes=["nrs"])
            P.op("act", lambda e: e.activation(out=nrs[:, 0:T], in_=nrs[:, 0:T], func=AF.Exp, scale=-0.5), reads=["nrs"], writes=["nrs"])
            for c in range(8):
                P.op("dve", lambda e, c=c: e.scalar_tensor_tensor(
                    nho[b][:, c, 0:T], xtile[:, c, :], cA[:, lg, goff + c:goff + c + 1], nrs[:, 0:T], ALU.mult, ALU.mult),
                    reads=xkeys + ["nrs", "cA"], writes=[("nho", b, c)])
            P.dma("sp", hT[:, PAD + t0:PAD + t0 + T].rearrange("(c p) t -> p c t", p=128), nho[b][:, :, 0:T],
                  reads=[("nho", b, c) for c in range(8)])

        def norm_bufs(st, T):
            return (sb(st, "fnsq", (128, 8, T), BF16), sb(st, "fnrs", (128, T), F32),
                    [sb(st, "fnho%d" % i, (128, 8, T), BF16) for i in range(2)])

        def load_w(st, name, src, kp, nk, ncols, c0=0):
            t = sb(st, name, (kp, nk, ncols), BF16)
            v = src.rearrange("(k p) n -> p k n", p=kp)
            step = max(1, 4096 // ncols)
            for k0 in range(0, nk, step):
                k1 = min(nk, k0 + step)
                P.dma("pool", t[:, k0:k1, :], v[:, k0:k1, c0:c0 + ncols], writes=[(name, k0)])
            return t, [(name, k0) for k0 in range(0, nk, step)]

        def phase_in(si, S):
            with ExitStack() as st:
                xin = [sb(st, "xin%d" % i, (128, 4, D), F32) for i in range(2)]
                xo = [sb(st, "xo%d" % i, (128, 8, 512), F32) for i in range(2)]
                P.dma("sp", hT[:, 0:PAD].rearrange("(c p) t -> p c t", p=128), zer[:], reads=["zer"])
                P.dma("sp", hT[:, PAD + S:PAD + S + PAD].rearrange("(c p) t -> p c t", p=128), zer[:], reads=["zer"])
                nt = S // 512

                def ld(it):
                    P.dma("sp", xin[it % 2][:], xs[si][it * 512:it * 512 + 512, :].rearrange("(j p) f -> p j f", p=128),
                          writes=[("xin", it % 2)])

                def body(it):
                    b = it % 2
                    t0 = it * 512
                    for c in range(8):
                        bk = bank()
                        for j in range(4):
                            P.op("pe", lambda e, bk=bk, j=j, c=c, b=b: e.transpose(
                                PS[bk][:, j * 128:(j + 1) * 128], xin[b][:, j, c * 128:(c + 1) * 128], IDF),
                                reads=[("xin", b), "cD"], writes=[("ps", bk)], sig=(j == 3))
                        eng = "act" if c % 2 else "dve"
                        if eng == "act":
                            P.op("act", lambda e, bk=bk, c=c, b=b: e.copy(xo[b][:, c, :], PS[bk][:]),
                                 reads=[("ps", bk)], writes=[("xo", b, c)])
                        else:
                            P.op("dve", lambda e, bk=bk, c=c, b=b: e.tensor_copy(xo[b][:, c, :], PS[bk][:]),
                                 reads=[("ps", bk)], writes=[("xo", b, c)])
                    P.dma("sp", xT[:, t0:t0 + 512].rearrange("(c p) t -> p c t", p=128), xo[b][:],
                          reads=[("xo", b, c) for c in range(8)])
                run_tiles(nt, ld, body)
                P.barrier()

        def phase_out(si, S):
            with ExitStack() as st:
                xi = [sb(st, "xi%d" % i, (128, 8, 512), F32) for i in range(2)]
                yo = [sb(st, "yo%d" % i, (128, 4, D), F32) for i in range(2)]
                def ld(it):
                    P.dma("sp", xi[it % 2][:], xT[:, it * 512:it * 512 + 512].rearrange("(c p) t -> p c t", p=128),
                          writes=[("xi", it % 2)])

                def body(it):
                    b = it % 2
                    t0 = it * 512
                    for j in range(4):
                        for half in range(2):
                            bk = bank()
                            for cc in range(4):
                                c = half * 4 + cc
                                P.op("pe", lambda e, bk=bk, j=j, c=c, cc=cc, b=b: e.transpose(
                                    PS[bk][:, cc * 128:(cc + 1) * 128], xi[b][:, c, j * 128:(j + 1) * 128], IDF),
                                    reads=[("xi", b), "cD"], writes=[("ps", bk)], sig=(cc == 3))
                            if half:
                                P.op("act", lambda e, bk=bk, j=j, b=b: e.copy(yo[b][:, j, 512:1024], PS[bk][:]),
                                     reads=[("ps", bk)], writes=[("yo", b, j, 1)])
                            else:
                                P.op("dve", lambda e, bk=bk, j=j, b=b: e.tensor_copy(yo[b][:, j, 0:512], PS[bk][:]),
                                     reads=[("ps", bk)], writes=[("yo", b, j, 0)])
                    P.dma("sp", ys[si][t0:t0 + 512, :].rearrange("(j p) f -> p j f", p=128), yo[b][:],
                          reads=[("yo", b, j, h) for j in range(4) for h in range(2)])
                run_tiles(S // 512, ld, body)
                P.barrier()

        def phase_norm(l, S, goff):
            with ExitStack() as st:
                xi = [sb(st, "nxi%d" % i, (128, 8, 512), F32) for i in range(2)]
                sq = sb(st, "nsq", (128, 8, 512), BF16)
                rs = sb(st, "nrs", (128, 512), F32)
                ho = [sb(st, "nho%d" % i, (128, 8, 512), BF16) for i in range(2)]
                def ld(it):
                    P.dma("sp", xi[it % 2][:], xT[:, it * 512:it * 512 + 512].rearrange("(c p) t -> p c t", p=128),
                          writes=[("xi", it % 2)])

                def body(it):
                    b = it % 2
                    t0 = it * 512
                    P.op("act", lambda e, b=b: e.activation(out=sq[:], in_=xi[b][:], func=AF.Square),
                         reads=[("xi", b)], writes=["sq"])
                    bk = bank()
                    for c in range(8):
                        P.op("pe", lambda e, bk=bk, c=c: e.matmul(PS[bk][:], ONEB, sq[:, c, :], start=(c == 0), stop=(c == 7)),
                             reads=["sq", "cDb"], writes=[("ps", bk)], sig=(c == 7))
                    P.op("act", lambda e, bk=bk: e.activation(out=rs[:], in_=PS[bk][:], func=AF.Ln, scale=1.0 / D, bias=EPSC),
                         reads=[("ps", bk)], writes=["rs"])
                    P.op("act", lambda e: e.activation(out=rs[:], in_=rs[:], func=AF.Exp, scale=-0.5), reads=["rs"], writes=["rs"])
                    for c in range(8):
                        P.op("dve", lambda e, c=c, b=b: e.scalar_tensor_tensor(
                            ho[b][:, c, :], xi[b][:, c, :], cA[:, l, goff + c:goff + c + 1], rs[:], ALU.mult, ALU.mult),
                            reads=[("xi", b), "rs", "cA"], writes=[("ho", b, c)])
                    P.dma("sp", hT[:, PAD + t0:PAD + t0 + 512].rearrange("(c p) t -> p c t", p=128), ho[b][:],
                          reads=[("ho", b, c) for c in range(8)])
                run_tiles(S // 512, ld, body)
                P.barrier()

        def phase_a1(l, S):
            with ExitStack() as st:
                W, wk = load_w(st, "wA", w_in[l], 128, 8, 2304, C_AQ)
                hi = [sb(st, "ahi%d" % i, (128, 8, 512), BF16) for i in range(2)]
                sq = [sb(st, "asq%d" % i, (128, 512), BF16) for i in range(2)]
                rs = [sb(st, "ars%d" % i, (128, 512), F32) for i in range(2)]
                qo = [sb(st, "aqo%d" % i, (128, 12, 512), BF16) for i in range(2)]
                vo = [sb(st, "avo%d" % i, (128, 4, 768), BF16) for i in range(2)]

                def ld(it):
                    P.dma("sp", hi[it % 2][:], hT[:, PAD + it * 512:PAD + it * 512 + 512].rearrange("(c p) t -> p c t", p=128),
                          writes=[("hi", it % 2)])

                def group(b, oc):
                    bk = bank()
                    for k in range(8):
                        P.op("pe", lambda e, bk=bk, k=k, oc=oc, b=b: e.matmul(
                            PS[bk][:], W[:, k, oc * 128:(oc + 1) * 128], hi[b][:, k, :], start=(k == 0), stop=(k == 7)),
                            reads=[("hi", b)] + wk, writes=[("ps", bk)], sig=(k == 7))
                    s = oc % 2
                    P.op("act", lambda e, bk=bk, s=s: e.activation(out=sq[s][:], in_=PS[bk][:], func=AF.Square),
                         reads=[("ps", bk)], writes=[("sq", s)])
                    return bk

                def norm(b, oc, bk):
                    s = oc % 2
                    bk2 = bank()
                    P.op("pe", lambda e, bk2=bk2, s=s: e.matmul(PS[bk2][:], BDB, sq[s][:], start=True, stop=True),
                         reads=[("sq", s), "cDb"], writes=[("ps", bk2)])
                    P.op("act", lambda e, bk2=bk2, s=s: e.activation(out=rs[s][:], in_=PS[bk2][:], func=AF.Ln, scale=1.0 / 64, bias=EPSC),
                         reads=[("ps", bk2)], writes=[("rs", s)])
                    P.op("act", lambda e, s=s: e.activation(out=rs[s][:], in_=rs[s][:], func=AF.Exp, scale=-0.5), reads=[("rs", s)], writes=[("rs", s)])
                    gcol = 24 + (0 if oc < 6 else 1)
                    P.op("dve", lambda e, bk=bk, s=s, oc=oc, b=b, gcol=gcol: e.scalar_tensor_tensor(
                        qo[b][:, oc, :], PS[bk][:], cA[:, l, gcol:gcol + 1], rs[s][:], ALU.mult, ALU.mult),
                        reads=[("ps", bk), ("rs", s), "cA"], writes=[("qo", b, oc)])

                def body(it):
                    b = it % 2
                    t0 = it * 512
                    prev = None
                    for oc in range(12):
                        bk = group(b, oc)
                        if prev is not None:
                            norm(b, prev[0], prev[1])
                        prev = (oc, bk)
                    vgroups = [(j, n0, nn) for j in range(4) for (n0, nn) in ((0, 512), (512, 256))]
                    for gi, (j, n0, nn) in enumerate(vgroups):
                        bk = bank()
                        for k in range(8):
                            P.op("pe", lambda e, bk=bk, k=k, j=j, n0=n0, nn=nn, b=b: e.matmul(
                                PS[bk][:, 0:nn], hi[b][:, k, j * 128:(j + 1) * 128], W[:, k, 1536 + n0:1536 + n0 + nn],
                                start=(k == 0), stop=(k == 7)),
                                reads=[("hi", b)] + wk, writes=[("ps", bk)], sig=(k == 7))
                        P.op("act", lambda e, bk=bk, j=j, n0=n0, nn=nn, b=b: e.copy(vo[b][:, j, n0:n0 + nn], PS[bk][:, 0:nn]),
                             reads=[("ps", bk)], writes=[("vo", b, j, n0)])
                        if gi == 0:
                            norm(b, prev[0], prev[1])
                            P.dma("sp", qaT[:, t0:t0 + 512].rearrange("(c p) t -> p c t", p=128), qo[b][:, 0:6, :],
                                  reads=[("qo", b, oc) for oc in range(6)])
                            P.dma("sp", kaT[:, t0:t0 + 512].rearrange("(c p) t -> p c t", p=128), qo[b][:, 6:12, :],
                                  reads=[("qo", b, oc) for oc in range(6, 12)])
                    P.dma("sp", va[t0:t0 + 512, :].rearrange("(j p) f -> p j f", p=128), vo[b][:],
                          reads=[("vo", b, j, n0) for j in range(4) for n0 in (0, 512)])
                run_tiles(S // 512, ld, body)
                P.barrier()

        def phase_a2(l, S):
            with ExitStack() as st:
                EBs = [sb(st, "EB%d" % i, (128, 3, 384), F32) for i in range(2)]
                ebv = eb_d.rearrange("p (g q n) -> p g q n", g=3, q=4)
                qs = sb(st, "a2q", (128, 6, 2048), BF16)
                kw = [2048 + 256 * d for d in DILS]
                ks = [sb(st, "a2k%d" % g, (128, 2, kw[g]), BF16) for g in range(3)]
                ntl = [2048 // d // 128 + 2 for d in DILS]
                vs = [sb(st, "a2v%d" % g, (128, ntl[g], DILS[g], 256), BF16) for g in range(3)]
                OD = sb(st, "a2od", (64, 2, 3, 2048), F32)
                DS = sb(st, "a2ds", (64, 2048), F32)
                AO = sb(st, "a2ao", (64, 2, 2048), BF16)
                Ee = [sb(st, "a2e%d" % i, (128, 384), F32) for i in range(3)]
                Pp = [sb(st, "a2p%d" % i, (128, 384), BF16) for i in range(3)]
                cnt = {"un": 0, "ao": 0, "eb": 0}

                def stage_a(U):
                    bk = bank()
                    U["bk"] = bk
                    for jj, ksl in enumerate(U["ksl"]):
                        P.op("pe", lambda e, bk=bk, jj=jj, ksl=ksl, qsl=U["qsl"]: e.matmul(
                            PS[bk][:, jj * 128:(jj + 1) * 128], ksl, qsl, start=True, stop=True),
                            reads=["qs", ("ks", U["g"])], writes=[("ps", bk)], sig=(jj == U["nb"] - 1))

                def stage_b1(U):
                    u, nb_, bk = U["u"], U["nb"], U["bk"]
                    P.op("act", lambda e: e.activation(
                        out=Ee[u][:, 0:128 * nb_], in_=PS[bk][:, 0:128 * nb_], func=AF.Exp, scale=0.125),
                        reads=[("ps", bk)], writes=[("Ee", u)])

                def stage_b2(U):
                    u, nb_, bk = U["u"], U["nb"], U["bk"]
                    eng = "dve"
                    EB, g, jlo = U["EB"], U["g"], U["jlo"]
                    P.op(eng, lambda e: e.tensor_tensor(
                        Pp[u][:, 0:128 * nb_], Ee[u][:, 0:128 * nb_], EB[:, g, jlo * 128:(jlo + nb_) * 128], ALU.mult),
                        reads=[("Ee", u), ("EB", U["ebi"])], writes=[("Pp", u)])

                def stage_c1(U):
                    u, nb_, g = U["u"], U["nb"], U["g"]
                    bk2 = bank()
                    U["bk2"] = bk2
                    for jj, (vsl, vkey) in enumerate(U["vsl"]):
                        P.op("pe", lambda e, jj=jj, vsl=vsl: e.matmul(
                            PS[bk2][0:64, 0:128], vsl, Pp[u][:, jj * 128:(jj + 1) * 128],
                            start=(jj == 0), stop=(jj == nb_ - 1)),
                            reads=[("Pp", u), vkey], writes=[("ps", bk2)], sig=False)
                    for jj in range(nb_):
                        P.op("pe", lambda e, jj=jj: e.matmul(
                            PS[bk2][0:64, 128:256], ONEB[:, 0:64], Pp[u][:, jj * 128:(jj + 1) * 128],
                            start=(jj == 0), stop=(jj == nb_ - 1)),
                            reads=[("Pp", u), "cDb"], writes=[("ps", bk2)], sig=(jj == nb_ - 1))

                def stage_c2(U):
                    osl, bk2, g = U["osl"], U["bk2"], U["g"]
                    P.op("act", lambda e: e.copy(osl, PS[bk2][0:64, 0:256].rearrange("p (a b) -> p a b", a=2)),
                         reads=[("ps", bk2)], writes=[("OD", g)])

                for sbi in range(S // 2048):
                    T0 = sbi * 2048
                    P.dma("sp", qs[:], qaT[:, T0:T0 + 2048].rearrange("(c p) t -> p c t", p=128), writes=["qs"])
                    K0s, mt0s = [], []
                    for g in range(3):
                        d = DILS[g]
                        K0 = max(0, T0 - 128 * d)
                        K1 = min(S, T0 + 2048 + 128 * d)
                        K0s.append(K0)
                        P.dma("sp", ks[g][:, :, 0:K1 - K0],
                              kaT[256 * g:256 * g + 256, K0:K1].rearrange("(c p) t -> p c t", p=128), writes=[("ks", g)])
                        m_lo = max(0, T0 // d // 128 - 1)
                        m_hi = min(S // d // 128, (T0 + 2048) // d // 128 + 1)
                        mt0s.append(m_lo)
                        for mm in range(m_lo, m_hi):
                            P.dma("sp", vs[g][:, mm - m_lo, :, :],
                                  va[mm * 128 * d:(mm + 1) * 128 * d, 256 * g:256 * g + 256].rearrange("(i r) f -> i r f", r=d),
                                  writes=[("vs", g, mm - m_lo)])
                    for hh in range(4):
                        ebi = cnt["eb"] % 2
                        cnt["eb"] += 1
                        EB = EBs[ebi]
                        P.dma("sp", EB[:], ebv[:, :, hh, :], writes=[("EB", ebi)])
                        units = []
                        for g in range(3):
                            if os.environ.get("A2G") and str(g) not in os.environ["A2G"]:
                                continue
                            d = DILS[g]
                            h = 4 * g + hh
                            c = h // 2
                            pb = 64 * (h % 2)
                            Lt = S // d // 128
                            for r in range(d):
                                for j in range(2048 // d // 128):
                                    m = T0 // d // 128 + j
                                    tiles = [mm for mm in (m - 1, m, m + 1) if 0 <= mm < Lt]
                                    U = {"g": g, "nb": len(tiles), "jlo": tiles[0] - (m - 1), "EB": EB, "ebi": ebi,
                                         "u": cnt["un"] % 3, "n": cnt["un"]}
                                    cnt["un"] += 1
                                    U["qsl"] = qs[pb:pb + 64, c, ssl(r + 128 * j * d, 128, d)]
                                    U["ksl"] = [ks[g][pb:pb + 64, c - 2 * g, ssl(mm * 128 * d + r - K0s[g], 128, d)] for mm in tiles]
                                    U["vsl"] = [(vs[g][:, mm - mt0s[g], r, 64 * hh:64 * hh + 64], ("vs", g, mm - mt0s[g])) for mm in tiles]
                                    U["osl"] = OD[:, :, g, ssl(r + 128 * j * d, 128, d)]
                                    units.append(U)
                        NU = len(units)
                        stage_a(units[0])
                        if NU > 1:
                            stage_a(units[1])
                        stage_b1(units[0])
                        stage_b2(units[0])
                        for i, U in enumerate(units):
                            if i + 2 < NU:
                                stage_a(units[i + 2])
                            if i + 1 < NU:
                                stage_b1(units[i + 1])
                            stage_c1(U)
                            if i + 1 < NU:
                                stage_b2(units[i + 1])
                            stage_c2(U)
                        P.op("dve", lambda e: e.tensor_tensor(DS[:], OD[:, 1, 0, :], OD[:, 1, 1, :], ALU.add),
                             reads=[("OD", 0), ("OD", 1)], writes=["DS"])
                        P.op("dve", lambda e: e.tensor_tensor(DS[:], DS[:], OD[:, 1, 2, :], ALU.add),
                             reads=["DS", ("OD", 2)], writes=["DS"])
                        P.op("act", lambda e: e.activation(out=DS[:], in_=DS[:], func=AF.Ln), reads=["DS"], writes=["DS"])
                        P.op("act", lambda e: e.activation(out=DS[:], in_=DS[:], func=AF.Exp, scale=-1.0), reads=["DS"], writes=["DS"])
                        for g in range(3):
                            ai = cnt["ao"] % 2
                            cnt["ao"] += 1
                            P.op("dve", lambda e, g=g, ai=ai: e.tensor_tensor(AO[:, ai, :], OD[:, 0, g, :], DS[:], ALU.mult),
                                 reads=["DS", ("OD", g)], writes=[("AO", ai)])
                            P.dma("sp", attT[64 * (4 * g + hh):64 * (4 * g + hh) + 64, T0:T0 + 2048], AO[:, ai, :],
                                  reads=[("AO", ai)])
                P.barrier()

        def phase_m1a(l, S):
            with ExitStack() as st:
                W, wk = load_w(st, "wMa", w_in[l], 128, 8, 1536, C_MQ)
                hi = [sb(st, "mhi%d" % i, (128, 8, 512), BF16) for i in range(2)]
                tmp = [sb(st, "mtmp%d" % i, (128, 512), F32) for i in range(4)]
                qo = [sb(st, "mqo%d" % i, (128, 12, 512), BF16) for i in range(2)]
                tl = [(t, min(510, S - t)) for t in range(0, S, 510)]

                def ld(it):
                    t0, ntok = tl[it]
                    P.dma("sp", hi[it % 2][:, :, 0:ntok + 2],
                          hT[:, PAD + t0 - 1:PAD + t0 + 1 + ntok].rearrange("(c p) t -> p c t", p=128), writes=[("hi", it % 2)])

                def body(it):
                    t0, ntok = tl[it]
                    ncol = ntok + 2
                    b = it % 2
                    pend = None
                    for oc in range(12):
                        bk = bank()
                        for k in range(8):
                            P.op("pe", lambda e, bk=bk, k=k, oc=oc: e.matmul(
                                PS[bk][:, 0:ncol], W[:, k, oc * 128:(oc + 1) * 128], hi[b][:, k, 0:ncol],
                                start=(k == 0), stop=(k == 7)),
                                reads=[("hi", b)] + wk, writes=[("ps", bk)], sig=(k == 7))
                        u = oc % 4
                        w0, w1, w2 = (cM[:, l, jj * 12 + oc:jj * 12 + oc + 1] for jj in range(3))
                        bb = cM[:, l, 36 + oc:37 + oc]
                        P.op("act", lambda e, bk=bk, u=u, w1=w1, bb=bb: e.activation(
                            out=tmp[u][:, 0:ntok], in_=PS[bk][:, 1:1 + ntok], func=AF.Identity, scale=w1, bias=bb),
                            reads=[("ps", bk), "cM"], writes=[("tmp", u)])
                        if pend is not None:
                            pend()
                        P.op("dve", lambda e, bk=bk, u=u, w0=w0: e.scalar_tensor_tensor(
                            tmp[u][:, 0:ntok], PS[bk][:, 0:ntok], w0, tmp[u][:, 0:ntok], ALU.mult, ALU.add),
                            reads=[("ps", bk), ("tmp", u), "cM"], writes=[("tmp", u)])
                        P.op("dve", lambda e, bk=bk, u=u, w2=w2: e.scalar_tensor_tensor(
                            tmp[u][:, 0:ntok], PS[bk][:, 2:2 + ntok], w2, tmp[u][:, 0:ntok], ALU.mult, ALU.add),
                            reads=[("ps", bk), ("tmp", u), "cM"], writes=[("tmp", u)])

                        def pend(u=u, oc=oc):
                            P.op("act", lambda e: e.activation(out=qo[b][:, oc, 0:ntok], in_=tmp[u][:, 0:ntok], func=AF.Silu),
                                 reads=[("tmp", u)], writes=[("qo", b, oc)])
                    pend()
                    P.dma("sp", mqT[:, t0:t0 + ntok].rearrange("(c p) t -> p c t", p=128), qo[b][:, 0:6, 0:ntok],
                          reads=[("qo", b, oc) for oc in range(6)])
                    P.dma("sp", mkT[:, t0:t0 + ntok].rearrange("(c p) t -> p c t", p=128), qo[b][:, 6:12, 0:ntok],
                          reads=[("qo", b, oc) for oc in range(6, 12)])
                run_tiles(len(tl), ld, body)
                P.barrier()

        def phase_m1b(l, S):
            with ExitStack() as st:
                W, wk = load_w(st, "wMb", w_in[l], 128, 8, 1552, C_MV)
                cCl = sb(st, "cCl", (128, 16), F32)
                P.dma("sp", cCl[:], cC_d[:, l * NCC:l * NCC + 16], writes=["cCl"])
                hi = [sb(st, "bhi%d" % i, (128, 8, 512), BF16) for i in range(2)]
                vo = [sb(st, "bvo%d" % i, (128, 4, 768), BF16) for i in range(2)]
                oo = [sb(st, "boo%d" % i, (128, 4, 768), F32) for i in range(2)]
                go = [sb(st, "bgo%d" % i, (128, 4, 16), F32) for i in range(2)]
                gt = sb(st, "bgt", (128, 4, 8), F32)
                def ld(it):
                    P.dma("sp", hi[it % 2][:], hT[:, PAD + it * 512:PAD + it * 512 + 512].rearrange("(c p) t -> p c t", p=128),
                          writes=[("hi", it % 2)])

                def body(it):
                    b = it % 2
                    t0 = it * 512
                    bkg = bank()
                    for j in range(4):
                        for k in range(8):
                            P.op("pe", lambda e, k=k, j=j, b=b, bkg=bkg: e.matmul(
                                PS[bkg][:, j * 16:(j + 1) * 16], hi[b][:, k, j * 128:(j + 1) * 128], W[:, k, 1536:1552],
                                start=(k == 0), stop=(k == 7)),
                                reads=[("hi", b)] + wk, writes=[("ps", bkg)], sig=(k == 7))
                    for j in range(4):
                        P.op("dve", lambda e, j=j, b=b, bkg=bkg: e.tensor_tensor(
                            go[b][:, j, :], PS[bkg][:, j * 16:(j + 1) * 16], cCl[:], ALU.add),
                            reads=[("ps", bkg), "cCl"], writes=[("go", b)])
                    for half in range(2):
                        src = go[b][:, :, 8 * half + 4:8 * half + 8]
                        P.op("act", lambda e, src=src, half=half: e.activation(out=gt[:, :, 4 * half:4 * half + 4], in_=src, func=AF.Exp, scale=-1.0),
                             reads=[("go", b)], writes=["gt"])
                    P.op("act", lambda e: e.activation(out=gt[:], in_=gt[:], func=AF.Ln, bias=ONEC, scale=1.0),
                         reads=["gt", "eps1"], writes=["gt"])
                    for half in range(2):
                        dst = go[b][:, :, 8 * half + 4:8 * half + 8]
                        P.op("dve", lambda e, dst=dst, half=half: e.tensor_scalar(dst, gt[:, :, 4 * half:4 * half + 4], -1.0, None, ALU.mult),
                             reads=["gt"], writes=[("go", b)])
                    P.dma("sp", gts[t0:t0 + 512, :].rearrange("(j p) f -> p j f", p=128), go[b][:], reads=[("go", b)])
                    for j in range(4):
                        for gi, (n0, nn) in enumerate(((0, 512), (512, 256), (768, 512), (1280, 256))):
                            bk = bank()
                            for k in range(8):
                                P.op("pe", lambda e, bk=bk, k=k, j=j, n0=n0, nn=nn, b=b: e.matmul(
                                    PS[bk][:, 0:nn], hi[b][:, k, j * 128:(j + 1) * 128], W[:, k, n0:n0 + nn],
                                    start=(k == 0), stop=(k == 7)),
                                    reads=[("hi", b)] + wk, writes=[("ps", bk)], sig=(k == 7))
                            if gi < 2:
                                if gi == 0:
                                    P.op("dve", lambda e, bk=bk, j=j, n0=n0, nn=nn, b=b: e.tensor_copy(vo[b][:, j, n0:n0 + nn], PS[bk][:, 0:nn]),
                                         reads=[("ps", bk)], writes=[("vo", b, j, gi)])
                                else:
                                    P.op("act", lambda e, bk=bk, j=j, n0=n0, nn=nn, b=b: e.copy(vo[b][:, j, n0:n0 + nn], PS[bk][:, 0:nn]),
                                         reads=[("ps", bk)], writes=[("vo", b, j, gi)])
                            else:
                                P.op("act", lambda e, bk=bk, j=j, n0=n0, nn=nn, b=b: e.activation(
                                    out=oo[b][:, j, n0 - 768:n0 - 768 + nn], in_=PS[bk][:, 0:nn], func=AF.Sigmoid),
                                    reads=[("ps", bk)], writes=[("oo", b, j, gi)])
                    P.dma("sp", mv[t0:t0 + 512, :].rearrange("(j p) f -> p j f", p=128), vo[b][:],
                          reads=[("vo", b, j, gi) for j in range(4) for gi in range(2)])
                    P.dma("sp", mo[t0:t0 + 512, :].rearrange("(j p) f -> p j f", p=128), oo[b][:],
                          reads=[("oo", b, j, gi) for j in range(4) for gi in (2, 3)])
                run_tiles(S // 512, ld, body)
                P.barrier()

        def phase_scan(l, S, fwd):
            with ExitStack() as st:
                gofs = 0 if fwd else 8
                MASK = TRIF if fwd else TRIB
                qg = [sb(st, "sq%d" % i, (96, 8, 512), BF16) for i in range(2)]
                kg = [sb(st, "sk%d" % i, (96, 8, 512), BF16) for i in range(2)]
                vg = [sb(st, "sv%d" % i, (128, 4, 4, 194), BF16) for i in range(2)]
                gg = [sb(st, "sg%d" % i, (128, 4, 16), F32) for i in range(2)]
                for i in range(2):
                    P.op("pool", lambda e, i=i: e.memset(vg[i][:, :, :, 192:194], 1.0), writes=[("vg1", i)])
                Cs = sb(st, "sC", (96, 4, 2, 194), F32)
                Cb = sb(st, "sCb", (96, 4, 2, 194), BF16)
                P.op("dve", lambda e: e.memset(Cs[:], 0.0), writes=[("C", h) for h in range(4)])
                P.op("pool", lambda e: e.memset(Cb[:], 0.0), writes=[("Cb", h) for h in range(4)])
                sm = [sb(st, "ssm%d" % i, (128, 20), F32) for i in range(3)]
                smb = [sb(st, "ssmb%d" % i, (128, 8), F32) for i in range(3)]
                kgs = [sb(st, "skgs%d" % i, (128, 768), BF16) for i in range(2)]
                smt = [sb(st, "ssmt%d" % i, (128, 128), BF16) for i in range(4)]
                t1 = [sb(st, "st1%d" % i, (128, 4), F32) for i in range(2)]
                ho = [sb(st, "sho%d" % i, (128, 4, 768), F32) for i in range(2)]
                if fwd:
                    hbg = [sb(st, "shb%d" % i, (128, 4, 768), F32) for i in range(2)]
                    mog = [sb(st, "smo%d" % i, (128, 4, 768), F32) for i in range(2)]
                    MHG = sb(st, "smhg", (128, 768), F32)
                    P.dma("sp", MHG[:], cC_d[:, l * NCC + 16:l * NCC + 16 + 768], writes=["MHG"])
                    ss = [sb(st, "sss%d" % i, (128, 4), F32) for i in range(2)]
                    hbf = [sb(st, "shbf%d" % i, (128, 768), BF16) for i in range(2)]
                    hmo = [sb(st, "shmo%d" % i, (128, 6, 512), BF16) for i in range(2)]
                ng = S // 512
                order = list(range(ng)) if fwd else list(range(ng - 1, -1, -1))

                def ld(gi_):
                    b = gi_ % 2
                    T0 = order[gi_] * 512
                    P.dma("sp", qg[b][:], mqT[:, T0:T0 + 512].rearrange("(c p) t -> p c t", p=96), writes=[("qg", b)])
                    P.dma("sp", kg[b][:], mkT[:, T0:T0 + 512].rearrange("(c p) t -> p c t", p=96), writes=[("kg", b)])
                    for j in range(4):
                        P.dma("sp", vg[b][:, j, :, 0:192], mv[T0 + j * 128:T0 + (j + 1) * 128, :].rearrange("p (h f) -> p h f", h=4),
                              writes=[("vg", b, j)])
                    P.dma("sp", gg[b][:], gts[T0:T0 + 512, :].rearrange("(j p) f -> p j f", p=128), writes=[("gg", b)])
                    if fwd:
                        P.dma("sp", hbg[b][:], hb[T0:T0 + 512, :].rearrange("(j p) f -> p j f", p=128), writes=[("hbg", b)])
                        P.dma("sp", mog[b][:], mo[T0:T0 + 512, :].rearrange("(j p) f -> p j f", p=128), writes=[("mog", b)])

                def chunk_s1(b, j, cn):
                    cs = slice(j * 128, (j + 1) * 128)
                    u = cn % 3
                    kb = cn % 2
                    smu = sm[u]
                    bkA = bank()
                    lf = gg[b][:, j, gofs + 4:gofs + 8]
                    li = gg[b][:, j, gofs:gofs + 4]
                    P.op("pe", lambda e: e.matmul(PS[bkA][:, 0:4], MASK, lf, start=True, stop=True),
                         reads=[("gg", b), "cD"], writes=[("ps", bkA)], sig=False)
                    P.op("pe", lambda e: e.matmul(PS[bkA][:, 4:8], ONEF, lf, start=True, stop=True),
                         reads=[("gg", b), "cD"], writes=[("ps", bkA)])
                    smb_ = smb[u]
                    P.op("act", lambda e: e.copy(smb_[:], PS[bkA][:, 0:8]), reads=[("ps", bkA)], writes=[("smb", u)])
                    P.op("dve", lambda e: e.tensor_tensor(smu[:, 0:4], li, smb_[:, 0:4], ALU.subtract),
                         reads=[("smb", u), ("gg", b)], writes=[("sm", u, 0)])
                    P.op("act", lambda e: e.activation(out=smu[:, 8:12], in_=smb_[:, 0:4], func=AF.Exp, bias=LNC, scale=1.0),
                         reads=[("smb", u), "eps2"], writes=[("sm", u, 2)])
                    P.op("act", lambda e: e.activation(out=smu[:, 12:16], in_=smb_[:, 4:8], func=AF.Exp),
                         reads=[("smb", u)], writes=[("sm", u, 3)])
                    P.op("act", lambda e: e.activation(out=smu[:, 4:8], in_=smu[:, 0:4], func=AF.Exp),
                         reads=[("sm", u, 0)], writes=[("sm", u, 1)])
                    P.op("dve", lambda e: e.tensor_tensor(smu[:, 16:20], smu[:, 4:8], smu[:, 12:16], ALU.mult),
                         reads=[("sm", u, 1), ("sm", u, 3)], writes=[("sm", u, 4)])
                    for c in range(8):
                        P.op("pe", lambda e, c=c: e.transpose(PSB[:, c * 96:(c + 1) * 96], kg[b][:, c, cs], IDB[0:96, 0:96]),
                             reads=[("kg", b), "cDb"], writes=["psb"], sig=(c == 7))
                    for h in range(4):
                        P.op("act", lambda e, h=h: e.mul(kgs[kb][:, h * 192:(h + 1) * 192], PSB[:, h * 192:(h + 1) * 192], smu[:, 16 + h:17 + h]),
                             reads=["psb", ("sm", u, 4)], writes=[("kgs", kb, h)])

                def chunk_rest(b, j, cn):
                    cs = slice(j * 128, (j + 1) * 128)
                    u = cn % 3
                    kb = cn % 2
                    smu = sm[u]
                    tt = t1[cn % 2]
                    bS = []
                    for h in range(4):
                        bkS = bank()
                        bS.append(bkS)
                        for cc in range(2):
                            P.op("pe", lambda e, bkS=bkS, cc=cc, h=h: e.matmul(
                                PS[bkS][:, 0:128], kg[b][:, 2 * h + cc, cs], qg[b][:, 2 * h + cc, cs], start=(cc == 0), stop=(cc == 1)),
                                reads=[("kg", b), ("qg", b)], writes=[("ps", bkS)], sig=(cc == 1))
                    for h in range(4):
                        P.op("dve", lambda e, h=h: e.scalar_tensor_tensor(
                            smt[h][:], PS[bS[h]][:, 0:128], smu[:, 4 + h:5 + h], MASK, ALU.mult, ALU.mult),
                            reads=[("ps", bS[h]), ("sm", u, 1), "cD"], writes=[("smt", h)])
                    bN = []
                    for h in range(4):
                        bkN = bank()
                        bN.append(bkN)
                        P.op("pe", lambda e, bkN=bkN, h=h: e.matmul(
                            PS[bkN][:, 0:193], smt[h][:], vg[b][:, j, h, 0:193], start=True, stop=False),
                            reads=[("smt", h), ("vg", b, j), ("vg1", b)], writes=[("ps", bkN)], sig=False)
                        for cc in range(2):
                            P.op("pe", lambda e, bkN=bkN, cc=cc, h=h: e.matmul(
                                PS[bkN][:, 0:193], qg[b][:, 2 * h + cc, cs], Cb[:, h, cc, 0:193], start=False, stop=(cc == 1)),
                                reads=[("qg", b), ("Cb", h)], writes=[("ps", bkN)], sig=(cc == 1))
                    for h in range(4):
                        P.op("act", lambda e, h=h: e.activation(
                            out=tt[:, h:h + 1], in_=PS[bN[h]][:, 192:193], func=AF.Abs, scale=smu[:, 8 + h:9 + h]),
                            reads=[("ps", bN[h]), ("sm", u, 2)], writes=[("t1", cn % 2)])
                    P.op("dve", lambda e: e.tensor_scalar_max(tt[:], tt[:], 1.0), reads=[("t1", cn % 2)], writes=[("t1", cn % 2)])
                    P.op("dve", lambda e: e.reciprocal(tt[:], tt[:]), reads=[("t1", cn % 2)], writes=[("t1", cn % 2)])
                    P.op("dve", lambda e: e.tensor_tensor(tt[:], tt[:], smu[:, 8:12], ALU.mult),
                         reads=[("t1", cn % 2), ("sm", u, 2)], writes=[("t1", cn % 2)])
                    for h in range(4):
                        P.op("act", lambda e, h=h: e.mul(ho[b][:, j, h * 192:(h + 1) * 192], PS[bN[h]][:, 0:192], tt[:, h:h + 1]),
                             reads=[("ps", bN[h]), ("t1", cn % 2)], writes=[("ho", b, j, h)])
                    for h in range(4):
                        bkC = bank()
                        for cc in range(2):
                            P.op("pe", lambda e, bkC=bkC, cc=cc, h=h: e.matmul(
                                PS[bkC][0:96, cc * 193:(cc + 1) * 193], kgs[kb][:, h * 192 + cc * 96:h * 192 + (cc + 1) * 96], vg[b][:, j, h, 0:193],
                                start=True, stop=True),
                                reads=[("kgs", kb, h), ("vg", b, j), ("vg1", b)], writes=[("ps", bkC)], sig=(cc == 1))
                        Cv = Cs[:, h, :, 0:193]
                        P.op("dve", lambda e, bkC=bkC, Cv=Cv, h=h: e.scalar_tensor_tensor(
                            Cv, Cv, smu[0:96, 12 + h:13 + h], PS[bkC][0:96, 0:386].rearrange("p (a b) -> p a b", a=2), ALU.mult, ALU.add),
                            reads=[("ps", bkC), ("C", h), ("sm", u, 3)], writes=[("C", h)])
                        P.op("act", lambda e, h=h: e.copy(Cb[:, h, :, :], Cs[:, h, :, :]),
                             reads=[("C", h)], writes=[("Cb", h)])
                    if fwd:
                        hv = ho[b][:, j, :]
                        si_ = cn % 2
                        hk = [("ho", b, j, h) for h in range(4)]
                        P.op("dve", lambda e: e.tensor_tensor(hv, hv, hbg[b][:, j, :], ALU.add),
                             reads=hk + [("hbg", b)], writes=hk)
                        for h in range(4):
                            P.op("act", lambda e, h=h: e.activation(
                                out=hbf[si_][:, h * 192:(h + 1) * 192], in_=hv[:, h * 192:(h + 1) * 192], func=AF.Square, accum_out=ss[si_][:, h:h + 1]),
                                reads=[("ho", b, j, h)], writes=[("ss", si_), ("hbf", si_)])
                        P.op("act", lambda e: e.activation(out=ss[si_][:], in_=ss[si_][:], func=AF.Ln, scale=1.0 / 192, bias=EPSC),
                             reads=[("ss", si_), "eps"], writes=[("ss", si_)])
                        P.op("act", lambda e: e.activation(out=ss[si_][:], in_=ss[si_][:], func=AF.Exp, scale=-0.5), reads=[("ss", si_)], writes=[("ss", si_)])
                        for h in range(4):
                            P.op("dve", lambda e, h=h: e.scalar_tensor_tensor(
                                hv[:, h * 192:(h + 1) * 192], hv[:, h * 192:(h + 1) * 192], ss[si_][:, h:h + 1],
                                MHG[:, h * 192:(h + 1) * 192], ALU.mult, ALU.mult),
                                reads=[("ho", b, j, h), ("ss", si_), "MHG"], writes=[("ho", b, j, h)])
                        P.op("dve", lambda e: e.tensor_tensor(hbf[si_][:], hv, mog[b][:, j, :], ALU.mult),
                             reads=hk + [("mog", b)], writes=[("hbf", si_)])
                        for c in range(6):
                            P.op("pe", lambda e, c=c: e.transpose(PSB2[:, c * 128:(c + 1) * 128], hbf[si_][:, c * 128:(c + 1) * 128], IDB),
                                 reads=[("hbf", si_), "cDb"], writes=["psb2"], sig=(c == 5))
                        P.op("act", lambda e: e.copy(hmo[b][:, :, cs], PSB2[:, 0:768].rearrange("p (a b) -> p a b", a=6)),
                             reads=["psb2"], writes=[("hmo", b, j)])

                def store(gi_):
                    b = gi_ % 2
                    T0 = order[gi_] * 512
                    if fwd:
                        P.dma("sp", hmT[:, T0:T0 + 512].rearrange("(c p) t -> p c t", p=128), hmo[b][:],
                              reads=[("hmo", b, j) for j in range(4)])
                    else:
                        P.dma("sp", hb[T0:T0 + 512, :].rearrange("(j p) f -> p j f", p=128), ho[b][:],
                              reads=[("ho", b, j, h) for j in range(4) for h in range(4)])

                chunks = []
                for gi_ in range(ng):
                    for ji, j in enumerate(range(4) if fwd else range(3, -1, -1)):
                        chunks.append((gi_, gi_ % 2, j, gi_ * 4 + ji, ji))
                ld(0)
                chunk_s1(chunks[0][1], chunks[0][2], chunks[0][3])
                for i, (gi_, b, j, cn, ji) in enumerate(chunks):
                    if ji == 0 and gi_ + 1 < ng:
                        ld(gi_ + 1)
                    if i + 1 < len(chunks):
                        nx = chunks[i + 1]
                        chunk_s1(nx[1], nx[2], nx[3])
                    chunk_rest(b, j, cn)
                    if ji == 3:
                        store(gi_)
                P.barrier()

        def phase_x1(l, S, si):
            with ExitStack() as st:
                Wkv, wkk = load_w(st, "wKV", w_kv[l], 128, 8, 1536, 0)
                Wq, wqk = load_w(st, "wXQ", w_in[l], 128, 8, 768, C_XQ)
                xkT = sb(st, "xkT", (96, 8, 256), BF16)
                xv = sb(st, "xv", (128, 2, 768), BF16)
                with ExitStack() as st2:
                    mt = sb(st2, "xmt", (128, 2, D), F32)
                    msq = sb(st2, "xmsq", (128, 2, D), BF16)
                    mss = sb(st2, "xmss", (128, 2), F32)
                    memT = sb(st2, "xmemT", (128, 8, 256), BF16)
                    ksq = sb(st2, "xksq", (96, 2, 256), BF16)
                    krs = sb(st2, "xkrs", (96, 256), F32)
                    P.dma("sp", mt[:], mems[si].rearrange("(j p) f -> p j f", p=128), writes=["mt"])
                    for j in range(2):
                        P.op("act", lambda e, j=j: e.activation(out=msq[:, j, :], in_=mt[:, j, :], func=AF.Square, accum_out=mss[:, j:j + 1]),
                             reads=["mt"], writes=["mss", ("msq", j)])
                    P.op("act", lambda e: e.activation(out=mss[:], in_=mss[:], func=AF.Sqrt, scale=1.0 / D, bias=EPSC),
                         reads=["mss", "eps"], writes=["mss"])
                    P.op("dve", lambda e: e.reciprocal(mss[:], mss[:]), reads=["mss"], writes=["mss"])
                    for j in range(2):
                        P.op("dve", lambda e, j=j: e.tensor_scalar(mt[:, j, :], mt[:, j, :], mss[:, j:j + 1], None, ALU.mult),
                             reads=["mt", "mss"], writes=[("mtn", j)])
                    for c in range(8):
                        bk = bank()
                        for j in range(2):
                            P.op("pe", lambda e, bk=bk, j=j, c=c: e.transpose(PS[bk][:, j * 128:(j + 1) * 128], mt[:, j, c * 128:(c + 1) * 128], IDF),
                                 reads=[("mtn", j), "cD"], writes=[("ps", bk)], sig=(j == 1))
                        P.op("dve", lambda e, bk=bk, c=c: e.tensor_scalar(memT[:, c, :], PS[bk][:, 0:256], cA[:, l, 16 + c:17 + c], None, ALU.mult),
                             reads=[("ps", bk), "cA"], writes=[("memT", c)])
                    mTk = [("memT", c) for c in range(8)]
                    for h in range(4):
                        bq = []
                        for cc in range(2):
                            bk = bank()
                            bq.append(bk)
                            for k in range(8):
                                P.op("pe", lambda e, bk=bk, k=k, h=h, cc=cc: e.matmul(
                                    PS[bk][0:96, 0:256], Wkv[:, k, (2 * h + cc) * 96:(2 * h + cc + 1) * 96], memT[:, k, :],
                                    start=(k == 0), stop=(k == 7)), reads=mTk + wkk, writes=[("ps", bk)], sig=(k == 7))
                            P.op("act", lambda e, bk=bk, cc=cc: e.activation(out=ksq[:, cc, :], in_=PS[bk][0:96, 0:256], func=AF.Square),
                                 reads=[("ps", bk)], writes=[("ksq", cc)])
                        bk2 = bank()
                        for cc in range(2):
                            P.op("pe", lambda e, bk2=bk2, cc=cc: e.matmul(PS[bk2][0:96, 0:256], ONEB[0:96, 0:96], ksq[:, cc, :], start=(cc == 0), stop=(cc == 1)),
                                 reads=[("ksq", cc), "cDb"], writes=[("ps", bk2)], sig=(cc == 1))
                        P.op("act", lambda e, bk2=bk2: e.activation(out=krs[:], in_=PS[bk2][0:96, 0:256], func=AF.Sqrt, scale=1.0 / 192, bias=EPSC[0:96, :]),
                             reads=[("ps", bk2), "eps"], writes=["krs"])
                        P.op("dve", lambda e: e.reciprocal(krs[:], krs[:]), reads=["krs"], writes=["krs"])
                        for cc in range(2):
                            P.op("dve", lambda e, cc=cc, h=h, bq=bq: e.scalar_tensor_tensor(
                                xkT[:, 2 * h + cc, :], PS[bq[cc]][0:96, 0:256], cB[0:96, l, 2 + cc:3 + cc], krs[:], ALU.mult, ALU.mult),
                                reads=[("ps", bq[cc]), "krs", "cB"], writes=[("xkT", h)])
                    for mc in range(2):
                        for gi, (n0, nn) in enumerate(((0, 512), (512, 256))):
                            bk = bank()
                            for k in range(8):
                                P.op("pe", lambda e, bk=bk, k=k, mc=mc, n0=n0, nn=nn: e.matmul(
                                    PS[bk][:, 0:nn], memT[:, k, mc * 128:(mc + 1) * 128], Wkv[:, k, 768 + n0:768 + n0 + nn],
                                    start=(k == 0), stop=(k == 7)), reads=mTk + wkk, writes=[("ps", bk)], sig=(k == 7))
                            P.op("act", lambda e, bk=bk, mc=mc, n0=n0, nn=nn: e.copy(xv[:, mc, n0:n0 + nn], PS[bk][:, 0:nn]),
                                 reads=[("ps", bk)], writes=[("xv", mc, gi)])
                    P.barrier()
                hi = [sb(st, "xhi%d" % i, (128, 8, 512), BF16) for i in range(2)]
                qsq = sb(st, "xqsq", (96, 2, 512), BF16)
                qrs = sb(st, "xqrs", (96, 512), F32)
                xq = [sb(st, "xxq%d" % i, (96, 2, 512), BF16) for i in range(2)]
                pT = [sb(st, "xpT%d" % i, (128, 2, 512), BF16) for i in range(2)]
                rD = [sb(st, "xrD%d" % i, (96, 512), F32) for i in range(2)]
                xo = [sb(st, "xxo%d" % i, (96, 8, 512), BF16) for i in range(2)]
                def ld(it):
                    P.dma("sp", hi[it % 2][:], hT[:, PAD + it * 512:PAD + it * 512 + 512].rearrange("(c p) t -> p c t", p=128), writes=[("hi", it % 2)])

                def body(it):
                    b = it % 2
                    t0 = it * 512
                    for h in range(4):
                        r = h % 2
                        bq = []
                        for cc in range(2):
                            bk = bank()
                            bq.append(bk)
                            for k in range(8):
                                P.op("pe", lambda e, bk=bk, k=k, h=h, cc=cc, b=b: e.matmul(
                                    PS[bk][0:96, :], Wq[:, k, (2 * h + cc) * 96:(2 * h + cc + 1) * 96], hi[b][:, k, :],
                                    start=(k == 0), stop=(k == 7)), reads=[("hi", b)] + wqk, writes=[("ps", bk)], sig=(k == 7))
                            P.op("act", lambda e, bk=bk, cc=cc: e.activation(out=qsq[:, cc, :], in_=PS[bk][0:96, :], func=AF.Square),
                                 reads=[("ps", bk)], writes=[("qsq", cc)])
                        bk2 = bank()
                        for cc in range(2):
                            P.op("pe", lambda e, bk2=bk2, cc=cc: e.matmul(PS[bk2][0:96, :], ONEB[0:96, 0:96], qsq[:, cc, :], start=(cc == 0), stop=(cc == 1)),
                                 reads=[("qsq", cc), "cDb"], writes=[("ps", bk2)], sig=(cc == 1))
                        P.op("act", lambda e, bk2=bk2: e.activation(out=qrs[:], in_=PS[bk2][0:96, :], func=AF.Ln, scale=1.0 / 192, bias=EPSC[0:96, :]),
                             reads=[("ps", bk2), "eps"], writes=["qrs"])
                        P.op("act", lambda e: e.activation(out=qrs[:], in_=qrs[:], func=AF.Exp, scale=-0.5), reads=["qrs"], writes=["qrs"])
                        for cc in range(2):
                            P.op("dve", lambda e, cc=cc, r=r, bq=bq: e.scalar_tensor_tensor(
                                xq[r][:, cc, :], PS[bq[cc]][0:96, :], cB[0:96, l, cc:cc + 1], qrs[:], ALU.mult, ALU.mult),
                                reads=[("ps", bq[cc]), "qrs", "cB"], writes=[("xq", r)])
                        for mc in range(2):
                            bk = bank()
                            for cc in range(2):
                                P.op("pe", lambda e, bk=bk, cc=cc, mc=mc, h=h, r=r: e.matmul(
                                    PS[bk][:, :], xkT[:, 2 * h + cc, mc * 128:(mc + 1) * 128], xq[r][:, cc, :], start=(cc == 0), stop=(cc == 1)),
                                    reads=[("xq", r), ("xkT", h)], writes=[("ps", bk)], sig=(cc == 1))
                            P.op("act", lambda e, bk=bk, mc=mc, r=r: e.activation(out=pT[r][:, mc, :], in_=PS[bk][:, :], func=AF.Exp, scale=192.0 ** -0.5),
                                 reads=[("ps", bk)], writes=[("pT", r, mc)])
                        bo = []
                        for cc in range(2):
                            bk = bank()
                            bo.append(bk)
                            for mc in range(2):
                                P.op("pe", lambda e, bk=bk, cc=cc, mc=mc, h=h, r=r: e.matmul(
                                    PS[bk][0:96, :], xv[:, mc, h * 192 + cc * 96:h * 192 + (cc + 1) * 96], pT[r][:, mc, :], start=(mc == 0), stop=(mc == 1)),
                                    reads=[("pT", r, mc), ("xv", mc, 0), ("xv", mc, 1)], writes=[("ps", bk)], sig=(mc == 1))
                        bkD = bank()
                        for mc in range(2):
                            P.op("pe", lambda e, bkD=bkD, mc=mc, r=r: e.matmul(PS[bkD][0:96, :], ONEB[:, 0:96], pT[r][:, mc, :], start=(mc == 0), stop=(mc == 1)),
                                 reads=[("pT", r, mc), "cDb"], writes=[("ps", bkD)], sig=(mc == 1))
                        P.op("act", lambda e, bkD=bkD, r=r: e.activation(out=rD[r][:], in_=PS[bkD][0:96, :], func=AF.Ln), reads=[("ps", bkD)], writes=[("rD", r)])
                        P.op("act", lambda e, r=r: e.activation(out=rD[r][:], in_=rD[r][:], func=AF.Exp, scale=-1.0), reads=[("rD", r)], writes=[("rD", r)])
                        for cc in range(2):
                            P.op("dve", lambda e, cc=cc, h=h, b=b, r=r, bo=bo: e.tensor_tensor(xo[b][:, 2 * h + cc, :], PS[bo[cc]][0:96, :], rD[r][:], ALU.mult),
                                 reads=[("ps", bo[cc]), ("rD", r)], writes=[("xo", b, h, cc)])
                    P.dma("sp", xoT[:, t0:t0 + 512].rearrange("(c p) t -> p c t", p=96), xo[b][:],
                          reads=[("xo", b, h, cc) for h in range(4) for cc in range(2)])
                run_tiles(S // 512, ld, body)
                P.barrier()

        def phase_g1(l, S):
            T = 256
            with ExitStack() as st:
                Wg, wgk = load_w(st, "wG", w_in[l], 128, 8, 3072, C_G)
                Wa, wak = load_w(st, "wBa", w_br[l][0:768, :], 128, 6, D)
                Wm, wmk = load_w(st, "wBm", w_br[l][768:1536, :], 128, 6, D)
                Wx, wxk = load_w(st, "wBx", w_br[l][1536:2304, :], 96, 8, D)
                Wo, wok = load_w(st, "wO", w_out[l], 128, 8, D)
                hi = [sb(st, "ghi%d" % i, (128, 8, T), BF16) for i in range(2)]
                ai = [sb(st, "gai%d" % i, (128, 6, T), BF16) for i in range(2)]
                mi = [sb(st, "gmi%d" % i, (128, 6, T), BF16) for i in range(2)]
                xi_ = [sb(st, "gxi%d" % i, (96, 8, T), BF16) for i in range(2)]
                xt = [sb(st, "gxt%d" % i, (128, 8, T), F32) for i in range(3)]
                pend = [None]
                mg = sb(st, "gmg", (128, 8, T), BF16)
                s01 = [sb(st, "gs01%d" % i, (128, 2 * T), F32) for i in range(2)]
                s2 = [sb(st, "gs2%d" % i, (128, T), F32) for i in range(2)]
                mm = [sb(st, "gmm%d" % i, (128, 3, T), F32) for i in range(2)]
                nb = norm_bufs(st, T)
                def ld(it):
                    b = it % 2
                    t0 = it * T
                    P.dma("sp", hi[b][:], hT[:, PAD + t0:PAD + t0 + T].rearrange("(c p) t -> p c t", p=128), writes=[("hi", b)])
                    P.dma("sp", ai[b][:], attT[:, t0:t0 + T].rearrange("(c p) t -> p c t", p=128), writes=[("ai", b)])
                    P.dma("sp", mi[b][:], hmT[:, t0:t0 + T].rearrange("(c p) t -> p c t", p=128), writes=[("mi", b)])
                    P.dma("sp", xi_[b][:], xoT[:, t0:t0 + T].rearrange("(c p) t -> p c t", p=96), writes=[("xi", b)])
                    P.dma("sp", xt[it % 3][:], xT[:, t0:t0 + T].rearrange("(c p) t -> p c t", p=128), writes=[("xt", it % 3)])

                def body(it):
                    b = it % 2
                    x3 = it % 3
                    t0 = it * T
                    for oc in range(8):
                        u = oc % 2
                        if oc == 1 and pend[0] is not None:
                            pend[0]()
                            pend[0] = None
                        ocs = slice(oc * 128, (oc + 1) * 128)
                        bA, bB, bC, bD = bank(), bank(), bank(), bank()
                        slots = [(bA, 0), (bA, T), (bB, 0)]
                        for br in range(3):
                            bk, c0 = slots[br]
                            for k in range(8):
                                P.op("pe", lambda e, bk=bk, c0=c0, k=k, br=br, b=b, ocs=ocs: e.matmul(
                                    PS[bk][:, c0:c0 + T], Wg[:, k, br * D + ocs.start:br * D + ocs.stop], hi[b][:, k, :],
                                    start=(k == 0), stop=(k == 7)), reads=[("hi", b)] + wgk, writes=[("ps", bk)], sig=(k == 7))
                        for k in range(6):
                            P.op("pe", lambda e, k=k, b=b, ocs=ocs, bC=bC: e.matmul(PS[bC][:, 0:T], Wa[:, k, ocs], ai[b][:, k, :], start=(k == 0), stop=(k == 5)),
                                 reads=[("ai", b)] + wak, writes=[("ps", bC)], sig=(k == 5))
                        for k in range(6):
                            P.op("pe", lambda e, k=k, b=b, ocs=ocs, bC=bC: e.matmul(PS[bC][:, T:2 * T], Wm[:, k, ocs], mi[b][:, k, :], start=(k == 0), stop=(k == 5)),
                                 reads=[("mi", b)] + wmk, writes=[("ps", bC)], sig=(k == 5))
                        for k in range(8):
                            P.op("pe", lambda e, k=k, b=b, ocs=ocs, bD=bD: e.matmul(PS[bD][0:128, 0:T], Wx[:, k, ocs], xi_[b][:, k, :], start=(k == 0), stop=(k == 7)),
                                 reads=[("xi", b)] + wxk, writes=[("ps", bD)], sig=(k == 7))
                        P.op("act", lambda e, u=u, bA=bA: e.activation(out=s01[u][:], in_=PS[bA][:, :], func=AF.Sigmoid),
                             reads=[("ps", bA)], writes=[("s01", u)])
                        P.op("act", lambda e, u=u, bB=bB: e.activation(out=s2[u][:], in_=PS[bB][:, 0:T], func=AF.Sigmoid),
                             reads=[("ps", bB)], writes=[("s2", u)])
                        P.op("dve", lambda e, u=u, bC=bC: e.tensor_tensor(mm[u][:, 0, :], PS[bC][:, 0:T], s01[u][:, 0:T], ALU.mult),
                             reads=[("ps", bC), ("s01", u)], writes=[("mm", u, 0)])
                        P.op("dve", lambda e, u=u, bC=bC: e.tensor_tensor(mm[u][:, 1, :], PS[bC][:, T:2 * T], s01[u][:, T:2 * T], ALU.mult),
                             reads=[("ps", bC), ("s01", u)], writes=[("mm", u, 1)])
                        P.op("dve", lambda e, u=u, bD=bD: e.tensor_tensor(mm[u][:, 2, :], PS[bD][:, 0:T], s2[u][:], ALU.mult),
                             reads=[("ps", bD), ("s2", u)], writes=[("mm", u, 2)])
                        P.op("dve", lambda e, u=u: e.tensor_tensor(mm[u][:, 0, :], mm[u][:, 0, :], mm[u][:, 1, :], ALU.add),
                             reads=[("mm", u, 0), ("mm", u, 1)], writes=[("mm", u, 0)])
                        P.op("dve", lambda e, u=u, oc=oc: e.tensor_tensor(mg[:, oc, :], mm[u][:, 0, :], mm[u][:, 2, :], ALU.add),
                             reads=[("mm", u, 0), ("mm", u, 2)], writes=[("mg", oc)])
                    for oc in range(8):
                        bk = bank()
                        for k in range(8):
                            P.op("pe", lambda e, bk=bk, k=k, oc=oc: e.matmul(PS[bk][:, 0:T], Wo[:, k, oc * 128:(oc + 1) * 128], mg[:, k, :], start=(k == 0), stop=(k == 7)),
                                 reads=[("mg", kk) for kk in range(8)] + wok, writes=[("ps", bk)], sig=(k == 7))
                        P.op("dve", lambda e, bk=bk, oc=oc: e.tensor_tensor(xt[x3][:, oc, :], PS[bk][:, 0:T], xt[x3][:, oc, :], ALU.add),
                             reads=[("ps", bk), ("xt", x3)], writes=[("xn", x3, oc)])
                    xk = [("xn", x3, oc) for oc in range(8)] + [("xt", x3)]
                    P.dma("sp", xT[:, t0:t0 + T].rearrange("(c p) t -> p c t", p=128), xt[x3][:], reads=xk)
                    pend[0] = lambda: fused_norm(nb, xt[x3][:], xk, T, l, 8, t0, b)
                run_tiles(S // T, ld, body)
                if pend[0] is not None:
                    pend[0]()
                P.barrier()

        def phase_f1(l, S):
            with ExitStack() as st:
                W, wk = load_w(st, "wUp", w_up[l], 128, 8, 2 * FFN, 0)
                hi = [sb(st, "fhi%d" % i, (128, 8, 512), BF16) for i in range(2)]
                ta = [sb(st, "fta%d" % i, (128, 512), F32) for i in range(3)]
                tv = [sb(st, "ftv%d" % i, (128, 512), F32) for i in range(3)]
                go = [sb(st, "fgo%d" % i, (128, 22, 512), BF16) for i in range(2)]
                tl = [(t, min(510, S - t)) for t in range(0, S, 510)]

                def ld(it):
                    t0, ntok = tl[it]
                    P.dma("sp", hi[it % 2][:, :, 0:ntok + 2],
                          hT[:, PAD + t0 - 1:PAD + t0 + 1 + ntok].rearrange("(c p) t -> p c t", p=128), writes=[("hi", it % 2)])

                def body(it):
                    t0, ntok = tl[it]
                    ncol = ntok + 2
                    b = it % 2
                    pend = None
                    for fc in range(22):
                        u = fc % 3
                        for part, (tt_, key) in enumerate(((ta[u], "ta"), (tv[u], "tv"))):
                            ch = part * 22 + fc
                            bk = bank()
                            for k in range(8):
                                P.op("pe", lambda e, bk=bk, k=k, ch=ch: e.matmul(
                                    PS[bk][:, 0:ncol], W[:, k, ch * 128:(ch + 1) * 128], hi[b][:, k, 0:ncol], start=(k == 0), stop=(k == 7)),
                                    reads=[("hi", b)] + wk, writes=[("ps", bk)], sig=(k == 7))
                            w0, w1, w2 = (cA[:, l, 26 + jj * 44 + ch:26 + jj * 44 + ch + 1] for jj in range(3))
                            bb = cA[:, l, 158 + ch:158 + ch + 1]
                            P.op("act", lambda e, bk=bk, tt_=tt_, w1=w1, bb=bb: e.activation(
                                out=tt_[:, 0:ntok], in_=PS[bk][:, 1:1 + ntok], func=AF.Identity, scale=w1, bias=bb),
                                reads=[("ps", bk), "cA"], writes=[(key, u)])
                            if part == 1 and pend is not None:
                                pend()
                            P.op("dve", lambda e, bk=bk, tt_=tt_, w0=w0: e.scalar_tensor_tensor(
                                tt_[:, 0:ntok], PS[bk][:, 0:ntok], w0, tt_[:, 0:ntok], ALU.mult, ALU.add),
                                reads=[("ps", bk), (key, u), "cA"], writes=[(key, u)])
                            P.op("dve", lambda e, bk=bk, tt_=tt_, w2=w2: e.scalar_tensor_tensor(
                                tt_[:, 0:ntok], PS[bk][:, 2:2 + ntok], w2, tt_[:, 0:ntok], ALU.mult, ALU.add),
                                reads=[("ps", bk), (key, u), "cA"], writes=[(key, u)])

                        def pend(u=u, fc=fc):
                            P.op("act", lambda e: e.activation(out=ta[u][:, 0:ntok], in_=ta[u][:, 0:ntok], func=AF.Gelu_apprx_tanh),
                                 reads=[("ta", u)], writes=[("ta", u)])
                            P.op("pool", lambda e: e.tensor_tensor(go[b][:, fc, 0:ntok], ta[u][:, 0:ntok], tv[u][:, 0:ntok], ALU.mult),
                                 reads=[("ta", u), ("tv", u)], writes=[("go", b, fc)])
                    pend()
                    P.dma("sp", gT[:, t0:t0 + ntok].rearrange("(c p) t -> p c t", p=128), go[b][:, :, 0:ntok],
                          reads=[("go", b, fc) for fc in range(22)])
                run_tiles(len(tl), ld, body)
                P.barrier()

        def phase_f2(l, S, fuse_next):
            with ExitStack() as st:
                W, wk = load_w(st, "wDn", w_dn[l], 128, 22, D, 0)
                gi = [sb(st, "dgi%d" % i, (128, 22, 512), BF16) for i in range(2)]
                xt = [sb(st, "dxt%d" % i, (128, 8, 512), F32) for i in range(3)]
                nb = norm_bufs(st, 512) if fuse_next else None
                pend = [None]

                def ld(it):
                    b = it % 2
                    x3 = it % 3
                    t0 = it * 512
                    P.dma("sp", gi[b][:], gT[:, t0:t0 + 512].rearrange("(c p) t -> p c t", p=128), writes=[("gi", b)])
                    P.dma("sp", xt[x3][:], xT[:, t0:t0 + 512].rearrange("(c p) t -> p c t", p=128), writes=[("xt", x3)])

                def body(it):
                    b = it % 2
                    x3 = it % 3
                    t0 = it * 512
                    for oc in range(8):
                        bk = bank()
                        for k in range(22):
                            P.op("pe", lambda e, bk=bk, k=k, oc=oc: e.matmul(PS[bk][:, :], W[:, k, oc * 128:(oc + 1) * 128], gi[b][:, k, :], start=(k == 0), stop=(k == 21)),
                                 reads=[("gi", b)] + wk, writes=[("ps", bk)], sig=(k == 21))
                        P.op("dve", lambda e, bk=bk, oc=oc: e.tensor_tensor(xt[x3][:, oc, :], PS[bk][:, :], xt[x3][:, oc, :], ALU.add),
                             reads=[("ps", bk), ("xt", x3)], writes=[("xn", x3, oc)])
                        if oc == 1 and pend[0] is not None:
                            pend[0]()
                            pend[0] = None
                    xk = [("xn", x3, oc) for oc in range(8)] + [("xt", x3)]
                    P.dma("sp", xT[:, t0:t0 + 512].rearrange("(c p) t -> p c t", p=128), xt[x3][:], reads=xk)
                    if fuse_next:
                        pend[0] = lambda: fused_norm(nb, xt[x3][:], xk, 512, l + 1, 0, t0, b)
                run_tiles(S // 512, ld, body)
                if pend[0] is not None:
                    pend[0]()
                P.barrier()

        phases = []
        for si, S in enumerate(seq_lens):
            phases.append(("in", lambda si=si, S=S: phase_in(si, S)))
            for l in range(depth):
                if l == 0:
                    phases.append(("n1", lambda l=l, S=S: phase_norm(l, S, 0)))
                phases.append(("a1", lambda l=l, S=S: phase_a1(l, S)))
                phases.append(("a2", lambda l=l, S=S: phase_a2(l, S)))
                phases.append(("m1a", lambda l=l, S=S: phase_m1a(l, S)))
                phases.append(("m1b", lambda l=l, S=S: phase_m1b(l, S)))
                phases.append(("m2", lambda l=l, S=S: phase_scan(l, S, False)))
                phases.append(("m3", lambda l=l, S=S: phase_scan(l, S, True)))
                phases.append(("x1", lambda l=l, S=S, si=si: phase_x1(l, S, si)))
                phases.append(("g1", lambda l=l, S=S: phase_g1(l, S)))
                phases.append(("f1", lambda l=l, S=S: phase_f1(l, S)))
                phases.append(("f2", lambda l=l, S=S: phase_f2(l, S, l + 1 < depth)))
            phases.append(("out", lambda si=si, S=S: phase_out(si, S)))
        for name, fn in phases:
            fn()
            if upto is not None and name == upto:
                break
        P.barrier()
    return nc


def kernel(**inputs):
    f = lambda k: np.ascontiguousarray(np.asarray(inputs[k], dtype=np.float32))
    xp, xsm = f("x_prompt"), f("x_sample")
    mp, ms = f("mem_prompt"), f("mem_sample")
    nc = build([xp.shape[1], xsm.shape[1]], depth=NL)
    cA, cB, cC = pack_params(inputs)
    eb, cd = host_consts()
    common = {"w_in": f("w_in"), "w_mem_kv": f("w_mem_kv"),
              "w_branch": f("w_branch").reshape(NL, 2304, D), "w_out": f("w_out"),
              "w_up": f("w_up"), "w_down": f("w_down"),
              "cA": cA, "cB": cB, "cC": cC, "cEB": eb, "cD": cd, "cM": pack_cm(inputs)}
    n = 8
    nsm = xsm.shape[0]
    in_maps = []
    for c in range(n):
        m = dict(common)
        m["x0"] = xp[c]
        m["mem0"] = mp[c]
        m["x1"] = xsm[c % nsm]
        m["mem1"] = ms[c % nsm]
        in_maps.append(m)
    res = run_bass_kernel_spmd(nc, in_maps, core_ids=list(range(n)))
    y_prompt = np.stack([np.asarray(res.results[c]["y0"], dtype=np.float32) for c in range(n)])
    y_sample = np.stack([np.asarray(res.results[c]["y1"], dtype=np.float32) for c in range(nsm)])
    return (y_prompt, y_sample)
```

```python
import bisect
import math
import os
from contextlib import ExitStack

import numpy as np
import concourse.bass as bass
import concourse.mybir as mybir
from concourse.bass_utils import run_bass_kernel_spmd

F32 = mybir.dt.float32
BF16 = mybir.dt.bfloat16
AF = mybir.ActivationFunctionType
ALU = mybir.AluOpType
AX = mybir.AxisListType

D = 1024
NL = 4
NMEM = 256
IN_DIM = 9232
FFN = 2816
EPS = 1e-6
C_AQ, C_AK, C_AV, C_MQ, C_MK, C_MV, C_MO, C_MIF, C_XQ, C_G = 0, 768, 1536, 2304, 3072, 3840, 4608, 5376, 5392, 6160
PAD = 64
DILS = (1, 4, 16)


class _Eng:
    def __init__(self, name, obj, sem):
        self.name, self.obj, self.sem = name, obj, sem
        self.seq = 0
        self.sig_seqs = []
        self.waited = {}


class Prog:
    def __init__(self, nc, es, n_sp=6, n_pool=4):
        self.nc = nc
        self.E = {}
        for name, obj in (("pe", nc.tensor), ("act", nc.scalar), ("dve", nc.vector),
                          ("pool", nc.gpsimd), ("sp", nc.sync)):
            sem = es.enter_context(nc.semaphore("s_" + name))
            self.E[name] = _Eng(name, obj, sem)
        self.dsems = []
        self.dq = {}
        for q, n in (("sp", n_sp), ("pool", n_pool)):
            lst = []
            for i in range(n):
                self.dsems.append(es.enter_context(nc.semaphore("d_%s%d" % (q, i))))
                lst.append([len(self.dsems) - 1, 0])
            self.dq[q] = lst
        self.dq_next = {"sp": 0, "pool": 0}
        self.lastw = {}
        self.readers = {}
        self.n_ops = 0

    def _wait(self, e, tok):
        if tok[0] == "c":
            src = self.E[tok[1]]
            i = bisect.bisect_left(src.sig_seqs, tok[2])
            assert i < len(src.sig_seqs), "dependency on an op with no later signal on %s" % src.name
            cnt, key, sem = i + 1, src.name, src.sem
        else:
            key, sem, cnt = ("d", tok[1]), self.dsems[tok[1]], tok[2]
        if e.waited.get(key, 0) >= cnt:
            return
        e.obj.wait_ge(sem, cnt)
        e.waited[key] = cnt

    def _deps(self, reads, writes):
        deps = []
        for r in reads:
            t = self.lastw.get(r)
            if t is not None:
                deps.append((t, "raw"))
        for w in writes:
            t = self.lastw.get(w)
            if t is not None:
                deps.append((t, "waw"))
            rd = self.readers.get(w)
            if rd:
                for t in rd.values():
                    deps.append((t, "war"))
        return deps

    def _commit(self, tok, reads, writes):
        key = tok[1] if tok[0] == "c" else tok
        for r in reads:
            self.readers.setdefault(r, {})[key] = tok
        for w in writes:
            self.lastw[w] = tok
            self.readers[w] = {}

    def op(self, eng, fn, reads=(), writes=(), sig=True):
        e = self.E[eng]
        for tok, kind in self._deps(reads, writes):
            if tok[0] == "c" and tok[1] == eng:
                if eng == "pe" or (kind != "raw" and eng != "pool"):
                    continue
            self._wait(e, tok)
        ins = fn(e.obj)
        e.seq += 1
        tok = ("c", eng, e.seq)
        if sig:
            ins.then_inc(e.sem, 1)
            e.sig_seqs.append(e.seq)
        self._commit(tok, reads, writes)
        self.n_ops += 1
        return tok

    def dma(self, q, out, in_, reads=(), writes=()):
        e = self.E[q]
        for tok, kind in self._deps(reads, writes):
            self._wait(e, tok)
        pool = self.dq[q]
        i = self.dq_next[q]
        self.dq_next[q] = (i + 1) % len(pool)
        ent = pool[i]
        if ent[1] > 0:
            self._wait(e, ("d", ent[0], ent[1]))
        ent[1] += 16
        e.obj.dma_start(out=out, in_=in_).then_inc(self.dsems[ent[0]], 16)
        tok = ("d", ent[0], ent[1])
        self._commit(tok, reads, writes)
        self.n_ops += 1
        return tok

    def barrier(self):
        for e in self.E.values():
            for src in self.E.values():
                if (src is e and e.name != "pool") or not src.sig_seqs:
                    continue
                cnt = len(src.sig_seqs)
                if e.waited.get(src.name, 0) < cnt:
                    e.obj.wait_ge(src.sem, cnt)
                    e.waited[src.name] = cnt
            for lst in self.dq.values():
                for semi, val in lst:
                    if val > 0 and e.waited.get(("d", semi), 0) < val:
                        e.obj.wait_ge(self.dsems[semi], val)
                        e.waited[("d", semi)] = val
        self.lastw.clear()
        self.readers.clear()


def ssl(start, n, step):
    return slice(start, start + (n - 1) * step + 1, step)


def alibi_slopes():
    return [2.0 ** (-8.0 * (h + 1) / 12.0) for h in range(12)]


def host_consts():
    kk = np.arange(128)[:, None]
    qq = np.arange(128)[None, :]
    eb = np.zeros((128, 12, 3, 128), np.float32)
    sl = alibi_slopes()
    for h in range(12):
        dil = DILS[h // 4]
        for j in range(3):
            delta = 128 * (j - 1) + kk - qq
            val = np.exp(-sl[h] * dil * np.abs(delta).astype(np.float64))
            eb[:, h, j, :] = np.where(np.abs(delta) <= 64, val, 0.0)
    cd = np.zeros((128, 5, 128), np.float32)
    cd[:, 0, :] = (kk <= qq)
    cd[:, 1, :] = (kk >= qq)
    cd[:, 2, :] = np.eye(128)
    cd[:, 3, :] = 1.0
    bd = np.zeros((128, 128), np.float32)
    bd[:64, :64] = 1.0
    bd[64:, 64:] = 1.0
    cd[:, 4, :] = bd
    return eb.reshape(128, -1), cd.reshape(128, -1)


def pack_params(inp):
    L = NL
    f = lambda k: np.asarray(inp[k], np.float32)
    cA = np.zeros((128, L, 8 * 3 + 2 + 3 * 44 + 44), np.float32)
    for l in range(L):
        o = 0
        for key in ("norm_mix_g", "norm_ffn_g", "norm_mem_g"):
            cA[:, l, o:o + 8] = f(key)[l].reshape(8, 128).T
            o += 8
        cA[:, l, o] = np.tile(f("att_q_g")[l], 2); o += 1
        cA[:, l, o] = np.tile(f("att_k_g")[l], 2); o += 1
        cA[:, l, o:o + 132] = f("ffn_conv_w")[l].reshape(3, 44, 128).transpose(2, 0, 1).reshape(128, 132); o += 132
        cA[:, l, o:o + 44] = f("ffn_conv_b")[l].reshape(44, 128).T; o += 44
    cB = np.zeros((128, L, 2 + 2 + 48 + 16), np.float32)
    for l in range(L):
        cB[:96, l, 0:2] = f("xatt_q_g")[l].reshape(2, 96).T
        cB[:96, l, 2:4] = f("xatt_k_g")[l].reshape(2, 96).T
        cB[:96, l, 4:52] = f("mlstm_conv_w")[l].reshape(3, 16, 96).transpose(2, 0, 1).reshape(96, 48)
        cB[:96, l, 52:68] = f("mlstm_conv_b")[l].reshape(16, 96).T
    cC = np.zeros((128, L, 16 + 768), np.float32)
    for l in range(L):
        cC[:, l, 0:16] = f("mlstm_gate_b")[l][None, :]
        cC[:, l, 16:] = f("mlstm_h_g")[l][None, :]
    return cA.reshape(128, -1), cB.reshape(128, -1), cC.reshape(128, -1)


def pack_cm(inp):
    f = lambda k: np.asarray(inp[k], np.float32)
    cM = np.zeros((128, NL, 48), np.float32)
    for l in range(NL):
        cM[:, l, 0:36] = f("mlstm_conv_w")[l].reshape(3, 12, 128).transpose(2, 0, 1).reshape(128, 36)
        cM[:, l, 36:48] = f("mlstm_conv_b")[l].reshape(12, 128).T
    return cM.reshape(128, -1)


NA = 8 * 3 + 2 + 132 + 44
NB = 68
NCC = 16 + 768


def build(seq_lens, depth=NL, dbg=(), upto=None):
    nc = bass.Bass("TRN2", target_bir_lowering=False)
    Smax = max(seq_lens)
    nseq = len(seq_lens)
    dt_in = lambda name, shape: nc.dram_tensor(name, list(shape), F32, kind="ExternalInput").ap()
    xs = [dt_in("x%d" % i, (seq_lens[i], D)) for i in range(nseq)]
    mems = [dt_in("mem%d" % i, (NMEM, D)) for i in range(nseq)]
    ys = [nc.dram_tensor("y%d" % i, [seq_lens[i], D], F32, kind="ExternalOutput").ap() for i in range(nseq)]
    w_in = dt_in("w_in", (NL, D, IN_DIM))
    w_kv = dt_in("w_mem_kv", (NL, D, 1536))
    w_br = dt_in("w_branch", (NL, 2304, D))
    w_out = dt_in("w_out", (NL, D, D))
    w_up = dt_in("w_up", (NL, D, 2 * FFN))
    w_dn = dt_in("w_down", (NL, FFN, D))
    cA_d = dt_in("cA", (128, NL * NA))
    cB_d = dt_in("cB", (128, NL * NB))
    cC_d = dt_in("cC", (128, NL * NCC))
    eb_d = dt_in("cEB", (128, 12 * 3 * 128))
    cM_d = dt_in("cM", (128, NL * 48))
    cd_d = dt_in("cD", (128, 5 * 128))

    def scratch(name, shape, dt):
        kind = "ExternalOutput" if name in dbg else "Internal"
        return nc.dram_tensor(name, list(shape), dt, kind=kind).ap()

    SP = Smax + 2 * PAD
    xT = scratch("xT", (D, Smax), F32)
    hT = scratch("hT", (D, SP), BF16)
    qaT = scratch("qaT", (768, Smax), BF16)
    kaT = scratch("kaT", (768, Smax), BF16)
    va = scratch("va", (Smax, 768), BF16)
    attT = scratch("attT", (768, Smax), BF16)
    mqT = scratch("mqT", (768, Smax), BF16)
    mkT = scratch("mkT", (768, Smax), BF16)
    mv = scratch("mv", (Smax, 768), BF16)
    mo = scratch("mo", (Smax, 768), F32)
    gts = scratch("gts", (Smax, 16), F32)
    hb = scratch("hb", (Smax, 768), F32)
    hmT = scratch("hmT", (768, Smax), BF16)
    xoT = scratch("xoT", (768, Smax), BF16)
    gT = scratch("gT", (FFN, Smax), BF16)

    es = ExitStack()
    with es:
        P = Prog(nc, es)
        uid = [0]

        def sb(st, name, shape, dt):
            uid[0] += 1
            return st.enter_context(nc.sbuf_tensor("sb%d_%s" % (uid[0], name), list(shape), dt))
        cA = sb(es, "cA", (128, NL, NA), F32)
        cB = sb(es, "cB", (128, NL, NB), F32)
        cM = sb(es, "cM", (128, NL, 48), F32)
        cD = sb(es, "cDs", (128, 5, 128), F32)
        cDb = sb(es, "cDb", (128, 5, 128), BF16)
        zer = sb(es, "zer", (128, 8, PAD), BF16)
        NPS = 6
        PS = [es.enter_context(nc.psum_tensor("ps%d" % i, [128, 512], F32)) for i in range(NPS)]
        PSB = es.enter_context(nc.psum_tensor("psb", [128, 1024], BF16))
        PSB2 = es.enter_context(nc.psum_tensor("psb2", [128, 1024], BF16))
        TRIF, TRIB, IDF, ONEF = cD[:, 0, :], cD[:, 1, :], cD[:, 2, :], cD[:, 3, :]
        IDB, ONEB, BDB = cDb[:, 2, :], cDb[:, 3, :], cDb[:, 4, :]
        P.dma("sp", cA[:].rearrange("p l n -> p (l n)"), cA_d, writes=["cA"])
        P.dma("sp", cB[:].rearrange("p l n -> p (l n)"), cB_d, writes=["cB"])
        P.dma("sp", cM[:].rearrange("p l n -> p (l n)"), cM_d, writes=["cM"])
        P.dma("sp", cD[:].rearrange("p l n -> p (l n)"), cd_d, writes=["cD"])
        P.dma("pool", cDb[:].rearrange("p l n -> p (l n)"), cd_d, writes=["cDb"])
        P.op("dve", lambda e: e.memset(zer[:], 0.0), writes=["zer"])
        epsT = sb(es, "epsT", (128, 4), F32)
        P.op("dve", lambda e: e.memset(epsT[:, 0:1], EPS), writes=["eps"])
        P.op("dve", lambda e: e.memset(epsT[:, 1:2], 1.0), writes=["eps1"])
        P.op("dve", lambda e: e.memset(epsT[:, 2:3], math.log(192.0 ** -0.5)), writes=["eps2"])
        EPSC = epsT[:, 0:1]
        ONEC = epsT[:, 1:2]
        LNC = epsT[:, 2:3]
        P.barrier()

        psn = [0]

        def run_tiles(n, ld, body):
            if n:
                ld(0)
            for it in range(n):
                if it + 1 < n:
                    ld(it + 1)
                body(it)

        pinned = set()

        def bank():
            for _ in range(NPS):
                psn[0] = (psn[0] + 1) % NPS
                if psn[0] not in pinned:
                    return psn[0]
            raise RuntimeError("all PSUM banks pinned")

        def fused_norm(nb, xtile, xkeys, T, lg, goff, t0, b):
            nsq, nrs, nho = nb
            P.op("act", lambda e: e.activation(out=nsq[:, :, 0:T], in_=xtile, func=AF.Square),
                 reads=xkeys, writes=["nsq"])
            bk = bank()
            for c in range(8):
                P.op("pe", lambda e, c=c: e.matmul(PS[bk][:, 0:T], ONEB, nsq[:, c, 0:T], start=(c == 0), stop=(c == 7)),
                     reads=["nsq", "cDb"], writes=[("ps", bk)], sig=(c == 7))
            P.op("act", lambda e: e.activation(out=nrs[:, 0:T], in_=PS[bk][:, 0:T], func=AF.Ln, scale=1.0 / D, bias=EPSC),
                 reads=[("ps", bk)], writes=["nrs"])
            P.op("act", lambda e: e.activation(out=nrs[:, 0:T], in_=nrs[:, 0:T], func=AF.Exp, scale=-0.5), reads=["nrs"], writes=["nrs"])
            for c in range(8):
                P.op("dve", lambda e, c=c: e.scalar_tensor_tensor(
                    nho[b][:, c, 0:T], xtile[:, c, :], cA[:, lg, goff + c:goff + c + 1], nrs[:, 0:T], ALU.mult, ALU.mult),
                    reads=xkeys + ["nrs", "cA"], writes=[("nho", b, c)])
            P.dma("sp", hT[:, PAD + t0:PAD + t0 + T].rearrange("(c p) t -> p c t", p=128), nho[b][:, :, 0:T],
                  reads=[("nho", b, c) for c in range(8)])

        def norm_bufs(st, T):
            return (sb(st, "fnsq", (128, 8, T), BF16), sb(st, "fnrs", (128, T), F32),
                    [sb(st, "fnho%d" % i, (128, 8, T), BF16) for i in range(2)])

        def load_w(st, name, src, kp, nk, ncols, c0=0):
            t = sb(st, name, (kp, nk, ncols), BF16)
            v = src.rearrange("(k p) n -> p k n", p=kp)
            step = max(1, 4096 // ncols)
            for k0 in range(0, nk, step):
                k1 = min(nk, k0 + step)
                P.dma("pool", t[:, k0:k1, :], v[:, k0:k1, c0:c0 + ncols], writes=[(name, k0)])
            return t, [(name, k0) for k0 in range(0, nk, step)]

        def phase_in(si, S):
            with ExitStack() as st:
                xin = [sb(st, "xin%d" % i, (128, 4, D), F32) for i in range(2)]
                xo = [sb(st, "xo%d" % i, (128, 8, 512), F32) for i in range(2)]
                P.dma("sp", hT[:, 0:PAD].rearrange("(c p) t -> p c t", p=128), zer[:], reads=["zer"])
                P.dma("sp", hT[:, PAD + S:PAD + S + PAD].rearrange("(c p) t -> p c t", p=128), zer[:], reads=["zer"])
                nt = S // 512

                def ld(it):
                    P.dma("sp", xin[it % 2][:], xs[si][it * 512:it * 512 + 512, :].rearrange("(j p) f -> p j f", p=128),
                          writes=[("xin", it % 2)])

                def body(it):
                    b = it % 2
                    t0 = it * 512
                    for c in range(8):
                        bk = bank()
                        for j in range(4):
                            P.op("pe", lambda e, bk=bk, j=j, c=c, b=b: e.transpose(
                                PS[bk][:, j * 128:(j + 1) * 128], xin[b][:, j, c * 128:(c + 1) * 128], IDF),
                                reads=[("xin", b), "cD"], writes=[("ps", bk)], sig=(j == 3))
                        eng = "act" if c % 2 else "dve"
                        if eng == "act":
                            P.op("act", lambda e, bk=bk, c=c, b=b: e.copy(xo[b][:, c, :], PS[bk][:]),
                                 reads=[("ps", bk)], writes=[("xo", b, c)])
                        else:
                            P.op("dve", lambda e, bk=bk, c=c, b=b: e.tensor_copy(xo[b][:, c, :], PS[bk][:]),
                                 reads=[("ps", bk)], writes=[("xo", b, c)])
                    P.dma("sp", xT[:, t0:t0 + 512].rearrange("(c p) t -> p c t", p=128), xo[b][:],
                          reads=[("xo", b, c) for c in range(8)])
                run_tiles(nt, ld, body)
                P.barrier()

        def phase_out(si, S):
            with ExitStack() as st:
                xi = [sb(st, "xi%d" % i, (128, 8, 512), F32) for i in range(2)]
                yo = [sb(st, "yo%d" % i, (128, 4, D), F32) for i in range(2)]
                def ld(it):
                    P.dma("sp", xi[it % 2][:], xT[:, it * 512:it * 512 + 512].rearrange("(c p) t -> p c t", p=128),
                          writes=[("xi", it % 2)])

                def body(it):
                    b = it % 2
                    t0 = it * 512
                    for j in range(4):
                        for half in range(2):
                            bk = bank()
                            for cc in range(4):
                                c = half * 4 + cc
                                P.op("pe", lambda e, bk=bk, j=j, c=c, cc=cc, b=b: e.transpose(
                                    PS[bk][:, cc * 128:(cc + 1) * 128], xi[b][:, c, j * 128:(j + 1) * 128], IDF),
                                    reads=[("xi", b), "cD"], writes=[("ps", bk)], sig=(cc == 3))
                            if half:
                                P.op("act", lambda e, bk=bk, j=j, b=b: e.copy(yo[b][:, j, 512:1024], PS[bk][:]),
                                     reads=[("ps", bk)], writes=[("yo", b, j, 1)])
                            else:
                                P.op("dve", lambda e, bk=bk, j=j, b=b: e.tensor_copy(yo[b][:, j, 0:512], PS[bk][:]),
                                     reads=[("ps", bk)], writes=[("yo", b, j, 0)])
                    P.dma("sp", ys[si][t0:t0 + 512, :].rearrange("(j p) f -> p j f", p=128), yo[b][:],
                          reads=[("yo", b, j, h) for j in range(4) for h in range(2)])
                run_tiles(S // 512, ld, body)
                P.barrier()

        def phase_norm(l, S, goff):
            with ExitStack() as st:
                xi = [sb(st, "nxi%d" % i, (128, 8, 512), F32) for i in range(2)]
                sq = sb(st, "nsq", (128, 8, 512), BF16)
                rs = sb(st, "nrs", (128, 512), F32)
                ho = [sb(st, "nho%d" % i, (128, 8, 512), BF16) for i in range(2)]
                def ld(it):
                    P.dma("sp", xi[it % 2][:], xT[:, it * 512:it * 512 + 512].rearrange("(c p) t -> p c t", p=128),
                          writes=[("xi", it % 2)])

                def body(it):
                    b = it % 2
                    t0 = it * 512
                    P.op("act", lambda e, b=b: e.activation(out=sq[:], in_=xi[b][:], func=AF.Square),
                         reads=[("xi", b)], writes=["sq"])
                    bk = bank()
                    for c in range(8):
                        P.op("pe", lambda e, bk=bk, c=c: e.matmul(PS[bk][:], ONEB, sq[:, c, :], start=(c == 0), stop=(c == 7)),
                             reads=["sq", "cDb"], writes=[("ps", bk)], sig=(c == 7))
                    P.op("act", lambda e, bk=bk: e.activation(out=rs[:], in_=PS[bk][:], func=AF.Ln, scale=1.0 / D, bias=EPSC),
                         reads=[("ps", bk)], writes=["rs"])
                    P.op("act", lambda e: e.activation(out=rs[:], in_=rs[:], func=AF.Exp, scale=-0.5), reads=["rs"], writes=["rs"])
                    for c in range(8):
                        P.op("dve", lambda e, c=c, b=b: e.scalar_tensor_tensor(
                            ho[b][:, c, :], xi[b][:, c, :], cA[:, l, goff + c:goff + c + 1], rs[:], ALU.mult, ALU.mult),
                            reads=[("xi", b), "rs", "cA"], writes=[("ho", b, c)])
                    P.dma("sp", hT[:, PAD + t0:PAD + t0 + 512].rearrange("(c p) t -> p c t", p=128), ho[b][:],
                          reads=[("ho", b, c) for c in range(8)])
                run_tiles(S // 512, ld, body)
                P.barrier()

        def phase_a1(l, S):
            with ExitStack() as st:
                W, wk = load_w(st, "wA", w_in[l], 128, 8, 2304, C_AQ)
                hi = [sb(st, "ahi%d" % i, (128, 8, 512), BF16) for i in range(2)]
                sq = [sb(st, "asq%d" % i, (128, 512), BF16) for i in range(2)]
                rs = [sb(st, "ars%d" % i, (128, 512), F32) for i in range(2)]
                qo = [sb(st, "aqo%d" % i, (128, 12, 512), BF16) for i in range(2)]
                vo = [sb(st, "avo%d" % i, (128, 4, 768), BF16) for i in range(2)]

                def ld(it):
                    P.dma("sp", hi[it % 2][:], hT[:, PAD + it * 512:PAD + it * 512 + 512].rearrange("(c p) t -> p c t", p=128),
                          writes=[("hi", it % 2)])

                def group(b, oc):
                    bk = bank()
                    for k in range(8):
                        P.op("pe", lambda e, bk=bk, k=k, oc=oc, b=b: e.matmul(
                            PS[bk][:], W[:, k, oc * 128:(oc + 1) * 128], hi[b][:, k, :], start=(k == 0), stop=(k == 7)),
                            reads=[("hi", b)] + wk, writes=[("ps", bk)], sig=(k == 7))
                    s = oc % 2
                    P.op("act", lambda e, bk=bk, s=s: e.activation(out=sq[s][:], in_=PS[bk][:], func=AF.Square),
                         reads=[("ps", bk)], writes=[("sq", s)])
                    return bk

                def norm(b, oc, bk):
                    s = oc % 2
                    bk2 = bank()
                    P.op("pe", lambda e, bk2=bk2, s=s: e.matmul(PS[bk2][:], BDB, sq[s][:], start=True, stop=True),
                         reads=[("sq", s), "cDb"], writes=[("ps", bk2)])
                    P.op("act", lambda e, bk2=bk2, s=s: e.activation(out=rs[s][:], in_=PS[bk2][:], func=AF.Ln, scale=1.0 / 64, bias=EPSC),
                         reads=[("ps", bk2)], writes=[("rs", s)])
                    P.op("act", lambda e, s=s: e.activation(out=rs[s][:], in_=rs[s][:], func=AF.Exp, scale=-0.5), reads=[("rs", s)], writes=[("rs", s)])
                    gcol = 24 + (0 if oc < 6 else 1)
                    P.op("dve", lambda e, bk=bk, s=s, oc=oc, b=b, gcol=gcol: e.scalar_tensor_tensor(
                        qo[b][:, oc, :], PS[bk][:], cA[:, l, gcol:gcol + 1], rs[s][:], ALU.mult, ALU.mult),
                        reads=[("ps", bk), ("rs", s), "cA"], writes=[("qo", b, oc)])

                def body(it):
                    b = it % 2
                    t0 = it * 512
                    prev = None
                    for oc in range(12):
                        bk = group(b, oc)
                        if prev is not None:
                            norm(b, prev[0], prev[1])
                        prev = (oc, bk)
                    vgroups = [(j, n0, nn) for j in range(4) for (n0, nn) in ((0, 512), (512, 256))]
                    for gi, (j, n0, nn) in enumerate(vgroups):
                        bk = bank()
                        for k in range(8):
                            P.op("pe", lambda e, bk=bk, k=k, j=j, n0=n0, nn=nn, b=b: e.matmul(
                                PS[bk][:, 0:nn], hi[b][:, k, j * 128:(j + 1) * 128], W[:, k, 1536 + n0:1536 + n0 + nn],
                                start=(k == 0), stop=(k == 7)),
                                reads=[("hi", b)] + wk, writes=[("ps", bk)], sig=(k == 7))
                        P.op("act", lambda e, bk=bk, j=j, n0=n0, nn=nn, b=b: e.copy(vo[b][:, j, n0:n0 + nn], PS[bk][:, 0:nn]),
                             reads=[("ps", bk)], writes=[("vo", b, j, n0)])
                        if gi == 0:
                            norm(b, prev[0], prev[1])
                            P.dma("sp", qaT[:, t0:t0 + 512].rearrange("(c p) t -> p c t", p=128), qo[b][:, 0:6, :],
                                  reads=[("qo", b, oc) for oc in range(6)])
                            P.dma("sp", kaT[:, t0:t0 + 512].rearrange("(c p) t -> p c t", p=128), qo[b][:, 6:12, :],
                                  reads=[("qo", b, oc) for oc in range(6, 12)])
                    P.dma("sp", va[t0:t0 + 512, :].rearrange("(j p) f -> p j f", p=128), vo[b][:],
                          reads=[("vo", b, j, n0) for j in range(4) for n0 in (0, 512)])
                run_tiles(S // 512, ld, body)
                P.barrier()

        def phase_a2(l, S):
            with ExitStack() as st:
                EBs = [sb(st, "EB%d" % i, (128, 3, 384), F32) for i in range(2)]
                ebv = eb_d.rearrange("p (g q n) -> p g q n", g=3, q=4)
                qs = sb(st, "a2q", (128, 6, 2048), BF16)
                kw = [2048 + 256 * d for d in DILS]
                ks = [sb(st, "a2k%d" % g, (128, 2, kw[g]), BF16) for g in range(3)]
                ntl = [2048 // d // 128 + 2 for d in DILS]
                vs = [sb(st, "a2v%d" % g, (128, ntl[g], DILS[g], 256), BF16) for g in range(3)]
                OD = sb(st, "a2od", (64, 2, 3, 2048), F32)
                DS = sb(st, "a2ds", (64, 2048), F32)
                AO = sb(st, "a2ao", (64, 2, 2048), BF16)
                Ee = [sb(st, "a2e%d" % i, (128, 384), F32) for i in range(3)]
                Pp = [sb(st, "a2p%d" % i, (128, 384), BF16) for i in range(3)]
                cnt = {"un": 0, "ao": 0, "eb": 0}

                def stage_a(U):
                    bk = bank()
                    U["bk"] = bk
                    for jj, ksl in enumerate(U["ksl"]):
                        P.op("pe", lambda e, bk=bk, jj=jj, ksl=ksl, qsl=U["qsl"]: e.matmul(
                            PS[bk][:, jj * 128:(jj + 1) * 128], ksl, qsl, start=True, stop=True),
                            reads=["qs", ("ks", U["g"])], writes=[("ps", bk)], sig=(jj == U["nb"] - 1))

                def stage_b1(U):
                    u, nb_, bk = U["u"], U["nb"], U["bk"]
                    P.op("act", lambda e: e.activation(
                        out=Ee[u][:, 0:128 * nb_], in_=PS[bk][:, 0:128 * nb_], func=AF.Exp, scale=0.125),
                        reads=[("ps", bk)], writes=[("Ee", u)])

                def stage_b2(U):
                    u, nb_, bk = U["u"], U["nb"], U["bk"]
                    eng = "dve"
                    EB, g, jlo = U["EB"], U["g"], U["jlo"]
                    P.op(eng, lambda e: e.tensor_tensor(
                        Pp[u][:, 0:128 * nb_], Ee[u][:, 0:128 * nb_], EB[:, g, jlo * 128:(jlo + nb_) * 128], ALU.mult),
                        reads=[("Ee", u), ("EB", U["ebi"])], writes=[("Pp", u)])

                def stage_c1(U):
                    u, nb_, g = U["u"], U["nb"], U["g"]
                    bk2 = bank()
                    U["bk2"] = bk2
                    for jj, (vsl, vkey) in enumerate(U["vsl"]):
                        P.op("pe", lambda e, jj=jj, vsl=vsl: e.matmul(
                            PS[bk2][0:64, 0:128], vsl, Pp[u][:, jj * 128:(jj + 1) * 128],
                            start=(jj == 0), stop=(jj == nb_ - 1)),
                            reads=[("Pp", u), vkey], writes=[("ps", bk2)], sig=False)
                    for jj in range(nb_):
                        P.op("pe", lambda e, jj=jj: e.matmul(
                            PS[bk2][0:64, 128:256], ONEB[:, 0:64], Pp[u][:, jj * 128:(jj + 1) * 128],
                            start=(jj == 0), stop=(jj == nb_ - 1)),
                            reads=[("Pp", u), "cDb"], writes=[("ps", bk2)], sig=(jj == nb_ - 1))

                def stage_c2(U):
                    osl, bk2, g = U["osl"], U["bk2"], U["g"]
                    P.op("act", lambda e: e.copy(osl, PS[bk2][0:64, 0:256].rearrange("p (a b) -> p a b", a=2)),
                         reads=[("ps", bk2)], writes=[("OD", g)])

                for sbi in range(S // 2048):
                    T0 = sbi * 2048
                    P.dma("sp", qs[:], qaT[:, T0:T0 + 2048].rearrange("(c p) t -> p c t", p=128), writes=["qs"])
                    K0s, mt0s = [], []
                    for g in range(3):
                        d = DILS[g]
                        K0 = max(0, T0 - 128 * d)
                        K1 = min(S, T0 + 2048 + 128 * d)
                        K0s.append(K0)
                        P.dma("sp", ks[g][:, :, 0:K1 - K0],
                              kaT[256 * g:256 * g + 256, K0:K1].rearrange("(c p) t -> p c t", p=128), writes=[("ks", g)])
                        m_lo = max(0, T0 // d // 128 - 1)
                        m_hi = min(S // d // 128, (T0 + 2048) // d // 128 + 1)
                        mt0s.append(m_lo)
                        for mm in range(m_lo, m_hi):
                            P.dma("sp", vs[g][:, mm - m_lo, :, :],
                                  va[mm * 128 * d:(mm + 1) * 128 * d, 256 * g:256 * g + 256].rearrange("(i r) f -> i r f", r=d),
                                  writes=[("vs", g, mm - m_lo)])
                    for hh in range(4):
                        ebi = cnt["eb"] % 2
                        cnt["eb"] += 1
                        EB = EBs[ebi]
                        P.dma("sp", EB[:], ebv[:, :, hh, :], writes=[("EB", ebi)])
                        units = []
                        for g in range(3):
                            if os.environ.get("A2G") and str(g) not in os.environ["A2G"]:
                                continue
                            d = DILS[g]
                            h = 4 * g + hh
                            c = h // 2
                            pb = 64 * (h % 2)
                            Lt = S // d // 128
                            for r in range(d):
                                for j in range(2048 // d // 128):
                                    m = T0 // d // 128 + j
                                    tiles = [mm for mm in (m - 1, m, m + 1) if 0 <= mm < Lt]
                                    U = {"g": g, "nb": len(tiles), "jlo": tiles[0] - (m - 1), "EB": EB, "ebi": ebi,
                                         "u": cnt["un"] % 3, "n": cnt["un"]}
                                    cnt["un"] += 1
                                    U["qsl"] = qs[pb:pb + 64, c, ssl(r + 128 * j * d, 128, d)]
                                    U["ksl"] = [ks[g][pb:pb + 64, c - 2 * g, ssl(mm * 128 * d + r - K0s[g], 128, d)] for mm in tiles]
                                    U["vsl"] = [(vs[g][:, mm - mt0s[g], r, 64 * hh:64 * hh + 64], ("vs", g, mm - mt0s[g])) for mm in tiles]
                                    U["osl"] = OD[:, :, g, ssl(r + 128 * j * d, 128, d)]
                                    units.append(U)
                        NU = len(units)
                        stage_a(units[0])
                        if NU > 1:
                            stage_a(units[1])
                        stage_b1(units[0])
                        stage_b2(units[0])
                        for i, U in enumerate(units):
                            if i + 2 < NU:
                                stage_a(units[i + 2])
                            if i + 1 < NU:
                                stage_b1(units[i + 1])
                            stage_c1(U)
                            if i + 1 < NU:
                                stage_b2(units[i + 1])
                            stage_c2(U)
                        P.op("dve", lambda e: e.tensor_tensor(DS[:], OD[:, 1, 0, :], OD[:, 1, 1, :], ALU.add),
                             reads=[("OD", 0), ("OD", 1)], writes=["DS"])
                        P.op("dve", lambda e: e.tensor_tensor(DS[:], DS[:], OD[:, 1, 2, :], ALU.add),
                             reads=["DS", ("OD", 2)], writes=["DS"])
                        P.op("act", lambda e: e.activation(out=DS[:], in_=DS[:], func=AF.Ln), reads=["DS"], writes=["DS"])
                        P.op("act", lambda e: e.activation(out=DS[:], in_=DS[:], func=AF.Exp, scale=-1.0), reads=["DS"], writes=["DS"])
                        for g in range(3):
                            ai = cnt["ao"] % 2
                            cnt["ao"] += 1
                            P.op("dve", lambda e, g=g, ai=ai: e.tensor_tensor(AO[:, ai, :], OD[:, 0, g, :], DS[:], ALU.mult),
                                 reads=["DS", ("OD", g)], writes=[("AO", ai)])
                            P.dma("sp", attT[64 * (4 * g + hh):64 * (4 * g + hh) + 64, T0:T0 + 2048], AO[:, ai, :],
                                  reads=[("AO", ai)])
                P.barrier()

        def phase_m1a(l, S):
            with ExitStack() as st:
                W, wk = load_w(st, "wMa", w_in[l], 128, 8, 1536, C_MQ)
                hi = [sb(st, "mhi%d" % i, (128, 8, 512), BF16) for i in range(2)]
                tmp = [sb(st, "mtmp%d" % i, (128, 512), F32) for i in range(4)]
                qo = [sb(st, "mqo%d" % i, (128, 12, 512), BF16) for i in range(2)]
                tl = [(t, min(510, S - t)) for t in range(0, S, 510)]

                def ld(it):
                    t0, ntok = tl[it]
                    P.dma("sp", hi[it % 2][:, :, 0:ntok + 2],
                          hT[:, PAD + t0 - 1:PAD + t0 + 1 + ntok].rearrange("(c p) t -> p c t", p=128), writes=[("hi", it % 2)])

                def body(it):
                    t0, ntok = tl[it]
                    ncol = ntok + 2
                    b = it % 2
                    pend = None
                    for oc in range(12):
                        bk = bank()
                        for k in range(8):
                            P.op("pe", lambda e, bk=bk, k=k, oc=oc: e.matmul(
                                PS[bk][:, 0:ncol], W[:, k, oc * 128:(oc + 1) * 128], hi[b][:, k, 0:ncol],
                                start=(k == 0), stop=(k == 7)),
                                reads=[("hi", b)] + wk, writes=[("ps", bk)], sig=(k == 7))
                        u = oc % 4
                        w0, w1, w2 = (cM[:, l, jj * 12 + oc:jj * 12 + oc + 1] for jj in range(3))
                        bb = cM[:, l, 36 + oc:37 + oc]
                        P.op("act", lambda e, bk=bk, u=u, w1=w1, bb=bb: e.activation(
                            out=tmp[u][:, 0:ntok], in_=PS[bk][:, 1:1 + ntok], func=AF.Identity, scale=w1, bias=bb),
                            reads=[("ps", bk), "cM"], writes=[("tmp", u)])
                        if pend is not None:
                            pend()
                        P.op("dve", lambda e, bk=bk, u=u, w0=w0: e.scalar_tensor_tensor(
                            tmp[u][:, 0:ntok], PS[bk][:, 0:ntok], w0, tmp[u][:, 0:ntok], ALU.mult, ALU.add),
                            reads=[("ps", bk), ("tmp", u), "cM"], writes=[("tmp", u)])
                        P.op("dve", lambda e, bk=bk, u=u, w2=w2: e.scalar_tensor_tensor(
                            tmp[u][:, 0:ntok], PS[bk][:, 2:2 + ntok], w2, tmp[u][:, 0:ntok], ALU.mult, ALU.add),
                            reads=[("ps", bk), ("tmp", u), "cM"], writes=[("tmp", u)])

                        def pend(u=u, oc=oc):
                            P.op("act", lambda e: e.activation(out=qo[b][:, oc, 0:ntok], in_=tmp[u][:, 0:ntok], func=AF.Silu),
                                 reads=[("tmp", u)], writes=[("qo", b, oc)])
                    pend()
                    P.dma("sp", mqT[:, t0:t0 + ntok].rearrange("(c p) t -> p c t", p=128), qo[b][:, 0:6, 0:ntok],
                          reads=[("qo", b, oc) for oc in range(6)])
                    P.dma("sp", mkT[:, t0:t0 + ntok].rearrange("(c p) t -> p c t", p=128), qo[b][:, 6:12, 0:ntok],
                          reads=[("qo", b, oc) for oc in range(6, 12)])
                run_tiles(len(tl), ld, body)
                P.barrier()

        def phase_m1b(l, S):
            with ExitStack() as st:
                W, wk = load_w(st, "wMb", w_in[l], 128, 8, 1552, C_MV)
                cCl = sb(st, "cCl", (128, 16), F32)
                P.dma("sp", cCl[:], cC_d[:, l * NCC:l * NCC + 16], writes=["cCl"])
                hi = [sb(st, "bhi%d" % i, (128, 8, 512), BF16) for i in range(2)]
                vo = [sb(st, "bvo%d" % i, (128, 4, 768), BF16) for i in range(2)]
                oo = [sb(st, "boo%d" % i, (128, 4, 768), F32) for i in range(2)]
                go = [sb(st, "bgo%d" % i, (128, 4, 16), F32) for i in range(2)]
                gt = sb(st, "bgt", (128, 4, 8), F32)
                def ld(it):
                    P.dma("sp", hi[it % 2][:], hT[:, PAD + it * 512:PAD + it * 512 + 512].rearrange("(c p) t -> p c t", p=128),
                          writes=[("hi", it % 2)])

                def body(it):
                    b = it % 2
                    t0 = it * 512
                    bkg = bank()
                    for j in range(4):
                        for k in range(8):
                            P.op("pe", lambda e, k=k, j=j, b=b, bkg=bkg: e.matmul(
                                PS[bkg][:, j * 16:(j + 1) * 16], hi[b][:, k, j * 128:(j + 1) * 128], W[:, k, 1536:1552],
                                start=(k == 0), stop=(k == 7)),
                                reads=[("hi", b)] + wk, writes=[("ps", bkg)], sig=(k == 7))
                    for j in range(4):
                        P.op("dve", lambda e, j=j, b=b, bkg=bkg: e.tensor_tensor(
                            go[b][:, j, :], PS[bkg][:, j * 16:(j + 1) * 16], cCl[:], ALU.add),
                            reads=[("ps", bkg), "cCl"], writes=[("go", b)])
                    for half in range(2):
                        src = go[b][:, :, 8 * half + 4:8 * half + 8]
                        P.op("act", lambda e, src=src, half=half: e.activation(out=gt[:, :, 4 * half:4 * half + 4], in_=src, func=AF.Exp, scale=-1.0),
                             reads=[("go", b)], writes=["gt"])
                    P.op("act", lambda e: e.activation(out=gt[:], in_=gt[:], func=AF.Ln, bias=ONEC, scale=1.0),
                         reads=["gt", "eps1"], writes=["gt"])
                    for half in range(2):
                        dst = go[b][:, :, 8 * half + 4:8 * half + 8]
                        P.op("dve", lambda e, dst=dst, half=half: e.tensor_scalar(dst, gt[:, :, 4 * half:4 * half + 4], -1.0, None, ALU.mult),
                             reads=["gt"], writes=[("go", b)])
                    P.dma("sp", gts[t0:t0 + 512, :].rearrange("(j p) f -> p j f", p=128), go[b][:], reads=[("go", b)])
                    for j in range(4):
                        for gi, (n0, nn) in enumerate(((0, 512), (512, 256), (768, 512), (1280, 256))):
                            bk = bank()
                            for k in range(8):
                                P.op("pe", lambda e, bk=bk, k=k, j=j, n0=n0, nn=nn, b=b: e.matmul(
                                    PS[bk][:, 0:nn], hi[b][:, k, j * 128:(j + 1) * 128], W[:, k, n0:n0 + nn],
                                    start=(k == 0), stop=(k == 7)),
                                    reads=[("hi", b)] + wk, writes=[("ps", bk)], sig=(k == 7))
                            if gi < 2:
                                if gi == 0:
                                    P.op("dve", lambda e, bk=bk, j=j, n0=n0, nn=nn, b=b: e.tensor_copy(vo[b][:, j, n0:n0 + nn], PS[bk][:, 0:nn]),
                                         reads=[("ps", bk)], writes=[("vo", b, j, gi)])
                                else:
                                    P.op("act", lambda e, bk=bk, j=j, n0=n0, nn=nn, b=b: e.copy(vo[b][:, j, n0:n0 + nn], PS[bk][:, 0:nn]),
                                         reads=[("ps", bk)], writes=[("vo", b, j, gi)])
                            else:
                                P.op("act", lambda e, bk=bk, j=j, n0=n0, nn=nn, b=b: e.activation(
                                    out=oo[b][:, j, n0 - 768:n0 - 768 + nn], in_=PS[bk][:, 0:nn], func=AF.Sigmoid),
                                    reads=[("ps", bk)], writes=[("oo", b, j, gi)])
                    P.dma("sp", mv[t0:t0 + 512, :].rearrange("(j p) f -> p j f", p=128), vo[b][:],
                          reads=[("vo", b, j, gi) for j in range(4) for gi in range(2)])
                    P.dma("sp", mo[t0:t0 + 512, :].rearrange("(j p) f -> p j f", p=128), oo[b][:],
                          reads=[("oo", b, j, gi) for j in range(4) for gi in (2, 3)])
                run_tiles(S // 512, ld, body)
                P.barrier()

        def phase_scan(l, S, fwd):
            with ExitStack() as st:
                gofs = 0 if fwd else 8
                MASK = TRIF if fwd else TRIB
                qg = [sb(st, "sq%d" % i, (96, 8, 512), BF16) for i in range(2)]
                kg = [sb(st, "sk%d" % i, (96, 8, 512), BF16) for i in range(2)]
                vg = [sb(st, "sv%d" % i, (128, 4, 4, 194), BF16) for i in range(2)]
                gg = [sb(st, "sg%d" % i, (128, 4, 16), F32) for i in range(2)]
                for i in range(2):
                    P.op("pool", lambda e, i=i: e.memset(vg[i][:, :, :, 192:194], 1.0), writes=[("vg1", i)])
                Cs = sb(st, "sC", (96, 4, 2, 194), F32)
                Cb = sb(st, "sCb", (96, 4, 2, 194), BF16)
                P.op("dve", lambda e: e.memset(Cs[:], 0.0), writes=[("C", h) for h in range(4)])
                P.op("pool", lambda e: e.memset(Cb[:], 0.0), writes=[("Cb", h) for h in range(4)])
                sm = [sb(st, "ssm%d" % i, (128, 20), F32) for i in range(3)]
                smb = [sb(st, "ssmb%d" % i, (128, 8), F32) for i in range(3)]
                kgs = [sb(st, "skgs%d" % i, (128, 768), BF16) for i in range(2)]
                smt = [sb(st, "ssmt%d" % i, (128, 128), BF16) for i in range(4)]
                t1 = [sb(st, "st1%d" % i, (128, 4), F32) for i in range(2)]
                ho = [sb(st, "sho%d" % i, (128, 4, 768), F32) for i in range(2)]
                if fwd:
                    hbg = [sb(st, "shb%d" % i, (128, 4, 768), F32) for i in range(2)]
                    mog = [sb(st, "smo%d" % i, (128, 4, 768), F32) for i in range(2)]
                    MHG = sb(st, "smhg", (128, 768), F32)
                    P.dma("sp", MHG[:], cC_d[:, l * NCC + 16:l * NCC + 16 + 768], writes=["MHG"])
                    ss = [sb(st, "sss%d" % i, (128, 4), F32) for i in range(2)]
                    hbf = [sb(st, "shbf%d" % i, (128, 768), BF16) for i in range(2)]
                    hmo = [sb(st, "shmo%d" % i, (128, 6, 512), BF16) for i in range(2)]
                ng = S // 512
                order = list(range(ng)) if fwd else list(range(ng - 1, -1, -1))

                def ld(gi_):
                    b = gi_ % 2
                    T0 = order[gi_] * 512
                    P.dma("sp", qg[b][:], mqT[:, T0:T0 + 512].rearrange("(c p) t -> p c t", p=96), writes=[("qg", b)])
                    P.dma("sp", kg[b][:], mkT[:, T0:T0 + 512].rearrange("(c p) t -> p c t", p=96), writes=[("kg", b)])
                    for j in range(4):
                        P.dma("sp", vg[b][:, j, :, 0:192], mv[T0 + j * 128:T0 + (j + 1) * 128, :].rearrange("p (h f) -> p h f", h=4),
                              writes=[("vg", b, j)])
                    P.dma("sp", gg[b][:], gts[T0:T0 + 512, :].rearrange("(j p) f -> p j f", p=128), writes=[("gg", b)])
                    if fwd:
                        P.dma("sp", hbg[b][:], hb[T0:T0 + 512, :].rearrange("(j p) f -> p j f", p=128), writes=[("hbg", b)])
                        P.dma("sp", mog[b][:], mo[T0:T0 + 512, :].rearrange("(j p) f -> p j f", p=128), writes=[("mog", b)])

                def chunk_s1(b, j, cn):
                    cs = slice(j * 128, (j + 1) * 128)
                    u = cn % 3
                    kb = cn % 2
                    smu = sm[u]
                    bkA = bank()
                    lf = gg[b][:, j, gofs + 4:gofs + 8]
                    li = gg[b][:, j, gofs:gofs + 4]
                    P.op("pe", lambda e: e.matmul(PS[bkA][:, 0:4], MASK, lf, start=True, stop=True),
                         reads=[("gg", b), "cD"], writes=[("ps", bkA)], sig=False)
                    P.op("pe", lambda e: e.matmul(PS[bkA][:, 4:8], ONEF, lf, start=True, stop=True),
                         reads=[("gg", b), "cD"], writes=[("ps", bkA)])
                    smb_ = smb[u]
                    P.op("act", lambda e: e.copy(smb_[:], PS[bkA][:, 0:8]), reads=[("ps", bkA)], writes=[("smb", u)])
                    P.op("dve", lambda e: e.tensor_tensor(smu[:, 0:4], li, smb_[:, 0:4], ALU.subtract),
                         reads=[("smb", u), ("gg", b)], writes=[("sm", u, 0)])
                    P.op("act", lambda e: e.activation(out=smu[:, 8:12], in_=smb_[:, 0:4], func=AF.Exp, bias=LNC, scale=1.0),
                         reads=[("smb", u), "eps2"], writes=[("sm", u, 2)])
                    P.op("act", lambda e: e.activation(out=smu[:, 12:16], in_=smb_[:, 4:8], func=AF.Exp),
                         reads=[("smb", u)], writes=[("sm", u, 3)])
                    P.op("act", lambda e: e.activation(out=smu[:, 4:8], in_=smu[:, 0:4], func=AF.Exp),
                         reads=[("sm", u, 0)], writes=[("sm", u, 1)])
                    P.op("dve", lambda e: e.tensor_tensor(smu[:, 16:20], smu[:, 4:8], smu[:, 12:16], ALU.mult),
                         reads=[("sm", u, 1), ("sm", u, 3)], writes=[("sm", u, 4)])
                    for c in range(8):
                        P.op("pe", lambda e, c=c: e.transpose(PSB[:, c * 96:(c + 1) * 96], kg[b][:, c, cs], IDB[0:96, 0:96]),
                             reads=[("kg", b), "cDb"], writes=["psb"], sig=(c == 7))
                    for h in range(4):
                        P.op("act", lambda e, h=h: e.mul(kgs[kb][:, h * 192:(h + 1) * 192], PSB[:, h * 192:(h + 1) * 192], smu[:, 16 + h:17 + h]),
                             reads=["psb", ("sm", u, 4)], writes=[("kgs", kb, h)])

                def chunk_rest(b, j, cn):
                    cs = slice(j * 128, (j + 1) * 128)
                    u = cn % 3
                    kb = cn % 2
                    smu = sm[u]
                    tt = t1[cn % 2]
                    bS = []
                    for h in range(4):
                        bkS = bank()
                        bS.append(bkS)
                        for cc in range(2):
                            P.op("pe", lambda e, bkS=bkS, cc=cc, h=h: e.matmul(
                                PS[bkS][:, 0:128], kg[b][:, 2 * h + cc, cs], qg[b][:, 2 * h + cc, cs], start=(cc == 0), stop=(cc == 1)),
                                reads=[("kg", b), ("qg", b)], writes=[("ps", bkS)], sig=(cc == 1))
                    for h in range(4):
                        P.op("dve", lambda e, h=h: e.scalar_tensor_tensor(
                            smt[h][:], PS[bS[h]][:, 0:128], smu[:, 4 + h:5 + h], MASK, ALU.mult, ALU.mult),
                            reads=[("ps", bS[h]), ("sm", u, 1), "cD"], writes=[("smt", h)])
                    bN = []
                    for h in range(4):
                        bkN = bank()
                        bN.append(bkN)
                        P.op("pe", lambda e, bkN=bkN, h=h: e.matmul(
                            PS[bkN][:, 0:193], smt[h][:], vg[b][:, j, h, 0:193], start=True, stop=False),
                            reads=[("smt", h), ("vg", b, j), ("vg1", b)], writes=[("ps", bkN)], sig=False)
                        for cc in range(2):
                            P.op("pe", lambda e, bkN=bkN, cc=cc, h=h: e.matmul(
                                PS[bkN][:, 0:193], qg[b][:, 2 * h + cc, cs], Cb[:, h, cc, 0:193], start=False, stop=(cc == 1)),
                                reads=[("qg", b), ("Cb", h)], writes=[("ps", bkN)], sig=(cc == 1))
                    for h in range(4):
                        P.op("act", lambda e, h=h: e.activation(
                            out=tt[:, h:h + 1], in_=PS[bN[h]][:, 192:193], func=AF.Abs, scale=smu[:, 8 + h:9 + h]),
                            reads=[("ps", bN[h]), ("sm", u, 2)], writes=[("t1", cn % 2)])
                    P.op("dve", lambda e: e.tensor_scalar_max(tt[:], tt[:], 1.0), reads=[("t1", cn % 2)], writes=[("t1", cn % 2)])
                    P.op("dve", lambda e: e.reciprocal(tt[:], tt[:]), reads=[("t1", cn % 2)], writes=[("t1", cn % 2)])
                    P.op("dve", lambda e: e.tensor_tensor(tt[:], tt[:], smu[:, 8:12], ALU.mult),
                         reads=[("t1", cn % 2), ("sm", u, 2)], writes=[("t1", cn % 2)])
                    for h in range(4):
                        P.op("act", lambda e, h=h: e.mul(ho[b][:, j, h * 192:(h + 1) * 192], PS[bN[h]][:, 0:192], tt[:, h:h + 1]),
                             reads=[("ps", bN[h]), ("t1", cn % 2)], writes=[("ho", b, j, h)])
                    for h in range(4):
                        bkC = bank()
                        for cc in range(2):
                            P.op("pe", lambda e, bkC=bkC, cc=cc, h=h: e.matmul(
                                PS[bkC][0:96, cc * 193:(cc + 1) * 193], kgs[kb][:, h * 192 + cc * 96:h * 192 + (cc + 1) * 96], vg[b][:, j, h, 0:193],
                                start=True, stop=True),
                                reads=[("kgs", kb, h), ("vg", b, j), ("vg1", b)], writes=[("ps", bkC)], sig=(cc == 1))
                        Cv = Cs[:, h, :, 0:193]
                        P.op("dve", lambda e, bkC=bkC, Cv=Cv, h=h: e.scalar_tensor_tensor(
                            Cv, Cv, smu[0:96, 12 + h:13 + h], PS[bkC][0:96, 0:386].rearrange("p (a b) -> p a b", a=2), ALU.mult, ALU.add),
                            reads=[("ps", bkC), ("C", h), ("sm", u, 3)], writes=[("C", h)])
                        P.op("act", lambda e, h=h: e.copy(Cb[:, h, :, :], Cs[:, h, :, :]),
                             reads=[("C", h)], writes=[("Cb", h)])
                    if fwd:
                        hv = ho[b][:, j, :]
                        si_ = cn % 2
                        hk = [("ho", b, j, h) for h in range(4)]
                        P.op("dve", lambda e: e.tensor_tensor(hv, hv, hbg[b][:, j, :], ALU.add),
                             reads=hk + [("hbg", b)], writes=hk)
                        for h in range(4):
                            P.op("act", lambda e, h=h: e.activation(
                                out=hbf[si_][:, h * 192:(h + 1) * 192], in_=hv[:, h * 192:(h + 1) * 192], func=AF.Square, accum_out=ss[si_][:, h:h + 1]),
                                reads=[("ho", b, j, h)], writes=[("ss", si_), ("hbf", si_)])
                        P.op("act", lambda e: e.activation(out=ss[si_][:], in_=ss[si_][:], func=AF.Ln, scale=1.0 / 192, bias=EPSC),
                             reads=[("ss", si_), "eps"], writes=[("ss", si_)])
                        P.op("act", lambda e: e.activation(out=ss[si_][:], in_=ss[si_][:], func=AF.Exp, scale=-0.5), reads=[("ss", si_)], writes=[("ss", si_)])
                        for h in range(4):
                            P.op("dve", lambda e, h=h: e.scalar_tensor_tensor(
                                hv[:, h * 192:(h + 1) * 192], hv[:, h * 192:(h + 1) * 192], ss[si_][:, h:h + 1],
                                MHG[:, h * 192:(h + 1) * 192], ALU.mult, ALU.mult),
                                reads=[("ho", b, j, h), ("ss", si_), "MHG"], writes=[("ho", b, j, h)])
                        P.op("dve", lambda e: e.tensor_tensor(hbf[si_][:], hv, mog[b][:, j, :], ALU.mult),
                             reads=hk + [("mog", b)], writes=[("hbf", si_)])
                        for c in range(6):
                            P.op("pe", lambda e, c=c: e.transpose(PSB2[:, c * 128:(c + 1) * 128], hbf[si_][:, c * 128:(c + 1) * 128], IDB),
                                 reads=[("hbf", si_), "cDb"], writes=["psb2"], sig=(c == 5))
                        P.op("act", lambda e: e.copy(hmo[b][:, :, cs], PSB2[:, 0:768].rearrange("p (a b) -> p a b", a=6)),
                             reads=["psb2"], writes=[("hmo", b, j)])

                def store(gi_):
                    b = gi_ % 2
                    T0 = order[gi_] * 512
                    if fwd:
                        P.dma("sp", hmT[:, T0:T0 + 512].rearrange("(c p) t -> p c t", p=128), hmo[b][:],
                              reads=[("hmo", b, j) for j in range(4)])
                    else:
                        P.dma("sp", hb[T0:T0 + 512, :].rearrange("(j p) f -> p j f", p=128), ho[b][:],
                              reads=[("ho", b, j, h) for j in range(4) for h in range(4)])

                chunks = []
                for gi_ in range(ng):
                    for ji, j in enumerate(range(4) if fwd else range(3, -1, -1)):
                        chunks.append((gi_, gi_ % 2, j, gi_ * 4 + ji, ji))
                ld(0)
                chunk_s1(chunks[0][1], chunks[0][2], chunks[0][3])
                for i, (gi_, b, j, cn, ji) in enumerate(chunks):
                    if ji == 0 and gi_ + 1 < ng:
                        ld(gi_ + 1)
                    if i + 1 < len(chunks):
                        nx = chunks[i + 1]
                        chunk_s1(nx[1], nx[2], nx[3])
                    chunk_rest(b, j, cn)
                    if ji == 3:
                        store(gi_)
                P.barrier()

        def phase_x1(l, S, si):
            with ExitStack() as st:
                Wkv, wkk = load_w(st, "wKV", w_kv[l], 128, 8, 1536, 0)
                Wq, wqk = load_w(st, "wXQ", w_in[l], 128, 8, 768, C_XQ)
                xkT = sb(st, "xkT", (96, 8, 256), BF16)
                xv = sb(st, "xv", (128, 2, 768), BF16)
                with ExitStack() as st2:
                    mt = sb(st2, "xmt", (128, 2, D), F32)
                    msq = sb(st2, "xmsq", (128, 2, D), BF16)
                    mss = sb(st2, "xmss", (128, 2), F32)
                    memT = sb(st2, "xmemT", (128, 8, 256), BF16)
                    ksq = sb(st2, "xksq", (96, 2, 256), BF16)
                    krs = sb(st2, "xkrs", (96, 256), F32)
                    P.dma("sp", mt[:], mems[si].rearrange("(j p) f -> p j f", p=128), writes=["mt"])
                    for j in range(2):
                        P.op("act", lambda e, j=j: e.activation(out=msq[:, j, :], in_=mt[:, j, :], func=AF.Square, accum_out=mss[:, j:j + 1]),
                             reads=["mt"], writes=["mss", ("msq", j)])
                    P.op("act", lambda e: e.activation(out=mss[:], in_=mss[:], func=AF.Sqrt, scale=1.0 / D, bias=EPSC),
                         reads=["mss", "eps"], writes=["mss"])
                    P.op("dve", lambda e: e.reciprocal(mss[:], mss[:]), reads=["mss"], writes=["mss"])
                    for j in range(2):
                        P.op("dve", lambda e, j=j: e.tensor_scalar(mt[:, j, :], mt[:, j, :], mss[:, j:j + 1], None, ALU.mult),
                             reads=["mt", "mss"], writes=[("mtn", j)])
                    for c in range(8):
                        bk = bank()
                        for j in range(2):
                            P.op("pe", lambda e, bk=bk, j=j, c=c: e.transpose(PS[bk][:, j * 128:(j + 1) * 128], mt[:, j, c * 128:(c + 1) * 128], IDF),
                                 reads=[("mtn", j), "cD"], writes=[("ps", bk)], sig=(j == 1))
                        P.op("dve", lambda e, bk=bk, c=c: e.tensor_scalar(memT[:, c, :], PS[bk][:, 0:256], cA[:, l, 16 + c:17 + c], None, ALU.mult),
                             reads=[("ps", bk), "cA"], writes=[("memT", c)])
                    mTk = [("memT", c) for c in range(8)]
                    for h in range(4):
                        bq = []
                        for cc in range(2):
                            bk = bank()
                            bq.append(bk)
                            for k in range(8):
                                P.op("pe", lambda e, bk=bk, k=k, h=h, cc=cc: e.matmul(
                                    PS[bk][0:96, 0:256], Wkv[:, k, (2 * h + cc) * 96:(2 * h + cc + 1) * 96], memT[:, k, :],
                                    start=(k == 0), stop=(k == 7)), reads=mTk + wkk, writes=[("ps", bk)], sig=(k == 7))
                            P.op("act", lambda e, bk=bk, cc=cc: e.activation(out=ksq[:, cc, :], in_=PS[bk][0:96, 0:256], func=AF.Square),
                                 reads=[("ps", bk)], writes=[("ksq", cc)])
                        bk2 = bank()
                        for cc in range(2):
                            P.op("pe", lambda e, bk2=bk2, cc=cc: e.matmul(PS[bk2][0:96, 0:256], ONEB[0:96, 0:96], ksq[:, cc, :], start=(cc == 0), stop=(cc == 1)),
                                 reads=[("ksq", cc), "cDb"], writes=[("ps", bk2)], sig=(cc == 1))
                        P.op("act", lambda e, bk2=bk2: e.activation(out=krs[:], in_=PS[bk2][0:96, 0:256], func=AF.Sqrt, scale=1.0 / 192, bias=EPSC[0:96, :]),
                             reads=[("ps", bk2), "eps"], writes=["krs"])
                        P.op("dve", lambda e: e.reciprocal(krs[:], krs[:]), reads=["krs"], writes=["krs"])
                        for cc in range(2):
                            P.op("dve", lambda e, cc=cc, h=h, bq=bq: e.scalar_tensor_tensor(
                                xkT[:, 2 * h + cc, :], PS[bq[cc]][0:96, 0:256], cB[0:96, l, 2 + cc:3 + cc], krs[:], ALU.mult, ALU.mult),
                                reads=[("ps", bq[cc]), "krs", "cB"], writes=[("xkT", h)])
                    for mc in range(2):
                        for gi, (n0, nn) in enumerate(((0, 512), (512, 256))):
                            bk = bank()
                            for k in range(8):
                                P.op("pe", lambda e, bk=bk, k=k, mc=mc, n0=n0, nn=nn: e.matmul(
                                    PS[bk][:, 0:nn], memT[:, k, mc * 128:(mc + 1) * 128], Wkv[:, k, 768 + n0:768 + n0 + nn],
                                    start=(k == 0), stop=(k == 7)), reads=mTk + wkk, writes=[("ps", bk)], sig=(k == 7))
                            P.op("act", lambda e, bk=bk, mc=mc, n0=n0, nn=nn: e.copy(xv[:, mc, n0:n0 + nn], PS[bk][:, 0:nn]),
                                 reads=[("ps", bk)], writes=[("xv", mc, gi)])
                    P.barrier()
                hi = [sb(st, "xhi%d" % i, (128, 8, 512), BF16) for i in range(2)]
                qsq = sb(st, "xqsq", (96, 2, 512), BF16)
                qrs = sb(st, "xqrs", (96, 512), F32)
                xq = [sb(st, "xxq%d" % i, (96, 2, 512), BF16) for i in range(2)]
                pT = [sb(st, "xpT%d" % i, (128, 2, 512), BF16) for i in range(2)]
                rD = [sb(st, "xrD%d" % i, (96, 512), F32) for i in range(2)]
                xo = [sb(st, "xxo%d" % i, (96, 8, 512), BF16) for i in range(2)]
                def ld(it):
                    P.dma("sp", hi[it % 2][:], hT[:, PAD + it * 512:PAD + it * 512 + 512].rearrange("(c p) t -> p c t", p=128), writes=[("hi", it % 2)])

                qsq2 = [qsq, sb(st, "xqsq2", (96, 2, 512), BF16)]
                nt = S // 512
                heads = [(it, h) for it in range(nt) for h in range(4)]
                H = {}

                def s1(i):
                    it, h = heads[i]
                    b = it % 2
                    qs_ = qsq2[i % 2]
                    bq = []
                    for cc in range(2):
                        bk = bank()
                        pinned.add(bk)
                        bq.append(bk)
                        for k in range(8):
                            P.op("pe", lambda e, bk=bk, k=k, cc=cc: e.matmul(
                                PS[bk][0:96, :], Wq[:, k, (2 * h + cc) * 96:(2 * h + cc + 1) * 96], hi[b][:, k, :],
                                start=(k == 0), stop=(k == 7)), reads=[("hi", b)] + wqk, writes=[("ps", bk)], sig=(k == 7))
                        P.op("act", lambda e, bk=bk, cc=cc: e.activation(out=qs_[:, cc, :], in_=PS[bk][0:96, :], func=AF.Square),
                             reads=[("ps", bk)], writes=[("qsq", i % 2, cc)])
                    H[i] = bq

                def s234(i):
                    it, h = heads[i]
                    b = it % 2
                    r = i % 2
                    qs_ = qsq2[i % 2]
                    bq = H.pop(i)
                    bk2 = bank()
                    for cc in range(2):
                        P.op("pe", lambda e, cc=cc: e.matmul(PS[bk2][0:96, :], ONEB[0:96, 0:96], qs_[:, cc, :], start=(cc == 0), stop=(cc == 1)),
                             reads=[("qsq", i % 2, cc), "cDb"], writes=[("ps", bk2)], sig=(cc == 1))
                    P.op("act", lambda e: e.activation(out=qrs[:], in_=PS[bk2][0:96, :], func=AF.Ln, scale=1.0 / 192, bias=EPSC[0:96, :]),
                         reads=[("ps", bk2), "eps"], writes=["qrs"])
                    P.op("act", lambda e: e.activation(out=qrs[:], in_=qrs[:], func=AF.Exp, scale=-0.5), reads=["qrs"], writes=["qrs"])
                    for cc in range(2):
                        P.op("dve", lambda e, cc=cc: e.scalar_tensor_tensor(
                            xq[r][:, cc, :], PS[bq[cc]][0:96, :], cB[0:96, l, cc:cc + 1], qrs[:], ALU.mult, ALU.mult),
                            reads=[("ps", bq[cc]), "qrs", "cB"], writes=[("xq", r)])
                    for bk in bq:
                        pinned.discard(bk)
                    if i + 1 < len(heads):
                        if heads[i + 1][1] == 0 and heads[i + 1][0] + 1 < nt:
                            ld(heads[i + 1][0] + 1)
                        s1(i + 1)
                    for mc in range(2):
                        bk = bank()
                        for cc in range(2):
                            P.op("pe", lambda e, bk=bk, cc=cc, mc=mc: e.matmul(
                                PS[bk][:, :], xkT[:, 2 * h + cc, mc * 128:(mc + 1) * 128], xq[r][:, cc, :], start=(cc == 0), stop=(cc == 1)),
                                reads=[("xq", r), ("xkT", h)], writes=[("ps", bk)], sig=(cc == 1))
                        P.op("act", lambda e, bk=bk, mc=mc: e.activation(out=pT[r][:, mc, :], in_=PS[bk][:, :], func=AF.Exp, scale=192.0 ** -0.5),
                             reads=[("ps", bk)], writes=[("pT", r, mc)])
                    bo = []
                    for cc in range(2):
                        bk = bank()
                        bo.append(bk)
                        for mc in range(2):
                            P.op("pe", lambda e, bk=bk, cc=cc, mc=mc: e.matmul(
                                PS[bk][0:96, :], xv[:, mc, h * 192 + cc * 96:h * 192 + (cc + 1) * 96], pT[r][:, mc, :], start=(mc == 0), stop=(mc == 1)),
                                reads=[("pT", r, mc), ("xv", mc, 0), ("xv", mc, 1)], writes=[("ps", bk)], sig=(mc == 1))
                    bkD = bank()
                    for mc in range(2):
                        P.op("pe", lambda e, mc=mc: e.matmul(PS[bkD][0:96, :], ONEB[:, 0:96], pT[r][:, mc, :], start=(mc == 0), stop=(mc == 1)),
                             reads=[("pT", r, mc), "cDb"], writes=[("ps", bkD)], sig=(mc == 1))
                    P.op("act", lambda e: e.activation(out=rD[r][:], in_=PS[bkD][0:96, :], func=AF.Ln), reads=[("ps", bkD)], writes=[("rD", r)])
                    P.op("act", lambda e: e.activation(out=rD[r][:], in_=rD[r][:], func=AF.Exp, scale=-1.0), reads=[("rD", r)], writes=[("rD", r)])
                    for cc in range(2):
                        P.op("dve", lambda e, cc=cc: e.tensor_tensor(xo[b][:, 2 * h + cc, :], PS[bo[cc]][0:96, :], rD[r][:], ALU.mult),
                             reads=[("ps", bo[cc]), ("rD", r)], writes=[("xo", b, h, cc)])
                    if h == 3:
                        t0 = it * 512
                        P.dma("sp", xoT[:, t0:t0 + 512].rearrange("(c p) t -> p c t", p=96), xo[b][:],
                              reads=[("xo", b, hh, cc) for hh in range(4) for cc in range(2)])

                ld(0)
                if nt > 1:
                    ld(1)
                s1(0)
                for i in range(len(heads)):
                    s234(i)
                P.barrier()

        def phase_g1(l, S):
            T = 256
            with ExitStack() as st:
                Wg, wgk = load_w(st, "wG", w_in[l], 128, 8, 3072, C_G)
                Wa, wak = load_w(st, "wBa", w_br[l][0:768, :], 128, 6, D)
                Wm, wmk = load_w(st, "wBm", w_br[l][768:1536, :], 128, 6, D)
                Wx, wxk = load_w(st, "wBx", w_br[l][1536:2304, :], 96, 8, D)
                Wo, wok = load_w(st, "wO", w_out[l], 128, 8, D)
                hi = [sb(st, "ghi%d" % i, (128, 8, T), BF16) for i in range(2)]
                ai = [sb(st, "gai%d" % i, (128, 6, T), BF16) for i in range(2)]
                mi = [sb(st, "gmi%d" % i, (128, 6, T), BF16) for i in range(2)]
                xi_ = [sb(st, "gxi%d" % i, (96, 8, T), BF16) for i in range(2)]
                xt = [sb(st, "gxt%d" % i, (128, 8, T), F32) for i in range(3)]
                pend = [None]
                mg = sb(st, "gmg", (128, 8, T), BF16)
                s01 = [sb(st, "gs01%d" % i, (128, 2 * T), F32) for i in range(2)]
                s2 = [sb(st, "gs2%d" % i, (128, T), F32) for i in range(2)]
                mm = [sb(st, "gmm%d" % i, (128, 3, T), F32) for i in range(2)]
                nb = norm_bufs(st, T)
                def ld(it):
                    b = it % 2
                    t0 = it * T
                    P.dma("sp", hi[b][:], hT[:, PAD + t0:PAD + t0 + T].rearrange("(c p) t -> p c t", p=128), writes=[("hi", b)])
                    P.dma("sp", ai[b][:], attT[:, t0:t0 + T].rearrange("(c p) t -> p c t", p=128), writes=[("ai", b)])
                    P.dma("sp", mi[b][:], hmT[:, t0:t0 + T].rearrange("(c p) t -> p c t", p=128), writes=[("mi", b)])
                    P.dma("sp", xi_[b][:], xoT[:, t0:t0 + T].rearrange("(c p) t -> p c t", p=96), writes=[("xi", b)])
                    P.dma("sp", xt[it % 3][:], xT[:, t0:t0 + T].rearrange("(c p) t -> p c t", p=128), writes=[("xt", it % 3)])

                def body(it):
                    b = it % 2
                    x3 = it % 3
                    t0 = it * T
                    for oc in range(8):
                        u = oc % 2
                        if oc == 1 and pend[0] is not None:
                            pend[0]()
                            pend[0] = None
                        ocs = slice(oc * 128, (oc + 1) * 128)
                        bA, bB, bC, bD = bank(), bank(), bank(), bank()
                        slots = [(bA, 0), (bA, T), (bB, 0)]
                        for br in range(3):
                            bk, c0 = slots[br]
                            for k in range(8):
                                P.op("pe", lambda e, bk=bk, c0=c0, k=k, br=br, b=b, ocs=ocs: e.matmul(
                                    PS[bk][:, c0:c0 + T], Wg[:, k, br * D + ocs.start:br * D + ocs.stop], hi[b][:, k, :],
                                    start=(k == 0), stop=(k == 7)), reads=[("hi", b)] + wgk, writes=[("ps", bk)], sig=(k == 7))
                        for k in range(6):
                            P.op("pe", lambda e, k=k, b=b, ocs=ocs, bC=bC: e.matmul(PS[bC][:, 0:T], Wa[:, k, ocs], ai[b][:, k, :], start=(k == 0), stop=(k == 5)),
                                 reads=[("ai", b)] + wak, writes=[("ps", bC)], sig=(k == 5))
                        for k in range(6):
                            P.op("pe", lambda e, k=k, b=b, ocs=ocs, bC=bC: e.matmul(PS[bC][:, T:2 * T], Wm[:, k, ocs], mi[b][:, k, :], start=(k == 0), stop=(k == 5)),
                                 reads=[("mi", b)] + wmk, writes=[("ps", bC)], sig=(k == 5))
                        for k in range(8):
                            P.op("pe", lambda e, k=k, b=b, ocs=ocs, bD=bD: e.matmul(PS[bD][0:128, 0:T], Wx[:, k, ocs], xi_[b][:, k, :], start=(k == 0), stop=(k == 7)),
                                 reads=[("xi", b)] + wxk, writes=[("ps", bD)], sig=(k == 7))
                        P.op("act", lambda e, u=u, bA=bA: e.activation(out=s01[u][:], in_=PS[bA][:, :], func=AF.Sigmoid),
                             reads=[("ps", bA)], writes=[("s01", u)])
                        P.op("act", lambda e, u=u, bB=bB: e.activation(out=s2[u][:], in_=PS[bB][:, 0:T], func=AF.Sigmoid),
                             reads=[("ps", bB)], writes=[("s2", u)])
                        P.op("dve", lambda e, u=u, bC=bC: e.tensor_tensor(mm[u][:, 0, :], PS[bC][:, 0:T], s01[u][:, 0:T], ALU.mult),
                             reads=[("ps", bC), ("s01", u)], writes=[("mm", u, 0)])
                        P.op("dve", lambda e, u=u, bC=bC: e.tensor_tensor(mm[u][:, 1, :], PS[bC][:, T:2 * T], s01[u][:, T:2 * T], ALU.mult),
                             reads=[("ps", bC), ("s01", u)], writes=[("mm", u, 1)])
                        P.op("dve", lambda e, u=u, bD=bD: e.tensor_tensor(mm[u][:, 2, :], PS[bD][:, 0:T], s2[u][:], ALU.mult),
                             reads=[("ps", bD), ("s2", u)], writes=[("mm", u, 2)])
                        P.op("dve", lambda e, u=u: e.tensor_tensor(mm[u][:, 0, :], mm[u][:, 0, :], mm[u][:, 1, :], ALU.add),
                             reads=[("mm", u, 0), ("mm", u, 1)], writes=[("mm", u, 0)])
                        P.op("dve", lambda e, u=u, oc=oc: e.tensor_tensor(mg[:, oc, :], mm[u][:, 0, :], mm[u][:, 2, :], ALU.add),
                             reads=[("mm", u, 0), ("mm", u, 2)], writes=[("mg", oc)])
                    for oc in range(8):
                        bk = bank()
                        for k in range(8):
                            P.op("pe", lambda e, bk=bk, k=k, oc=oc: e.matmul(PS[bk][:, 0:T], Wo[:, k, oc * 128:(oc + 1) * 128], mg[:, k, :], start=(k == 0), stop=(k == 7)),
                                 reads=[("mg", kk) for kk in range(8)] + wok, writes=[("ps", bk)], sig=(k == 7))
                        P.op("dve", lambda e, bk=bk, oc=oc: e.tensor_tensor(xt[x3][:, oc, :], PS[bk][:, 0:T], xt[x3][:, oc, :], ALU.add),
                             reads=[("ps", bk), ("xt", x3)], writes=[("xn", x3, oc)])
                    xk = [("xn", x3, oc) for oc in range(8)] + [("xt", x3)]
                    P.dma("sp", xT[:, t0:t0 + T].rearrange("(c p) t -> p c t", p=128), xt[x3][:], reads=xk)
                    pend[0] = lambda: fused_norm(nb, xt[x3][:], xk, T, l, 8, t0, b)
                run_tiles(S // T, ld, body)
                if pend[0] is not None:
                    pend[0]()
                P.barrier()

        def phase_f1(l, S):
            with ExitStack() as st:
                W, wk = load_w(st, "wUp", w_up[l], 128, 8, 2 * FFN, 0)
                hi = [sb(st, "fhi%d" % i, (128, 8, 512), BF16) for i in range(2)]
                ta = [sb(st, "fta%d" % i, (128, 512), F32) for i in range(3)]
                tv = [sb(st, "ftv%d" % i, (128, 512), F32) for i in range(3)]
                go = [sb(st, "fgo%d" % i, (128, 22, 512), BF16) for i in range(2)]
                tl = [(t, min(510, S - t)) for t in range(0, S, 510)]

                def ld(it):
                    t0, ntok = tl[it]
                    P.dma("sp", hi[it % 2][:, :, 0:ntok + 2],
                          hT[:, PAD + t0 - 1:PAD + t0 + 1 + ntok].rearrange("(c p) t -> p c t", p=128), writes=[("hi", it % 2)])

                def body(it):
                    t0, ntok = tl[it]
                    ncol = ntok + 2
                    b = it % 2
                    pend = None
                    for fc in range(22):
                        u = fc % 3
                        for part, (tt_, key) in enumerate(((ta[u], "ta"), (tv[u], "tv"))):
                            ch = part * 22 + fc
                            bk = bank()
                            for k in range(8):
                                P.op("pe", lambda e, bk=bk, k=k, ch=ch: e.matmul(
                                    PS[bk][:, 0:ncol], W[:, k, ch * 128:(ch + 1) * 128], hi[b][:, k, 0:ncol], start=(k == 0), stop=(k == 7)),
                                    reads=[("hi", b)] + wk, writes=[("ps", bk)], sig=(k == 7))
                            w0, w1, w2 = (cA[:, l, 26 + jj * 44 + ch:26 + jj * 44 + ch + 1] for jj in range(3))
                            bb = cA[:, l, 158 + ch:158 + ch + 1]
                            P.op("act", lambda e, bk=bk, tt_=tt_, w1=w1, bb=bb: e.activation(
                                out=tt_[:, 0:ntok], in_=PS[bk][:, 1:1 + ntok], func=AF.Identity, scale=w1, bias=bb),
                                reads=[("ps", bk), "cA"], writes=[(key, u)])
                            if part == 1 and pend is not None:
                                pend()
                            P.op("dve", lambda e, bk=bk, tt_=tt_, w0=w0: e.scalar_tensor_tensor(
                                tt_[:, 0:ntok], PS[bk][:, 0:ntok], w0, tt_[:, 0:ntok], ALU.mult, ALU.add),
                                reads=[("ps", bk), (key, u), "cA"], writes=[(key, u)])
                            P.op("dve", lambda e, bk=bk, tt_=tt_, w2=w2: e.scalar_tensor_tensor(
                                tt_[:, 0:ntok], PS[bk][:, 2:2 + ntok], w2, tt_[:, 0:ntok], ALU.mult, ALU.add),
                                reads=[("ps", bk), (key, u), "cA"], writes=[(key, u)])

                        def pend(u=u, fc=fc):
                            P.op("act", lambda e: e.activation(out=ta[u][:, 0:ntok], in_=ta[u][:, 0:ntok], func=AF.Gelu_apprx_tanh),
                                 reads=[("ta", u)], writes=[("ta", u)])
                            P.op("pool", lambda e: e.tensor_tensor(go[b][:, fc, 0:ntok], ta[u][:, 0:ntok], tv[u][:, 0:ntok], ALU.mult),
                                 reads=[("ta", u), ("tv", u)], writes=[("go", b, fc)])
                    pend()
                    P.dma("sp", gT[:, t0:t0 + ntok].rearrange("(c p) t -> p c t", p=128), go[b][:, :, 0:ntok],
                          reads=[("go", b, fc) for fc in range(22)])
                run_tiles(len(tl), ld, body)
                P.barrier()

        def phase_f2(l, S, fuse_next):
            with ExitStack() as st:
                W, wk = load_w(st, "wDn", w_dn[l], 128, 22, D, 0)
                gi = [sb(st, "dgi%d" % i, (128, 22, 512), BF16) for i in range(2)]
                xt = [sb(st, "dxt%d" % i, (128, 8, 512), F32) for i in range(3)]
                nb = norm_bufs(st, 512) if fuse_next else None
                pend = [None]

                def ld(it):
                    b = it % 2
                    x3 = it % 3
                    t0 = it * 512
                    P.dma("sp", gi[b][:], gT[:, t0:t0 + 512].rearrange("(c p) t -> p c t", p=128), writes=[("gi", b)])
                    P.dma("sp", xt[x3][:], xT[:, t0:t0 + 512].rearrange("(c p) t -> p c t", p=128), writes=[("xt", x3)])

                def body(it):
                    b = it % 2
                    x3 = it % 3
                    t0 = it * 512
                    for oc in range(8):
                        bk = bank()
                        for k in range(22):
                            P.op("pe", lambda e, bk=bk, k=k, oc=oc: e.matmul(PS[bk][:, :], W[:, k, oc * 128:(oc + 1) * 128], gi[b][:, k, :], start=(k == 0), stop=(k == 21)),
                                 reads=[("gi", b)] + wk, writes=[("ps", bk)], sig=(k == 21))
                        P.op("dve", lambda e, bk=bk, oc=oc: e.tensor_tensor(xt[x3][:, oc, :], PS[bk][:, :], xt[x3][:, oc, :], ALU.add),
                             reads=[("ps", bk), ("xt", x3)], writes=[("xn", x3, oc)])
                        if oc == 1 and pend[0] is not None:
                            pend[0]()
                            pend[0] = None
                    xk = [("xn", x3, oc) for oc in range(8)] + [("xt", x3)]
                    P.dma("sp", xT[:, t0:t0 + 512].rearrange("(c p) t -> p c t", p=128), xt[x3][:], reads=xk)
                    if fuse_next:
                        pend[0] = lambda: fused_norm(nb, xt[x3][:], xk, 512, l + 1, 0, t0, b)
                run_tiles(S // 512, ld, body)
                if pend[0] is not None:
                    pend[0]()
                P.barrier()

        phases = []
        for si, S in enumerate(seq_lens):
            phases.append(("in", lambda si=si, S=S: phase_in(si, S)))
            for l in range(depth):
                if l == 0:
                    phases.append(("n1", lambda l=l, S=S: phase_norm(l, S, 0)))
                phases.append(("a1", lambda l=l, S=S: phase_a1(l, S)))
                phases.append(("a2", lambda l=l, S=S: phase_a2(l, S)))
                phases.append(("m1a", lambda l=l, S=S: phase_m1a(l, S)))
                phases.append(("m1b", lambda l=l, S=S: phase_m1b(l, S)))
                phases.append(("m2", lambda l=l, S=S: phase_scan(l, S, False)))
                phases.append(("m3", lambda l=l, S=S: phase_scan(l, S, True)))
                phases.append(("x1", lambda l=l, S=S, si=si: phase_x1(l, S, si)))
                phases.append(("g1", lambda l=l, S=S: phase_g1(l, S)))
                phases.append(("f1", lambda l=l, S=S: phase_f1(l, S)))
                phases.append(("f2", lambda l=l, S=S: phase_f2(l, S, l + 1 < depth)))
            phases.append(("out", lambda si=si, S=S: phase_out(si, S)))
        for name, fn in phases:
            fn()
            if upto is not None and name == upto:
                break
        P.barrier()
    return nc


def kernel(**inputs):
    f = lambda k: np.ascontiguousarray(np.asarray(inputs[k], dtype=np.float32))
    xp, xsm = f("x_prompt"), f("x_sample")
    mp, ms = f("mem_prompt"), f("mem_sample")
    nc = build([xp.shape[1], xsm.shape[1]], depth=NL)
    cA, cB, cC = pack_params(inputs)
    eb, cd = host_consts()
    common = {"w_in": f("w_in"), "w_mem_kv": f("w_mem_kv"),
              "w_branch": f("w_branch").reshape(NL, 2304, D), "w_out": f("w_out"),
              "w_up": f("w_up"), "w_down": f("w_down"),
              "cA": cA, "cB": cB, "cC": cC, "cEB": eb, "cD": cd, "cM": pack_cm(inputs)}
    n = 8
    nsm = xsm.shape[0]
    in_maps = []
    for c in range(n):
        m = dict(common)
        m["x0"] = xp[c]
        m["mem0"] = mp[c]
        m["x1"] = xsm[c % nsm]
        m["mem1"] = ms[c % nsm]
        in_maps.append(m)
    res = run_bass_kernel_spmd(nc, in_maps, core_ids=list(range(n)))
    y_prompt = np.stack([np.asarray(res.results[c]["y0"], dtype=np.float32) for c in range(n)])
    y_sample = np.stack([np.asarray(res.results[c]["y1"], dtype=np.float32) for c in range(nsm)])
    return (y_prompt, y_sample)
```

```python
import bisect
import math
import os
from contextlib import ExitStack

import numpy as np
import concourse.bass as bass
import concourse.mybir as mybir
from concourse.bass_utils import run_bass_kernel_spmd

F32 = mybir.dt.float32
BF16 = mybir.dt.bfloat16
AF = mybir.ActivationFunctionType
ALU = mybir.AluOpType
AX = mybir.AxisListType

D = 1024
NL = 4
NMEM = 256
IN_DIM = 9232
FFN = 2816
EPS = 1e-6
C_AQ, C_AK, C_AV, C_MQ, C_MK, C_MV, C_MO, C_MIF, C_XQ, C_G = 0, 768, 1536, 2304, 3072, 3840, 4608, 5376, 5392, 6160
PAD = 64
DILS = (1, 4, 16)


class _Eng:
    def __init__(self, name, obj, sem):
        self.name, self.obj, self.sem = name, obj, sem
        self.seq = 0
        self.sig_seqs = []
        self.waited = {}


class Prog:
    def __init__(self, nc, es, n_sp=6, n_pool=4):
        self.nc = nc
        self.E = {}
        for name, obj in (("pe", nc.tensor), ("act", nc.scalar), ("dve", nc.vector),
                          ("pool", nc.gpsimd), ("sp", nc.sync)):
            sem = es.enter_context(nc.semaphore("s_" + name))
            self.E[name] = _Eng(name, obj, sem)
        self.dsems = []
        self.dq = {}
        for q, n in (("sp", n_sp), ("pool", n_pool)):
            lst = []
            for i in range(n):
                self.dsems.append(es.enter_context(nc.semaphore("d_%s%d" % (q, i))))
                lst.append([len(self.dsems) - 1, 0])
            self.dq[q] = lst
        self.dq_next = {"sp": 0, "pool": 0}
        self.lastw = {}
        self.readers = {}
        self.n_ops = 0

    def _wait(self, e, tok):
        if tok[0] == "c":
            src = self.E[tok[1]]
            i = bisect.bisect_left(src.sig_seqs, tok[2])
            assert i < len(src.sig_seqs), "dependency on an op with no later signal on %s" % src.name
            cnt, key, sem = i + 1, src.name, src.sem
        else:
            key, sem, cnt = ("d", tok[1]), self.dsems[tok[1]], tok[2]
        if e.waited.get(key, 0) >= cnt:
            return
        e.obj.wait_ge(sem, cnt)
        e.waited[key] = cnt

    def _deps(self, reads, writes):
        deps = []
        for r in reads:
            t = self.lastw.get(r)
            if t is not None:
                deps.append((t, "raw"))
        for w in writes:
            t = self.lastw.get(w)
            if t is not None:
                deps.append((t, "waw"))
            rd = self.readers.get(w)
            if rd:
                for t in rd.values():
                    deps.append((t, "war"))
        return deps

    def _commit(self, tok, reads, writes):
        key = tok[1] if tok[0] == "c" else tok
        for r in reads:
            self.readers.setdefault(r, {})[key] = tok
        for w in writes:
            self.lastw[w] = tok
            self.readers[w] = {}

    def op(self, eng, fn, reads=(), writes=(), sig=True):
        e = self.E[eng]
        for tok, kind in self._deps(reads, writes):
            if tok[0] == "c" and tok[1] == eng:
                if eng == "pe" or (kind != "raw" and eng != "pool"):
                    continue
            self._wait(e, tok)
        ins = fn(e.obj)
        e.seq += 1
        tok = ("c", eng, e.seq)
        if sig:
            ins.then_inc(e.sem, 1)
            e.sig_seqs.append(e.seq)
        self._commit(tok, reads, writes)
        self.n_ops += 1
        return tok

    def dma(self, q, out, in_, reads=(), writes=()):
        e = self.E[q]
        for tok, kind in self._deps(reads, writes):
            self._wait(e, tok)
        pool = self.dq[q]
        i = self.dq_next[q]
        self.dq_next[q] = (i + 1) % len(pool)
        ent = pool[i]
        if ent[1] > 0:
            self._wait(e, ("d", ent[0], ent[1]))
        ent[1] += 16
        e.obj.dma_start(out=out, in_=in_).then_inc(self.dsems[ent[0]], 16)
        tok = ("d", ent[0], ent[1])
        self._commit(tok, reads, writes)
        self.n_ops += 1
        return tok

    def barrier(self):
        for e in self.E.values():
            for src in self.E.values():
                if (src is e and e.name != "pool") or not src.sig_seqs:
                    continue
                cnt = len(src.sig_seqs)
                if e.waited.get(src.name, 0) < cnt:
                    e.obj.wait_ge(src.sem, cnt)
                    e.waited[src.name] = cnt
            for lst in self.dq.values():
                for semi, val in lst:
                    if val > 0 and e.waited.get(("d", semi), 0) < val:
                        e.obj.wait_ge(self.dsems[semi], val)
                        e.waited[("d", semi)] = val
        self.lastw.clear()
        self.readers.clear()


def ssl(start, n, step):
    return slice(start, start + (n - 1) * step + 1, step)


def alibi_slopes():
    return [2.0 ** (-8.0 * (h + 1) / 12.0) for h in range(12)]


def host_consts():
    kk = np.arange(128)[:, None]
    qq = np.arange(128)[None, :]
    eb = np.zeros((128, 12, 3, 128), np.float32)
    sl = alibi_slopes()
    for h in range(12):
        dil = DILS[h // 4]
        for j in range(3):
            delta = 128 * (j - 1) + kk - qq
            val = np.exp(-sl[h] * dil * np.abs(delta).astype(np.float64))
            eb[:, h, j, :] = np.where(np.abs(delta) <= 64, val, 0.0)
    cd = np.zeros((128, 5, 128), np.float32)
    cd[:, 0, :] = (kk <= qq)
    cd[:, 1, :] = (kk >= qq)
    cd[:, 2, :] = np.eye(128)
    cd[:, 3, :] = 1.0
    bd = np.zeros((128, 128), np.float32)
    bd[:64, :64] = 1.0
    bd[64:, 64:] = 1.0
    cd[:, 4, :] = bd
    return eb.reshape(128, -1), cd.reshape(128, -1)


def pack_params(inp):
    L = NL
    f = lambda k: np.asarray(inp[k], np.float32)
    cA = np.zeros((128, L, 8 * 3 + 2 + 3 * 44 + 44), np.float32)
    for l in range(L):
        o = 0
        for key in ("norm_mix_g", "norm_ffn_g", "norm_mem_g"):
            cA[:, l, o:o + 8] = f(key)[l].reshape(8, 128).T
            o += 8
        cA[:, l, o] = np.tile(f("att_q_g")[l], 2); o += 1
        cA[:, l, o] = np.tile(f("att_k_g")[l], 2); o += 1
        cA[:, l, o:o + 132] = f("ffn_conv_w")[l].reshape(3, 44, 128).transpose(2, 0, 1).reshape(128, 132); o += 132
        cA[:, l, o:o + 44] = f("ffn_conv_b")[l].reshape(44, 128).T; o += 44
    cB = np.zeros((128, L, 2 + 2 + 48 + 16), np.float32)
    for l in range(L):
        cB[:96, l, 0:2] = f("xatt_q_g")[l].reshape(2, 96).T
        cB[:96, l, 2:4] = f("xatt_k_g")[l].reshape(2, 96).T
        cB[:96, l, 4:52] = f("mlstm_conv_w")[l].reshape(3, 16, 96).transpose(2, 0, 1).reshape(96, 48)
        cB[:96, l, 52:68] = f("mlstm_conv_b")[l].reshape(16, 96).T
    cC = np.zeros((128, L, 16 + 768), np.float32)
    for l in range(L):
        cC[:, l, 0:16] = f("mlstm_gate_b")[l][None, :]
        cC[:, l, 16:] = f("mlstm_h_g")[l][None, :]
    return cA.reshape(128, -1), cB.reshape(128, -1), cC.reshape(128, -1)


def pack_cm(inp):
    f = lambda k: np.asarray(inp[k], np.float32)
    cM = np.zeros((128, NL, 48), np.float32)
    for l in range(NL):
        cM[:, l, 0:36] = f("mlstm_conv_w")[l].reshape(3, 12, 128).transpose(2, 0, 1).reshape(128, 36)
        cM[:, l, 36:48] = f("mlstm_conv_b")[l].reshape(12, 128).T
    return cM.reshape(128, -1)


NA = 8 * 3 + 2 + 132 + 44
NB = 68
NCC = 16 + 768


def build(seq_lens, depth=NL, dbg=(), upto=None):
    nc = bass.Bass("TRN2", target_bir_lowering=False)
    Smax = max(seq_lens)
    nseq = len(seq_lens)
    dt_in = lambda name, shape: nc.dram_tensor(name, list(shape), F32, kind="ExternalInput").ap()
    xs = [dt_in("x%d" % i, (seq_lens[i], D)) for i in range(nseq)]
    mems = [dt_in("mem%d" % i, (NMEM, D)) for i in range(nseq)]
    ys = [nc.dram_tensor("y%d" % i, [seq_lens[i], D], F32, kind="ExternalOutput").ap() for i in range(nseq)]
    w_in = dt_in("w_in", (NL, D, IN_DIM))
    w_kv = dt_in("w_mem_kv", (NL, D, 1536))
    w_br = dt_in("w_branch", (NL, 2304, D))
    w_out = dt_in("w_out", (NL, D, D))
    w_up = dt_in("w_up", (NL, D, 2 * FFN))
    w_dn = dt_in("w_down", (NL, FFN, D))
    cA_d = dt_in("cA", (128, NL * NA))
    cB_d = dt_in("cB", (128, NL * NB))
    cC_d = dt_in("cC", (128, NL * NCC))
    eb_d = dt_in("cEB", (128, 12 * 3 * 128))
    cM_d = dt_in("cM", (128, NL * 48))
    cd_d = dt_in("cD", (128, 5 * 128))

    def scratch(name, shape, dt):
        kind = "ExternalOutput" if name in dbg else "Internal"
        return nc.dram_tensor(name, list(shape), dt, kind=kind).ap()

    SP = Smax + 2 * PAD
    xT = scratch("xT", (D, Smax), F32)
    hT = scratch("hT", (D, SP), BF16)
    qaT = scratch("qaT", (768, Smax), BF16)
    kaT = scratch("kaT", (768, Smax), BF16)
    va = scratch("va", (Smax, 768), BF16)
    attT = scratch("attT", (768, Smax), BF16)
    mqT = scratch("mqT", (768, Smax), BF16)
    mkT = scratch("mkT", (768, Smax), BF16)
    mv = scratch("mv", (Smax, 768), BF16)
    mo = scratch("mo", (Smax, 768), F32)
    gts = scratch("gts", (Smax, 16), F32)
    hb = scratch("hb", (Smax, 768), F32)
    hmT = scratch("hmT", (768, Smax), BF16)
    xoT = scratch("xoT", (768, Smax), BF16)
    gT = scratch("gT", (FFN, Smax), BF16)

    es = ExitStack()
    with es:
        P = Prog(nc, es)
        uid = [0]

        def sb(st, name, shape, dt):
            uid[0] += 1
            return st.enter_context(nc.sbuf_tensor("sb%d_%s" % (uid[0], name), list(shape), dt))
        cA = sb(es, "cA", (128, NL, NA), F32)
        cB = sb(es, "cB", (128, NL, NB), F32)
        cM = sb(es, "cM", (128, NL, 48), F32)
        cD = sb(es, "cDs", (128, 5, 128), F32)
        cDb = sb(es, "cDb", (128, 5, 128), BF16)
        zer = sb(es, "zer", (128, 8, PAD), BF16)
        NPS = 6
        PS = [es.enter_context(nc.psum_tensor("ps%d" % i, [128, 512], F32)) for i in range(NPS)]
        PSB = es.enter_context(nc.psum_tensor("psb", [128, 1024], BF16))
        PSB2 = es.enter_context(nc.psum_tensor("psb2", [128, 1024], BF16))
        TRIF, TRIB, IDF, ONEF = cD[:, 0, :], cD[:, 1, :], cD[:, 2, :], cD[:, 3, :]
        IDB, ONEB, BDB = cDb[:, 2, :], cDb[:, 3, :], cDb[:, 4, :]
        P.dma("sp", cA[:].rearrange("p l n -> p (l n)"), cA_d, writes=["cA"])
        P.dma("sp", cB[:].rearrange("p l n -> p (l n)"), cB_d, writes=["cB"])
        P.dma("sp", cM[:].rearrange("p l n -> p (l n)"), cM_d, writes=["cM"])
        P.dma("sp", cD[:].rearrange("p l n -> p (l n)"), cd_d, writes=["cD"])
        P.dma("pool", cDb[:].rearrange("p l n -> p (l n)"), cd_d, writes=["cDb"])
        P.op("dve", lambda e: e.memset(zer[:], 0.0), writes=["zer"])
        epsT = sb(es, "epsT", (128, 4), F32)
        P.op("dve", lambda e: e.memset(epsT[:, 0:1], EPS), writes=["eps"])
        P.op("dve", lambda e: e.memset(epsT[:, 1:2], 1.0), writes=["eps1"])
        P.op("dve", lambda e: e.memset(epsT[:, 2:3], math.log(192.0 ** -0.5)), writes=["eps2"])
        EPSC = epsT[:, 0:1]
        ONEC = epsT[:, 1:2]
        LNC = epsT[:, 2:3]
        P.barrier()

        psn = [0]

        def run_tiles(n, ld, body):
            if n:
                ld(0)
            for it in range(n):
                if it + 1 < n:
                    ld(it + 1)
                body(it)

        pinned = set()

        def bank():
            for _ in range(NPS):
                psn[0] = (psn[0] + 1) % NPS
                if psn[0] not in pinned:
                    return psn[0]
            raise RuntimeError("all PSUM banks pinned")

        def fused_norm(nb, xtile, xkeys, T, lg, goff, t0, b):
            nsq, nrs, nho = nb
            P.op("act", lambda e: e.activation(out=nsq[:, :, 0:T], in_=xtile, func=AF.Square),
                 reads=xkeys, writes=["nsq"])
            bk = bank()
            for c in range(8):
                P.op("pe", lambda e, c=c: e.matmul(PS[bk][:, 0:T], ONEB, nsq[:, c, 0:T], start=(c == 0), stop=(c == 7)),
                     reads=["nsq", "cDb"], writes=[("ps", bk)], sig=(c == 7))
            P.op("act", lambda e: e.activation(out=nrs[:, 0:T], in_=PS[bk][:, 0:T], func=AF.Ln, scale=1.0 / D, bias=EPSC),
                 reads=[("ps", bk)], writes=["nrs"])
            P.op("act", lambda e: e.activation(out=nrs[:, 0:T], in_=nrs[:, 0:T], func=AF.Exp, scale=-0.5), reads=["nrs"], writes=["nrs"])
            for c in range(8):
                P.op("dve", lambda e, c=c: e.scalar_tensor_tensor(
                    nho[b][:, c, 0:T], xtile[:, c, :], cA[:, lg, goff + c:goff + c + 1], nrs[:, 0:T], ALU.mult, ALU.mult),
                    reads=xkeys + ["nrs", "cA"], writes=[("nho", b, c)])
            P.dma("sp", hT[:, PAD + t0:PAD + t0 + T].rearrange("(c p) t -> p c t", p=128), nho[b][:, :, 0:T],
                  reads=[("nho", b, c) for c in range(8)])

        def norm_bufs(st, T):
            return (sb(st, "fnsq", (128, 8, T), BF16), sb(st, "fnrs", (128, T), F32),
                    [sb(st, "fnho%d" % i, (128, 8, T), BF16) for i in range(2)])

        def load_w(st, name, src, kp, nk, ncols, c0=0):
            t = sb(st, name, (kp, nk, ncols), BF16)
            v = src.rearrange("(k p) n -> p k n", p=kp)
            step = max(1, 4096 // ncols)
            for k0 in range(0, nk, step):
                k1 = min(nk, k0 + step)
                P.dma("pool", t[:, k0:k1, :], v[:, k0:k1, c0:c0 + ncols], writes=[(name, k0)])
            return t, [(name, k0) for k0 in range(0, nk, step)]

        def phase_in(si, S):
            with ExitStack() as st:
                xin = [sb(st, "xin%d" % i, (128, 4, D), F32) for i in range(2)]
                xo = [sb(st, "xo%d" % i, (128, 8, 512), F32) for i in range(2)]
                P.dma("sp", hT[:, 0:PAD].rearrange("(c p) t -> p c t", p=128), zer[:], reads=["zer"])
                P.dma("sp", hT[:, PAD + S:PAD + S + PAD].rearrange("(c p) t -> p c t", p=128), zer[:], reads=["zer"])
                nt = S // 512

                def ld(it):
                    P.dma("sp", xin[it % 2][:], xs[si][it * 512:it * 512 + 512, :].rearrange("(j p) f -> p j f", p=128),
                          writes=[("xin", it % 2)])

                def body(it):
                    b = it % 2
                    t0 = it * 512
                    for c in range(8):
                        bk = bank()
                        for j in range(4):
                            P.op("pe", lambda e, bk=bk, j=j, c=c, b=b: e.transpose(
                                PS[bk][:, j * 128:(j + 1) * 128], xin[b][:, j, c * 128:(c + 1) * 128], IDF),
                                reads=[("xin", b), "cD"], writes=[("ps", bk)], sig=(j == 3))
                        eng = "act" if c % 2 else "dve"
                        if eng == "act":
                            P.op("act", lambda e, bk=bk, c=c, b=b: e.copy(xo[b][:, c, :], PS[bk][:]),
                                 reads=[("ps", bk)], writes=[("xo", b, c)])
                        else:
                            P.op("dve", lambda e, bk=bk, c=c, b=b: e.tensor_copy(xo[b][:, c, :], PS[bk][:]),
                                 reads=[("ps", bk)], writes=[("xo", b, c)])
                    P.dma("sp", xT[:, t0:t0 + 512].rearrange("(c p) t -> p c t", p=128), xo[b][:],
                          reads=[("xo", b, c) for c in range(8)])
                run_tiles(nt, ld, body)
                P.barrier()

        def phase_out(si, S):
            with ExitStack() as st:
                xi = [sb(st, "xi%d" % i, (128, 8, 512), F32) for i in range(2)]
                yo = [sb(st, "yo%d" % i, (128, 4, D), F32) for i in range(2)]
                def ld(it):
                    P.dma("sp", xi[it % 2][:], xT[:, it * 512:it * 512 + 512].rearrange("(c p) t -> p c t", p=128),
                          writes=[("xi", it % 2)])

                def body(it):
                    b = it % 2
                    t0 = it * 512
                    for j in range(4):
                        for half in range(2):
                            bk = bank()
                            for cc in range(4):
                                c = half * 4 + cc
                                P.op("pe", lambda e, bk=bk, j=j, c=c, cc=cc, b=b: e.transpose(
                                    PS[bk][:, cc * 128:(cc + 1) * 128], xi[b][:, c, j * 128:(j + 1) * 128], IDF),
                                    reads=[("xi", b), "cD"], writes=[("ps", bk)], sig=(cc == 3))
                            if half:
                                P.op("act", lambda e, bk=bk, j=j, b=b: e.copy(yo[b][:, j, 512:1024], PS[bk][:]),
                                     reads=[("ps", bk)], writes=[("yo", b, j, 1)])
                            else:
                                P.op("dve", lambda e, bk=bk, j=j, b=b: e.tensor_copy(yo[b][:, j, 0:512], PS[bk][:]),
                                     reads=[("ps", bk)], writes=[("yo", b, j, 0)])
                    P.dma("sp", ys[si][t0:t0 + 512, :].rearrange("(j p) f -> p j f", p=128), yo[b][:],
                          reads=[("yo", b, j, h) for j in range(4) for h in range(2)])
                run_tiles(S // 512, ld, body)
                P.barrier()

        def phase_norm(l, S, goff):
            with ExitStack() as st:
                xi = [sb(st, "nxi%d" % i, (128, 8, 512), F32) for i in range(2)]
                sq = sb(st, "nsq", (128, 8, 512), BF16)
                rs = sb(st, "nrs", (128, 512), F32)
                ho = [sb(st, "nho%d" % i, (128, 8, 512), BF16) for i in range(2)]
                def ld(it):
                    P.dma("sp", xi[it % 2][:], xT[:, it * 512:it * 512 + 512].rearrange("(c p) t -> p c t", p=128),
                          writes=[("xi", it % 2)])

                def body(it):
                    b = it % 2
                    t0 = it * 512
                    P.op("act", lambda e, b=b: e.activation(out=sq[:], in_=xi[b][:], func=AF.Square),
                         reads=[("xi", b)], writes=["sq"])
                    bk = bank()
                    for c in range(8):
                        P.op("pe", lambda e, bk=bk, c=c: e.matmul(PS[bk][:], ONEB, sq[:, c, :], start=(c == 0), stop=(c == 7)),
                             reads=["sq", "cDb"], writes=[("ps", bk)], sig=(c == 7))
                    P.op("act", lambda e, bk=bk: e.activation(out=rs[:], in_=PS[bk][:], func=AF.Ln, scale=1.0 / D, bias=EPSC),
                         reads=[("ps", bk)], writes=["rs"])
                    P.op("act", lambda e: e.activation(out=rs[:], in_=rs[:], func=AF.Exp, scale=-0.5), reads=["rs"], writes=["rs"])
                    for c in range(8):
                        P.op("dve", lambda e, c=c, b=b: e.scalar_tensor_tensor(
                            ho[b][:, c, :], xi[b][:, c, :], cA[:, l, goff + c:goff + c + 1], rs[:], ALU.mult, ALU.mult),
                            reads=[("xi", b), "rs", "cA"], writes=[("ho", b, c)])
                    P.dma("sp", hT[:, PAD + t0:PAD + t0 + 512].rearrange("(c p) t -> p c t", p=128), ho[b][:],
                          reads=[("ho", b, c) for c in range(8)])
                run_tiles(S // 512, ld, body)
                P.barrier()

        def phase_a1(l, S):
            with ExitStack() as st:
                W, wk = load_w(st, "wA", w_in[l], 128, 8, 2304, C_AQ)
                hi = [sb(st, "ahi%d" % i, (128, 8, 512), BF16) for i in range(2)]
                sq = [sb(st, "asq%d" % i, (128, 512), BF16) for i in range(2)]
                rs = [sb(st, "ars%d" % i, (128, 512), F32) for i in range(2)]
                qo = [sb(st, "aqo%d" % i, (128, 12, 512), BF16) for i in range(2)]
                vo = [sb(st, "avo%d" % i, (128, 4, 768), BF16) for i in range(2)]

                def ld(it):
                    P.dma("sp", hi[it % 2][:], hT[:, PAD + it * 512:PAD + it * 512 + 512].rearrange("(c p) t -> p c t", p=128),
                          writes=[("hi", it % 2)])

                def group(b, oc):
                    bk = bank()
                    for k in range(8):
                        P.op("pe", lambda e, bk=bk, k=k, oc=oc, b=b: e.matmul(
                            PS[bk][:], W[:, k, oc * 128:(oc + 1) * 128], hi[b][:, k, :], start=(k == 0), stop=(k == 7)),
                            reads=[("hi", b)] + wk, writes=[("ps", bk)], sig=(k == 7))
                    s = oc % 2
                    P.op("act", lambda e, bk=bk, s=s: e.activation(out=sq[s][:], in_=PS[bk][:], func=AF.Square),
                         reads=[("ps", bk)], writes=[("sq", s)])
                    return bk

                def norm(b, oc, bk):
                    s = oc % 2
                    bk2 = bank()
                    P.op("pe", lambda e, bk2=bk2, s=s: e.matmul(PS[bk2][:], BDB, sq[s][:], start=True, stop=True),
                         reads=[("sq", s), "cDb"], writes=[("ps", bk2)])
                    P.op("act", lambda e, bk2=bk2, s=s: e.activation(out=rs[s][:], in_=PS[bk2][:], func=AF.Ln, scale=1.0 / 64, bias=EPSC),
                         reads=[("ps", bk2)], writes=[("rs", s)])
                    P.op("act", lambda e, s=s: e.activation(out=rs[s][:], in_=rs[s][:], func=AF.Exp, scale=-0.5), reads=[("rs", s)], writes=[("rs", s)])
                    gcol = 24 + (0 if oc < 6 else 1)
                    P.op("dve", lambda e, bk=bk, s=s, oc=oc, b=b, gcol=gcol: e.scalar_tensor_tensor(
                        qo[b][:, oc, :], PS[bk][:], cA[:, l, gcol:gcol + 1], rs[s][:], ALU.mult, ALU.mult),
                        reads=[("ps", bk), ("rs", s), "cA"], writes=[("qo", b, oc)])

                def body(it):
                    b = it % 2
                    t0 = it * 512
                    prev = None
                    for oc in range(12):
                        bk = group(b, oc)
                        if prev is not None:
                            norm(b, prev[0], prev[1])
                        prev = (oc, bk)
                    vgroups = [(j, n0, nn) for j in range(4) for (n0, nn) in ((0, 512), (512, 256))]
                    for gi, (j, n0, nn) in enumerate(vgroups):
                        bk = bank()
                        for k in range(8):
                            P.op("pe", lambda e, bk=bk, k=k, j=j, n0=n0, nn=nn, b=b: e.matmul(
                                PS[bk][:, 0:nn], hi[b][:, k, j * 128:(j + 1) * 128], W[:, k, 1536 + n0:1536 + n0 + nn],
                                start=(k == 0), stop=(k == 7)),
                                reads=[("hi", b)] + wk, writes=[("ps", bk)], sig=(k == 7))
                        P.op("act", lambda e, bk=bk, j=j, n0=n0, nn=nn, b=b: e.copy(vo[b][:, j, n0:n0 + nn], PS[bk][:, 0:nn]),
                             reads=[("ps", bk)], writes=[("vo", b, j, n0)])
                        if gi == 0:
                            norm(b, prev[0], prev[1])
                            P.dma("sp", qaT[:, t0:t0 + 512].rearrange("(c p) t -> p c t", p=128), qo[b][:, 0:6, :],
                                  reads=[("qo", b, oc) for oc in range(6)])
                            P.dma("sp", kaT[:, t0:t0 + 512].rearrange("(c p) t -> p c t", p=128), qo[b][:, 6:12, :],
                                  reads=[("qo", b, oc) for oc in range(6, 12)])
                    P.dma("sp", va[t0:t0 + 512, :].rearrange("(j p) f -> p j f", p=128), vo[b][:],
                          reads=[("vo", b, j, n0) for j in range(4) for n0 in (0, 512)])
                run_tiles(S // 512, ld, body)
                P.barrier()

        def phase_a2(l, S):
            with ExitStack() as st:
                EBs = [sb(st, "EB%d" % i, (128, 3, 384), F32) for i in range(2)]
                ebv = eb_d.rearrange("p (g q n) -> p g q n", g=3, q=4)
                qs = sb(st, "a2q", (128, 6, 2048), BF16)
                kw = [2048 + 256 * d for d in DILS]
                ks = [sb(st, "a2k%d" % g, (128, 2, kw[g]), BF16) for g in range(3)]
                ntl = [2048 // d // 128 + 2 for d in DILS]
                vs = [sb(st, "a2v%d" % g, (128, ntl[g], DILS[g], 256), BF16) for g in range(3)]
                OD = sb(st, "a2od", (64, 2, 3, 2048), F32)
                DS = sb(st, "a2ds", (64, 2048), F32)
                AO = sb(st, "a2ao", (64, 2, 2048), BF16)
                Ee = [sb(st, "a2e%d" % i, (128, 384), F32) for i in range(3)]
                Pp = [sb(st, "a2p%d" % i, (128, 384), BF16) for i in range(3)]
                cnt = {"un": 0, "ao": 0, "eb": 0}

                def stage_a(U):
                    bk = bank()
                    U["bk"] = bk
                    for jj, ksl in enumerate(U["ksl"]):
                        P.op("pe", lambda e, bk=bk, jj=jj, ksl=ksl, qsl=U["qsl"]: e.matmul(
                            PS[bk][:, jj * 128:(jj + 1) * 128], ksl, qsl, start=True, stop=True),
                            reads=["qs", ("ks", U["g"])], writes=[("ps", bk)], sig=(jj == U["nb"] - 1))

                def stage_b1(U):
                    u, nb_, bk = U["u"], U["nb"], U["bk"]
                    P.op("act", lambda e: e.activation(
                        out=Ee[u][:, 0:128 * nb_], in_=PS[bk][:, 0:128 * nb_], func=AF.Exp, scale=0.125),
                        reads=[("ps", bk)], writes=[("Ee", u)])

                def stage_b2(U):
                    u, nb_, bk = U["u"], U["nb"], U["bk"]
                    eng = "dve"
                    EB, g, jlo = U["EB"], U["g"], U["jlo"]
                    P.op(eng, lambda e: e.tensor_tensor(
                        Pp[u][:, 0:128 * nb_], Ee[u][:, 0:128 * nb_], EB[:, g, jlo * 128:(jlo + nb_) * 128], ALU.mult),
                        reads=[("Ee", u), ("EB", U["ebi"])], writes=[("Pp", u)])

                def stage_c1(U):
                    u, nb_, g = U["u"], U["nb"], U["g"]
                    bk2 = bank()
                    U["bk2"] = bk2
                    for jj, (vsl, vkey) in enumerate(U["vsl"]):
                        P.op("pe", lambda e, jj=jj, vsl=vsl: e.matmul(
                            PS[bk2][0:64, 0:128], vsl, Pp[u][:, jj * 128:(jj + 1) * 128],
                            start=(jj == 0), stop=(jj == nb_ - 1)),
                            reads=[("Pp", u), vkey], writes=[("ps", bk2)], sig=False)
                    for jj in range(nb_):
                        P.op("pe", lambda e, jj=jj: e.matmul(
                            PS[bk2][0:64, 128:256], ONEB[:, 0:64], Pp[u][:, jj * 128:(jj + 1) * 128],
                            start=(jj == 0), stop=(jj == nb_ - 1)),
                            reads=[("Pp", u), "cDb"], writes=[("ps", bk2)], sig=(jj == nb_ - 1))

                def stage_c2(U):
                    osl, bk2, g = U["osl"], U["bk2"], U["g"]
                    P.op("act", lambda e: e.copy(osl, PS[bk2][0:64, 0:256].rearrange("p (a b) -> p a b", a=2)),
                         reads=[("ps", bk2)], writes=[("OD", g)])

                for sbi in range(S // 2048):
                    T0 = sbi * 2048
                    P.dma("sp", qs[:], qaT[:, T0:T0 + 2048].rearrange("(c p) t -> p c t", p=128), writes=["qs"])
                    K0s, mt0s = [], []
                    for g in range(3):
                        d = DILS[g]
                        K0 = max(0, T0 - 128 * d)
                        K1 = min(S, T0 + 2048 + 128 * d)
                        K0s.append(K0)
                        P.dma("sp", ks[g][:, :, 0:K1 - K0],
                              kaT[256 * g:256 * g + 256, K0:K1].rearrange("(c p) t -> p c t", p=128), writes=[("ks", g)])
                        m_lo = max(0, T0 // d // 128 - 1)
                        m_hi = min(S // d // 128, (T0 + 2048) // d // 128 + 1)
                        mt0s.append(m_lo)
                        for mm in range(m_lo, m_hi):
                            P.dma("sp", vs[g][:, mm - m_lo, :, :],
                                  va[mm * 128 * d:(mm + 1) * 128 * d, 256 * g:256 * g + 256].rearrange("(i r) f -> i r f", r=d),
                                  writes=[("vs", g, mm - m_lo)])
                    for hh in range(4):
                        ebi = cnt["eb"] % 2
                        cnt["eb"] += 1
                        EB = EBs[ebi]
                        P.dma("sp", EB[:], ebv[:, :, hh, :], writes=[("EB", ebi)])
                        units = []
                        for g in range(3):
                            if os.environ.get("A2G") and str(g) not in os.environ["A2G"]:
                                continue
                            d = DILS[g]
                            h = 4 * g + hh
                            c = h // 2
                            pb = 64 * (h % 2)
                            Lt = S // d // 128
                            for r in range(d):
                                for j in range(2048 // d // 128):
                                    m = T0 // d // 128 + j
                                    tiles = [mm for mm in (m - 1, m, m + 1) if 0 <= mm < Lt]
                                    U = {"g": g, "nb": len(tiles), "jlo": tiles[0] - (m - 1), "EB": EB, "ebi": ebi,
                                         "u": cnt["un"] % 3, "n": cnt["un"]}
                                    cnt["un"] += 1
                                    U["qsl"] = qs[pb:pb + 64, c, ssl(r + 128 * j * d, 128, d)]
                                    U["ksl"] = [ks[g][pb:pb + 64, c - 2 * g, ssl(mm * 128 * d + r - K0s[g], 128, d)] for mm in tiles]
                                    U["vsl"] = [(vs[g][:, mm - mt0s[g], r, 64 * hh:64 * hh + 64], ("vs", g, mm - mt0s[g])) for mm in tiles]
                                    U["osl"] = OD[:, :, g, ssl(r + 128 * j * d, 128, d)]
                                    units.append(U)
                        NU = len(units)
                        stage_a(units[0])
                        if NU > 1:
                            stage_a(units[1])
                        stage_b1(units[0])
                        stage_b2(units[0])
                        for i, U in enumerate(units):
                            if i + 2 < NU:
                                stage_a(units[i + 2])
                            if i + 1 < NU:
                                stage_b1(units[i + 1])
                            stage_c1(U)
                            if i + 1 < NU:
                                stage_b2(units[i + 1])
                            stage_c2(U)
                        P.op("dve", lambda e: e.tensor_tensor(DS[:], OD[:, 1, 0, :], OD[:, 1, 1, :], ALU.add),
                             reads=[("OD", 0), ("OD", 1)], writes=["DS"])
                        P.op("dve", lambda e: e.tensor_tensor(DS[:], DS[:], OD[:, 1, 2, :], ALU.add),
                             reads=["DS", ("OD", 2)], writes=["DS"])
                        P.op("act", lambda e: e.activation(out=DS[:], in_=DS[:], func=AF.Ln), reads=["DS"], writes=["DS"])
                        P.op("act", lambda e: e.activation(out=DS[:], in_=DS[:], func=AF.Exp, scale=-1.0), reads=["DS"], writes=["DS"])
                        for g in range(3):
                            ai = cnt["ao"] % 2
                            cnt["ao"] += 1
                            P.op("dve", lambda e, g=g, ai=ai: e.tensor_tensor(AO[:, ai, :], OD[:, 0, g, :], DS[:], ALU.mult),
                                 reads=["DS", ("OD", g)], writes=[("AO", ai)])
                            P.dma("sp", attT[64 * (4 * g + hh):64 * (4 * g + hh) + 64, T0:T0 + 2048], AO[:, ai, :],
                                  reads=[("AO", ai)])
                P.barrier()

        def phase_m1a(l, S):
            with ExitStack() as st:
                W, wk = load_w(st, "wMa", w_in[l], 128, 8, 1536, C_MQ)
                hi = [sb(st, "mhi%d" % i, (128, 8, 512), BF16) for i in range(2)]
                tmp = [sb(st, "mtmp%d" % i, (128, 512), F32) for i in range(4)]
                qo = [sb(st, "mqo%d" % i, (128, 12, 512), BF16) for i in range(2)]
                tl = [(t, min(510, S - t)) for t in range(0, S, 510)]

                def ld(it):
                    t0, ntok = tl[it]
                    P.dma("sp", hi[it % 2][:, :, 0:ntok + 2],
                          hT[:, PAD + t0 - 1:PAD + t0 + 1 + ntok].rearrange("(c p) t -> p c t", p=128), writes=[("hi", it % 2)])

                def body(it):
                    t0, ntok = tl[it]
                    ncol = ntok + 2
                    b = it % 2
                    pend = None
                    for oc in range(12):
                        bk = bank()
                        for k in range(8):
                            P.op("pe", lambda e, bk=bk, k=k, oc=oc: e.matmul(
                                PS[bk][:, 0:ncol], W[:, k, oc * 128:(oc + 1) * 128], hi[b][:, k, 0:ncol],
                                start=(k == 0), stop=(k == 7)),
                                reads=[("hi", b)] + wk, writes=[("ps", bk)], sig=(k == 7))
                        u = oc % 4
                        w0, w1, w2 = (cM[:, l, jj * 12 + oc:jj * 12 + oc + 1] for jj in range(3))
                        bb = cM[:, l, 36 + oc:37 + oc]
                        P.op("act", lambda e, bk=bk, u=u, w1=w1, bb=bb: e.activation(
                            out=tmp[u][:, 0:ntok], in_=PS[bk][:, 1:1 + ntok], func=AF.Identity, scale=w1, bias=bb),
                            reads=[("ps", bk), "cM"], writes=[("tmp", u)])
                        if pend is not None:
                            pend()
                        P.op("dve", lambda e, bk=bk, u=u, w0=w0: e.scalar_tensor_tensor(
                            tmp[u][:, 0:ntok], PS[bk][:, 0:ntok], w0, tmp[u][:, 0:ntok], ALU.mult, ALU.add),
                            reads=[("ps", bk), ("tmp", u), "cM"], writes=[("tmp", u)])
                        P.op("dve", lambda e, bk=bk, u=u, w2=w2: e.scalar_tensor_tensor(
                            tmp[u][:, 0:ntok], PS[bk][:, 2:2 + ntok], w2, tmp[u][:, 0:ntok], ALU.mult, ALU.add),
                            reads=[("ps", bk), ("tmp", u), "cM"], writes=[("tmp", u)])

                        def pend(u=u, oc=oc):
                            P.op("act", lambda e: e.activation(out=qo[b][:, oc, 0:ntok], in_=tmp[u][:, 0:ntok], func=AF.Silu),
                                 reads=[("tmp", u)], writes=[("qo", b, oc)])
                    pend()
                    P.dma("sp", mqT[:, t0:t0 + ntok].rearrange("(c p) t -> p c t", p=128), qo[b][:, 0:6, 0:ntok],
                          reads=[("qo", b, oc) for oc in range(6)])
                    P.dma("sp", mkT[:, t0:t0 + ntok].rearrange("(c p) t -> p c t", p=128), qo[b][:, 6:12, 0:ntok],
                          reads=[("qo", b, oc) for oc in range(6, 12)])
                run_tiles(len(tl), ld, body)
                P.barrier()

        def phase_m1b(l, S):
            with ExitStack() as st:
                W, wk = load_w(st, "wMb", w_in[l], 128, 8, 1552, C_MV)
                cCl = sb(st, "cCl", (128, 16), F32)
                P.dma("sp", cCl[:], cC_d[:, l * NCC:l * NCC + 16], writes=["cCl"])
                hi = [sb(st, "bhi%d" % i, (128, 8, 512), BF16) for i in range(2)]
                vo = [sb(st, "bvo%d" % i, (128, 4, 768), BF16) for i in range(2)]
                oo = [sb(st, "boo%d" % i, (128, 4, 768), F32) for i in range(2)]
                go = [sb(st, "bgo%d" % i, (128, 4, 16), F32) for i in range(2)]
                gt = sb(st, "bgt", (128, 4, 8), F32)
                def ld(it):
                    P.dma("sp", hi[it % 2][:], hT[:, PAD + it * 512:PAD + it * 512 + 512].rearrange("(c p) t -> p c t", p=128),
                          writes=[("hi", it % 2)])

                def body(it):
                    b = it % 2
                    t0 = it * 512
                    bkg = bank()
                    for j in range(4):
                        for k in range(8):
                            P.op("pe", lambda e, k=k, j=j, b=b, bkg=bkg: e.matmul(
                                PS[bkg][:, j * 16:(j + 1) * 16], hi[b][:, k, j * 128:(j + 1) * 128], W[:, k, 1536:1552],
                                start=(k == 0), stop=(k == 7)),
                                reads=[("hi", b)] + wk, writes=[("ps", bkg)], sig=(k == 7))
                    for j in range(4):
                        P.op("dve", lambda e, j=j, b=b, bkg=bkg: e.tensor_tensor(
                            go[b][:, j, :], PS[bkg][:, j * 16:(j + 1) * 16], cCl[:], ALU.add),
                            reads=[("ps", bkg), "cCl"], writes=[("go", b)])
                    for half in range(2):
                        src = go[b][:, :, 8 * half + 4:8 * half + 8]
                        P.op("act", lambda e, src=src, half=half: e.activation(out=gt[:, :, 4 * half:4 * half + 4], in_=src, func=AF.Exp, scale=-1.0),
                             reads=[("go", b)], writes=["gt"])
                    P.op("act", lambda e: e.activation(out=gt[:], in_=gt[:], func=AF.Ln, bias=ONEC, scale=1.0),
                         reads=["gt", "eps1"], writes=["gt"])
                    for half in range(2):
                        dst = go[b][:, :, 8 * half + 4:8 * half + 8]
                        P.op("dve", lambda e, dst=dst, half=half: e.tensor_scalar(dst, gt[:, :, 4 * half:4 * half + 4], -1.0, None, ALU.mult),
                             reads=["gt"], writes=[("go", b)])
                    P.dma("sp", gts[t0:t0 + 512, :].rearrange("(j p) f -> p j f", p=128), go[b][:], reads=[("go", b)])
                    for j in range(4):
                        for gi, (n0, nn) in enumerate(((0, 512), (512, 256), (768, 512), (1280, 256))):
                            bk = bank()
                            for k in range(8):
                                P.op("pe", lambda e, bk=bk, k=k, j=j, n0=n0, nn=nn, b=b: e.matmul(
                                    PS[bk][:, 0:nn], hi[b][:, k, j * 128:(j + 1) * 128], W[:, k, n0:n0 + nn],
                                    start=(k == 0), stop=(k == 7)),
                                    reads=[("hi", b)] + wk, writes=[("ps", bk)], sig=(k == 7))
                            if gi < 2:
                                if gi == 0:
                                    P.op("dve", lambda e, bk=bk, j=j, n0=n0, nn=nn, b=b: e.tensor_copy(vo[b][:, j, n0:n0 + nn], PS[bk][:, 0:nn]),
                                         reads=[("ps", bk)], writes=[("vo", b, j, gi)])
                                else:
                                    P.op("act", lambda e, bk=bk, j=j, n0=n0, nn=nn, b=b: e.copy(vo[b][:, j, n0:n0 + nn], PS[bk][:, 0:nn]),
                                         reads=[("ps", bk)], writes=[("vo", b, j, gi)])
                            else:
                                P.op("act", lambda e, bk=bk, j=j, n0=n0, nn=nn, b=b: e.activation(
                                    out=oo[b][:, j, n0 - 768:n0 - 768 + nn], in_=PS[bk][:, 0:nn], func=AF.Sigmoid),
                                    reads=[("ps", bk)], writes=[("oo", b, j, gi)])
                    P.dma("sp", mv[t0:t0 + 512, :].rearrange("(j p) f -> p j f", p=128), vo[b][:],
                          reads=[("vo", b, j, gi) for j in range(4) for gi in range(2)])
                    P.dma("sp", mo[t0:t0 + 512, :].rearrange("(j p) f -> p j f", p=128), oo[b][:],
                          reads=[("oo", b, j, gi) for j in range(4) for gi in (2, 3)])
                run_tiles(S // 512, ld, body)
                P.barrier()

        def phase_scan(l, S, fwd):
            with ExitStack() as st:
                gofs = 0 if fwd else 8
                MASK = TRIF if fwd else TRIB
                qg = [sb(st, "sq%d" % i, (96, 8, 512), BF16) for i in range(2)]
                kg = [sb(st, "sk%d" % i, (96, 8, 512), BF16) for i in range(2)]
                vg = [sb(st, "sv%d" % i, (128, 4, 4, 194), BF16) for i in range(2)]
                gg = [sb(st, "sg%d" % i, (128, 4, 16), F32) for i in range(2)]
                for i in range(2):
                    P.op("pool", lambda e, i=i: e.memset(vg[i][:, :, :, 192:194], 1.0), writes=[("vg1", i)])
                Cs = sb(st, "sC", (96, 4, 2, 194), F32)
                Cb = sb(st, "sCb", (96, 4, 2, 194), BF16)
                P.op("dve", lambda e: e.memset(Cs[:], 0.0), writes=[("C", h) for h in range(4)])
                P.op("pool", lambda e: e.memset(Cb[:], 0.0), writes=[("Cb", h) for h in range(4)])
                sm = [sb(st, "ssm%d" % i, (128, 20), F32) for i in range(3)]
                smb = [sb(st, "ssmb%d" % i, (128, 8), F32) for i in range(3)]
                kgs = [sb(st, "skgs%d" % i, (128, 768), BF16) for i in range(2)]
                smt = [sb(st, "ssmt%d" % i, (128, 128), BF16) for i in range(4)]
                t1 = [sb(st, "st1%d" % i, (128, 4), F32) for i in range(2)]
                ho = [sb(st, "sho%d" % i, (128, 4, 768), F32) for i in range(2)]
                if fwd:
                    hbg = [sb(st, "shb%d" % i, (128, 4, 768), F32) for i in range(2)]
                    mog = [sb(st, "smo%d" % i, (128, 4, 768), F32) for i in range(2)]
                    MHG = sb(st, "smhg", (128, 768), F32)
                    P.dma("sp", MHG[:], cC_d[:, l * NCC + 16:l * NCC + 16 + 768], writes=["MHG"])
                    ss = [sb(st, "sss%d" % i, (128, 4), F32) for i in range(2)]
                    hbf = [sb(st, "shbf%d" % i, (128, 768), BF16) for i in range(2)]
                    hmo = [sb(st, "shmo%d" % i, (128, 6, 512), BF16) for i in range(2)]
                ng = S // 512
                order = list(range(ng)) if fwd else list(range(ng - 1, -1, -1))

                def ld(gi_):
                    b = gi_ % 2
                    T0 = order[gi_] * 512
                    P.dma("sp", qg[b][:], mqT[:, T0:T0 + 512].rearrange("(c p) t -> p c t", p=96), writes=[("qg", b)])
                    P.dma("sp", kg[b][:], mkT[:, T0:T0 + 512].rearrange("(c p) t -> p c t", p=96), writes=[("kg", b)])
                    for j in range(4):
                        P.dma("sp", vg[b][:, j, :, 0:192], mv[T0 + j * 128:T0 + (j + 1) * 128, :].rearrange("p (h f) -> p h f", h=4),
                              writes=[("vg", b, j)])
                    P.dma("sp", gg[b][:], gts[T0:T0 + 512, :].rearrange("(j p) f -> p j f", p=128), writes=[("gg", b)])
                    if fwd:
                        P.dma("sp", hbg[b][:], hb[T0:T0 + 512, :].rearrange("(j p) f -> p j f", p=128), writes=[("hbg", b)])
                        P.dma("sp", mog[b][:], mo[T0:T0 + 512, :].rearrange("(j p) f -> p j f", p=128), writes=[("mog", b)])

                def chunk_s1(b, j, cn):
                    cs = slice(j * 128, (j + 1) * 128)
                    u = cn % 3
                    kb = cn % 2
                    smu = sm[u]
                    bkA = bank()
                    lf = gg[b][:, j, gofs + 4:gofs + 8]
                    li = gg[b][:, j, gofs:gofs + 4]
                    P.op("pe", lambda e: e.matmul(PS[bkA][:, 0:4], MASK, lf, start=True, stop=True),
                         reads=[("gg", b), "cD"], writes=[("ps", bkA)], sig=False)
                    P.op("pe", lambda e: e.matmul(PS[bkA][:, 4:8], ONEF, lf, start=True, stop=True),
                         reads=[("gg", b), "cD"], writes=[("ps", bkA)])
                    smb_ = smb[u]
                    P.op("act", lambda e: e.copy(smb_[:], PS[bkA][:, 0:8]), reads=[("ps", bkA)], writes=[("smb", u)])
                    P.op("dve", lambda e: e.tensor_tensor(smu[:, 0:4], li, smb_[:, 0:4], ALU.subtract),
                         reads=[("smb", u), ("gg", b)], writes=[("sm", u, 0)])
                    P.op("act", lambda e: e.activation(out=smu[:, 8:12], in_=smb_[:, 0:4], func=AF.Exp, bias=LNC, scale=1.0),
                         reads=[("smb", u), "eps2"], writes=[("sm", u, 2)])
                    P.op("act", lambda e: e.activation(out=smu[:, 12:16], in_=smb_[:, 4:8], func=AF.Exp),
                         reads=[("smb", u)], writes=[("sm", u, 3)])
                    P.op("act", lambda e: e.activation(out=smu[:, 4:8], in_=smu[:, 0:4], func=AF.Exp),
                         reads=[("sm", u, 0)], writes=[("sm", u, 1)])
                    P.op("dve", lambda e: e.tensor_tensor(smu[:, 16:20], smu[:, 4:8], smu[:, 12:16], ALU.mult),
                         reads=[("sm", u, 1), ("sm", u, 3)], writes=[("sm", u, 4)])
                    for c in range(8):
                        P.op("pe", lambda e, c=c: e.transpose(PSB[:, c * 96:(c + 1) * 96], kg[b][:, c, cs], IDB[0:96, 0:96]),
                             reads=[("kg", b), "cDb"], writes=["psb"], sig=(c == 7))
                    for h in range(4):
                        P.op("act", lambda e, h=h: e.mul(kgs[kb][:, h * 192:(h + 1) * 192], PSB[:, h * 192:(h + 1) * 192], smu[:, 16 + h:17 + h]),
                             reads=["psb", ("sm", u, 4)], writes=[("kgs", kb, h)])

                def chunk_rest(b, j, cn, gi_, last):
                    cs = slice(j * 128, (j + 1) * 128)
                    u = cn % 3
                    kb = cn % 2
                    smu = sm[u]
                    tt = t1[cn % 2]
                    bS = []
                    for h in range(4):
                        bkS = bank()
                        bS.append(bkS)
                        for cc in range(2):
                            P.op("pe", lambda e, bkS=bkS, cc=cc, h=h: e.matmul(
                                PS[bkS][:, 0:128], kg[b][:, 2 * h + cc, cs], qg[b][:, 2 * h + cc, cs], start=(cc == 0), stop=(cc == 1)),
                                reads=[("kg", b), ("qg", b)], writes=[("ps", bkS)], sig=(cc == 1))
                    for h in range(4):
                        P.op("dve", lambda e, h=h: e.scalar_tensor_tensor(
                            smt[h][:], PS[bS[h]][:, 0:128], smu[:, 4 + h:5 + h], MASK, ALU.mult, ALU.mult),
                            reads=[("ps", bS[h]), ("sm", u, 1), "cD"], writes=[("smt", h)])
                    bN = []
                    for h in range(4):
                        bkN = bank()
                        bN.append(bkN)
                        P.op("pe", lambda e, bkN=bkN, h=h: e.matmul(
                            PS[bkN][:, 0:193], smt[h][:], vg[b][:, j, h, 0:193], start=True, stop=False),
                            reads=[("smt", h), ("vg", b, j), ("vg1", b)], writes=[("ps", bkN)], sig=False)
                        for cc in range(2):
                            P.op("pe", lambda e, bkN=bkN, cc=cc, h=h: e.matmul(
                                PS[bkN][:, 0:193], qg[b][:, 2 * h + cc, cs], Cb[:, h, cc, 0:193], start=False, stop=(cc == 1)),
                                reads=[("qg", b), ("Cb", h)], writes=[("ps", bkN)], sig=(cc == 1))
                    if pendc[0] is not None:
                        pendc[0]()
                        pendc[0] = None
                    for h in range(4):
                        P.op("act", lambda e, h=h: e.activation(
                            out=tt[:, h:h + 1], in_=PS[bN[h]][:, 192:193], func=AF.Abs, scale=smu[:, 8 + h:9 + h]),
                            reads=[("ps", bN[h]), ("sm", u, 2)], writes=[("t1", cn % 2)])
                    P.op("dve", lambda e: e.tensor_scalar_max(tt[:], tt[:], 1.0), reads=[("t1", cn % 2)], writes=[("t1", cn % 2)])
                    P.op("dve", lambda e: e.reciprocal(tt[:], tt[:]), reads=[("t1", cn % 2)], writes=[("t1", cn % 2)])
                    P.op("dve", lambda e: e.tensor_tensor(tt[:], tt[:], smu[:, 8:12], ALU.mult),
                         reads=[("t1", cn % 2), ("sm", u, 2)], writes=[("t1", cn % 2)])
                    for h in range(4):
                        P.op("act", lambda e, h=h: e.mul(ho[b][:, j, h * 192:(h + 1) * 192], PS[bN[h]][:, 0:192], tt[:, h:h + 1]),
                             reads=[("ps", bN[h]), ("t1", cn % 2)], writes=[("ho", b, j, h)])
                    for h in range(4):
                        bkC = bank()
                        for cc in range(2):
                            P.op("pe", lambda e, bkC=bkC, cc=cc, h=h: e.matmul(
                                PS[bkC][0:96, cc * 193:(cc + 1) * 193], kgs[kb][:, h * 192 + cc * 96:h * 192 + (cc + 1) * 96], vg[b][:, j, h, 0:193],
                                start=True, stop=True),
                                reads=[("kgs", kb, h), ("vg", b, j), ("vg1", b)], writes=[("ps", bkC)], sig=(cc == 1))
                        Cv = Cs[:, h, :, 0:193]
                        P.op("dve", lambda e, bkC=bkC, Cv=Cv, h=h: e.scalar_tensor_tensor(
                            Cv, Cv, smu[0:96, 12 + h:13 + h], PS[bkC][0:96, 0:386].rearrange("p (a b) -> p a b", a=2), ALU.mult, ALU.add),
                            reads=[("ps", bkC), ("C", h), ("sm", u, 3)], writes=[("C", h)])
                        P.op("act", lambda e, h=h: e.copy(Cb[:, h, :, :], Cs[:, h, :, :]),
                             reads=[("C", h)], writes=[("Cb", h)])
                    if fwd:
                        def comb(b=b, j=j, cn=cn, cs=cs, last=last, gi_=gi_):
                            hv = ho[b][:, j, :]
                            si_ = cn % 2
                            hk = [("ho", b, j, h) for h in range(4)]
                            P.op("dve", lambda e: e.tensor_tensor(hv, hv, hbg[b][:, j, :], ALU.add),
                                 reads=hk + [("hbg", b)], writes=hk)
                            for h in range(4):
                                P.op("act", lambda e, h=h: e.activation(
                                    out=hbf[si_][:, h * 192:(h + 1) * 192], in_=hv[:, h * 192:(h + 1) * 192], func=AF.Square, accum_out=ss[si_][:, h:h + 1]),
                                    reads=[("ho", b, j, h)], writes=[("ss", si_), ("hbf", si_)])
                            P.op("act", lambda e: e.activation(out=ss[si_][:], in_=ss[si_][:], func=AF.Ln, scale=1.0 / 192, bias=EPSC),
                                 reads=[("ss", si_), "eps"], writes=[("ss", si_)])
                            P.op("act", lambda e: e.activation(out=ss[si_][:], in_=ss[si_][:], func=AF.Exp, scale=-0.5), reads=[("ss", si_)], writes=[("ss", si_)])
                            for h in range(4):
                                P.op("dve", lambda e, h=h: e.scalar_tensor_tensor(
                                    hv[:, h * 192:(h + 1) * 192], hv[:, h * 192:(h + 1) * 192], ss[si_][:, h:h + 1],
                                    MHG[:, h * 192:(h + 1) * 192], ALU.mult, ALU.mult),
                                    reads=[("ho", b, j, h), ("ss", si_), "MHG"], writes=[("ho", b, j, h)])
                            P.op("dve", lambda e: e.tensor_tensor(hbf[si_][:], hv, mog[b][:, j, :], ALU.mult),
                                 reads=hk + [("mog", b)], writes=[("hbf", si_)])
                            for c in range(6):
                                P.op("pe", lambda e, c=c: e.transpose(PSB2[:, c * 128:(c + 1) * 128], hbf[si_][:, c * 128:(c + 1) * 128], IDB),
                                     reads=[("hbf", si_), "cDb"], writes=["psb2"], sig=(c == 5))
                            P.op("act", lambda e: e.copy(hmo[b][:, :, cs], PSB2[:, 0:768].rearrange("p (a b) -> p a b", a=6)),
                                 reads=["psb2"], writes=[("hmo", b, j)])
                            if last:
                                store(gi_)
                        pendc[0] = comb

                def store(gi_):
                    b = gi_ % 2
                    T0 = order[gi_] * 512
                    if fwd:
                        P.dma("sp", hmT[:, T0:T0 + 512].rearrange("(c p) t -> p c t", p=128), hmo[b][:],
                              reads=[("hmo", b, j) for j in range(4)])
                    else:
                        P.dma("sp", hb[T0:T0 + 512, :].rearrange("(j p) f -> p j f", p=128), ho[b][:],
                              reads=[("ho", b, j, h) for j in range(4) for h in range(4)])

                chunks = []
                for gi_ in range(ng):
                    for ji, j in enumerate(range(4) if fwd else range(3, -1, -1)):
                        chunks.append((gi_, gi_ % 2, j, gi_ * 4 + ji, ji))
                pendc = [None]
                ld(0)
                chunk_s1(chunks[0][1], chunks[0][2], chunks[0][3])
                for i, (gi_, b, j, cn, ji) in enumerate(chunks):
                    if ji == 1 and gi_ + 1 < ng:
                        ld(gi_ + 1)
                    if i + 1 < len(chunks) and not (chunks[i + 1][4] == 0 and ji == 3 and False):
                        nx = chunks[i + 1]
                        if not (nx[4] == 0 and nx[0] > 0 and ji != 3):
                            pass
                        chunk_s1(nx[1], nx[2], nx[3])
                    chunk_rest(b, j, cn, gi_, ji == 3)
                    if ji == 3 and not fwd:
                        store(gi_)
                if pendc[0] is not None:
                    pendc[0]()
                    pendc[0] = None
                P.barrier()

        def phase_x1(l, S, si):
            with ExitStack() as st:
                Wkv, wkk = load_w(st, "wKV", w_kv[l], 128, 8, 1536, 0)
                Wq, wqk = load_w(st, "wXQ", w_in[l], 128, 8, 768, C_XQ)
                xkT = sb(st, "xkT", (96, 8, 256), BF16)
                xv = sb(st, "xv", (128, 2, 768), BF16)
                with ExitStack() as st2:
                    mt = sb(st2, "xmt", (128, 2, D), F32)
                    msq = sb(st2, "xmsq", (128, 2, D), BF16)
                    mss = sb(st2, "xmss", (128, 2), F32)
                    memT = sb(st2, "xmemT", (128, 8, 256), BF16)
                    ksq = sb(st2, "xksq", (96, 2, 256), BF16)
                    krs = sb(st2, "xkrs", (96, 256), F32)
                    P.dma("sp", mt[:], mems[si].rearrange("(j p) f -> p j f", p=128), writes=["mt"])
                    for j in range(2):
                        P.op("act", lambda e, j=j: e.activation(out=msq[:, j, :], in_=mt[:, j, :], func=AF.Square, accum_out=mss[:, j:j + 1]),
                             reads=["mt"], writes=["mss", ("msq", j)])
                    P.op("act", lambda e: e.activation(out=mss[:], in_=mss[:], func=AF.Sqrt, scale=1.0 / D, bias=EPSC),
                         reads=["mss", "eps"], writes=["mss"])
                    P.op("dve", lambda e: e.reciprocal(mss[:], mss[:]), reads=["mss"], writes=["mss"])
                    for j in range(2):
                        P.op("dve", lambda e, j=j: e.tensor_scalar(mt[:, j, :], mt[:, j, :], mss[:, j:j + 1], None, ALU.mult),
                             reads=["mt", "mss"], writes=[("mtn", j)])
                    for c in range(8):
                        bk = bank()
                        for j in range(2):
                            P.op("pe", lambda e, bk=bk, j=j, c=c: e.transpose(PS[bk][:, j * 128:(j + 1) * 128], mt[:, j, c * 128:(c + 1) * 128], IDF),
                                 reads=[("mtn", j), "cD"], writes=[("ps", bk)], sig=(j == 1))
                        P.op("dve", lambda e, bk=bk, c=c: e.tensor_scalar(memT[:, c, :], PS[bk][:, 0:256], cA[:, l, 16 + c:17 + c], None, ALU.mult),
                             reads=[("ps", bk), "cA"], writes=[("memT", c)])
                    mTk = [("memT", c) for c in range(8)]
                    for h in range(4):
                        bq = []
                        for cc in range(2):
                            bk = bank()
                            bq.append(bk)
                            for k in range(8):
                                P.op("pe", lambda e, bk=bk, k=k, h=h, cc=cc: e.matmul(
                                    PS[bk][0:96, 0:256], Wkv[:, k, (2 * h + cc) * 96:(2 * h + cc + 1) * 96], memT[:, k, :],
                                    start=(k == 0), stop=(k == 7)), reads=mTk + wkk, writes=[("ps", bk)], sig=(k == 7))
                            P.op("act", lambda e, bk=bk, cc=cc: e.activation(out=ksq[:, cc, :], in_=PS[bk][0:96, 0:256], func=AF.Square),
                                 reads=[("ps", bk)], writes=[("ksq", cc)])
                        bk2 = bank()
                        for cc in range(2):
                            P.op("pe", lambda e, bk2=bk2, cc=cc: e.matmul(PS[bk2][0:96, 0:256], ONEB[0:96, 0:96], ksq[:, cc, :], start=(cc == 0), stop=(cc == 1)),
                                 reads=[("ksq", cc), "cDb"], writes=[("ps", bk2)], sig=(cc == 1))
                        P.op("act", lambda e, bk2=bk2: e.activation(out=krs[:], in_=PS[bk2][0:96, 0:256], func=AF.Sqrt, scale=1.0 / 192, bias=EPSC[0:96, :]),
                             reads=[("ps", bk2), "eps"], writes=["krs"])
                        P.op("dve", lambda e: e.reciprocal(krs[:], krs[:]), reads=["krs"], writes=["krs"])
                        for cc in range(2):
                            P.op("dve", lambda e, cc=cc, h=h, bq=bq: e.scalar_tensor_tensor(
                                xkT[:, 2 * h + cc, :], PS[bq[cc]][0:96, 0:256], cB[0:96, l, 2 + cc:3 + cc], krs[:], ALU.mult, ALU.mult),
                                reads=[("ps", bq[cc]), "krs", "cB"], writes=[("xkT", h)])
                    for mc in range(2):
                        for gi, (n0, nn) in enumerate(((0, 512), (512, 256))):
                            bk = bank()
                            for k in range(8):
                                P.op("pe", lambda e, bk=bk, k=k, mc=mc, n0=n0, nn=nn: e.matmul(
                                    PS[bk][:, 0:nn], memT[:, k, mc * 128:(mc + 1) * 128], Wkv[:, k, 768 + n0:768 + n0 + nn],
                                    start=(k == 0), stop=(k == 7)), reads=mTk + wkk, writes=[("ps", bk)], sig=(k == 7))
                            P.op("act", lambda e, bk=bk, mc=mc, n0=n0, nn=nn: e.copy(xv[:, mc, n0:n0 + nn], PS[bk][:, 0:nn]),
                                 reads=[("ps", bk)], writes=[("xv", mc, gi)])
                    P.barrier()
                hi = [sb(st, "xhi%d" % i, (128, 8, 512), BF16) for i in range(2)]
                qsq = sb(st, "xqsq", (96, 2, 512), BF16)
                qrs = sb(st, "xqrs", (96, 512), F32)
                xq = [sb(st, "xxq%d" % i, (96, 2, 512), BF16) for i in range(2)]
                pT = [sb(st, "xpT%d" % i, (128, 2, 512), BF16) for i in range(2)]
                rD = [sb(st, "xrD%d" % i, (96, 512), F32) for i in range(2)]
                xo = [sb(st, "xxo%d" % i, (96, 8, 512), BF16) for i in range(2)]
                def ld(it):
                    P.dma("sp", hi[it % 2][:], hT[:, PAD + it * 512:PAD + it * 512 + 512].rearrange("(c p) t -> p c t", p=128), writes=[("hi", it % 2)])

                qsq2 = [qsq, sb(st, "xqsq2", (96, 2, 512), BF16)]
                nt = S // 512
                heads = [(it, h) for it in range(nt) for h in range(4)]
                H = {}

                def s1(i):
                    it, h = heads[i]
                    b = it % 2
                    qs_ = qsq2[i % 2]
                    bq = []
                    for cc in range(2):
                        bk = bank()
                        pinned.add(bk)
                        bq.append(bk)
                        for k in range(8):
                            P.op("pe", lambda e, bk=bk, k=k, cc=cc: e.matmul(
                                PS[bk][0:96, :], Wq[:, k, (2 * h + cc) * 96:(2 * h + cc + 1) * 96], hi[b][:, k, :],
                                start=(k == 0), stop=(k == 7)), reads=[("hi", b)] + wqk, writes=[("ps", bk)], sig=(k == 7))
                        P.op("act", lambda e, bk=bk, cc=cc: e.activation(out=qs_[:, cc, :], in_=PS[bk][0:96, :], func=AF.Square),
                             reads=[("ps", bk)], writes=[("qsq", i % 2, cc)])
                    H[i] = bq

                def s234(i):
                    it, h = heads[i]
                    b = it % 2
                    r = i % 2
                    qs_ = qsq2[i % 2]
                    bq = H.pop(i)
                    bk2 = bank()
                    for cc in range(2):
                        P.op("pe", lambda e, cc=cc: e.matmul(PS[bk2][0:96, :], ONEB[0:96, 0:96], qs_[:, cc, :], start=(cc == 0), stop=(cc == 1)),
                             reads=[("qsq", i % 2, cc), "cDb"], writes=[("ps", bk2)], sig=(cc == 1))
                    P.op("act", lambda e: e.activation(out=qrs[:], in_=PS[bk2][0:96, :], func=AF.Ln, scale=1.0 / 192, bias=EPSC[0:96, :]),
                         reads=[("ps", bk2), "eps"], writes=["qrs"])
                    P.op("act", lambda e: e.activation(out=qrs[:], in_=qrs[:], func=AF.Exp, scale=-0.5), reads=["qrs"], writes=["qrs"])
                    for cc in range(2):
                        P.op("dve", lambda e, cc=cc: e.scalar_tensor_tensor(
                            xq[r][:, cc, :], PS[bq[cc]][0:96, :], cB[0:96, l, cc:cc + 1], qrs[:], ALU.mult, ALU.mult),
                            reads=[("ps", bq[cc]), "qrs", "cB"], writes=[("xq", r)])
                    for bk in bq:
                        pinned.discard(bk)
                    if i + 1 < len(heads):
                        if heads[i + 1][1] == 0 and heads[i + 1][0] + 1 < nt:
                            ld(heads[i + 1][0] + 1)
                        s1(i + 1)
                    for mc in range(2):
                        bk = bank()
                        for cc in range(2):
                            P.op("pe", lambda e, bk=bk, cc=cc, mc=mc: e.matmul(
                                PS[bk][:, :], xkT[:, 2 * h + cc, mc * 128:(mc + 1) * 128], xq[r][:, cc, :], start=(cc == 0), stop=(cc == 1)),
                                reads=[("xq", r), ("xkT", h)], writes=[("ps", bk)], sig=(cc == 1))
                        P.op("act", lambda e, bk=bk, mc=mc: e.activation(out=pT[r][:, mc, :], in_=PS[bk][:, :], func=AF.Exp, scale=192.0 ** -0.5),
                             reads=[("ps", bk)], writes=[("pT", r, mc)])
                    bo = []
                    for cc in range(2):
                        bk = bank()
                        bo.append(bk)
                        for mc in range(2):
                            P.op("pe", lambda e, bk=bk, cc=cc, mc=mc: e.matmul(
                                PS[bk][0:96, :], xv[:, mc, h * 192 + cc * 96:h * 192 + (cc + 1) * 96], pT[r][:, mc, :], start=(mc == 0), stop=(mc == 1)),
                                reads=[("pT", r, mc), ("xv", mc, 0), ("xv", mc, 1)], writes=[("ps", bk)], sig=(mc == 1))
                    bkD = bank()
                    for mc in range(2):
                        P.op("pe", lambda e, mc=mc: e.matmul(PS[bkD][0:96, :], ONEB[:, 0:96], pT[r][:, mc, :], start=(mc == 0), stop=(mc == 1)),
                             reads=[("pT", r, mc), "cDb"], writes=[("ps", bkD)], sig=(mc == 1))
                    P.op("act", lambda e: e.activation(out=rD[r][:], in_=PS[bkD][0:96, :], func=AF.Ln), reads=[("ps", bkD)], writes=[("rD", r)])
                    P.op("act", lambda e: e.activation(out=rD[r][:], in_=rD[r][:], func=AF.Exp, scale=-1.0), reads=[("rD", r)], writes=[("rD", r)])
                    for cc in range(2):
                        P.op("dve", lambda e, cc=cc: e.tensor_tensor(xo[b][:, 2 * h + cc, :], PS[bo[cc]][0:96, :], rD[r][:], ALU.mult),
                             reads=[("ps", bo[cc]), ("rD", r)], writes=[("xo", b, h, cc)])
                    if h == 3:
                        t0 = it * 512
                        P.dma("sp", xoT[:, t0:t0 + 512].rearrange("(c p) t -> p c t", p=96), xo[b][:],
                              reads=[("xo", b, hh, cc) for hh in range(4) for cc in range(2)])

                ld(0)
                if nt > 1:
                    ld(1)
                s1(0)
                for i in range(len(heads)):
                    s234(i)
                P.barrier()

        def phase_g1(l, S):
            T = 256
            with ExitStack() as st:
                Wg, wgk = load_w(st, "wG", w_in[l], 128, 8, 3072, C_G)
                Wa, wak = load_w(st, "wBa", w_br[l][0:768, :], 128, 6, D)
                Wm, wmk = load_w(st, "wBm", w_br[l][768:1536, :], 128, 6, D)
                Wx, wxk = load_w(st, "wBx", w_br[l][1536:2304, :], 96, 8, D)
                Wo, wok = load_w(st, "wO", w_out[l], 128, 8, D)
                hi = [sb(st, "ghi%d" % i, (128, 8, T), BF16) for i in range(2)]
                ai = [sb(st, "gai%d" % i, (128, 6, T), BF16) for i in range(2)]
                mi = [sb(st, "gmi%d" % i, (128, 6, T), BF16) for i in range(2)]
                xi_ = [sb(st, "gxi%d" % i, (96, 8, T), BF16) for i in range(2)]
                xt = [sb(st, "gxt%d" % i, (128, 8, T), F32) for i in range(3)]
                pend = [None]
                mg = sb(st, "gmg", (128, 8, T), BF16)
                s01 = [sb(st, "gs01%d" % i, (128, 2 * T), F32) for i in range(2)]
                s2 = [sb(st, "gs2%d" % i, (128, T), F32) for i in range(2)]
                mm = [sb(st, "gmm%d" % i, (128, 3, T), F32) for i in range(2)]
                nb = norm_bufs(st, T)
                def ld(it):
                    b = it % 2
                    t0 = it * T
                    P.dma("sp", hi[b][:], hT[:, PAD + t0:PAD + t0 + T].rearrange("(c p) t -> p c t", p=128), writes=[("hi", b)])
                    P.dma("sp", ai[b][:], attT[:, t0:t0 + T].rearrange("(c p) t -> p c t", p=128), writes=[("ai", b)])
                    P.dma("sp", mi[b][:], hmT[:, t0:t0 + T].rearrange("(c p) t -> p c t", p=128), writes=[("mi", b)])
                    P.dma("sp", xi_[b][:], xoT[:, t0:t0 + T].rearrange("(c p) t -> p c t", p=96), writes=[("xi", b)])
                    P.dma("sp", xt[it % 3][:], xT[:, t0:t0 + T].rearrange("(c p) t -> p c t", p=128), writes=[("xt", it % 3)])

                def body(it):
                    b = it % 2
                    x3 = it % 3
                    t0 = it * T
                    for oc in range(8):
                        u = oc % 2
                        if oc == 1 and pend[0] is not None:
                            pend[0]()
                            pend[0] = None
                        ocs = slice(oc * 128, (oc + 1) * 128)
                        bA, bB, bC, bD = bank(), bank(), bank(), bank()
                        slots = [(bA, 0), (bA, T), (bB, 0)]
                        for br in range(3):
                            bk, c0 = slots[br]
                            for k in range(8):
                                P.op("pe", lambda e, bk=bk, c0=c0, k=k, br=br, b=b, ocs=ocs: e.matmul(
                                    PS[bk][:, c0:c0 + T], Wg[:, k, br * D + ocs.start:br * D + ocs.stop], hi[b][:, k, :],
                                    start=(k == 0), stop=(k == 7)), reads=[("hi", b)] + wgk, writes=[("ps", bk)], sig=(k == 7))
                        for k in range(6):
                            P.op("pe", lambda e, k=k, b=b, ocs=ocs, bC=bC: e.matmul(PS[bC][:, 0:T], Wa[:, k, ocs], ai[b][:, k, :], start=(k == 0), stop=(k == 5)),
                                 reads=[("ai", b)] + wak, writes=[("ps", bC)], sig=(k == 5))
                        for k in range(6):
                            P.op("pe", lambda e, k=k, b=b, ocs=ocs, bC=bC: e.matmul(PS[bC][:, T:2 * T], Wm[:, k, ocs], mi[b][:, k, :], start=(k == 0), stop=(k == 5)),
                                 reads=[("mi", b)] + wmk, writes=[("ps", bC)], sig=(k == 5))
                        for k in range(8):
                            P.op("pe", lambda e, k=k, b=b, ocs=ocs, bD=bD: e.matmul(PS[bD][0:128, 0:T], Wx[:, k, ocs], xi_[b][:, k, :], start=(k == 0), stop=(k == 7)),
                                 reads=[("xi", b)] + wxk, writes=[("ps", bD)], sig=(k == 7))
                        P.op("act", lambda e, u=u, bA=bA: e.activation(out=s01[u][:], in_=PS[bA][:, :], func=AF.Sigmoid),
                             reads=[("ps", bA)], writes=[("s01", u)])
                        P.op("act", lambda e, u=u, bB=bB: e.activation(out=s2[u][:], in_=PS[bB][:, 0:T], func=AF.Sigmoid),
                             reads=[("ps", bB)], writes=[("s2", u)])
                        P.op("dve", lambda e, u=u, bC=bC: e.tensor_tensor(mm[u][:, 0, :], PS[bC][:, 0:T], s01[u][:, 0:T], ALU.mult),
                             reads=[("ps", bC), ("s01", u)], writes=[("mm", u, 0)])
                        P.op("dve", lambda e, u=u, bC=bC: e.tensor_tensor(mm[u][:, 1, :], PS[bC][:, T:2 * T], s01[u][:, T:2 * T], ALU.mult),
                             reads=[("ps", bC), ("s01", u)], writes=[("mm", u, 1)])
                        P.op("dve", lambda e, u=u, bD=bD: e.tensor_tensor(mm[u][:, 2, :], PS[bD][:, 0:T], s2[u][:], ALU.mult),
                             reads=[("ps", bD), ("s2", u)], writes=[("mm", u, 2)])
                        P.op("dve", lambda e, u=u: e.tensor_tensor(mm[u][:, 0, :], mm[u][:, 0, :], mm[u][:, 1, :], ALU.add),
                             reads=[("mm", u, 0), ("mm", u, 1)], writes=[("mm", u, 0)])
                        P.op("dve", lambda e, u=u, oc=oc: e.tensor_tensor(mg[:, oc, :], mm[u][:, 0, :], mm[u][:, 2, :], ALU.add),
                             reads=[("mm", u, 0), ("mm", u, 2)], writes=[("mg", oc)])
                    for oc in range(8):
                        bk = bank()
                        for k in range(8):
                            P.op("pe", lambda e, bk=bk, k=k, oc=oc: e.matmul(PS[bk][:, 0:T], Wo[:, k, oc * 128:(oc + 1) * 128], mg[:, k, :], start=(k == 0), stop=(k == 7)),
                                 reads=[("mg", kk) for kk in range(8)] + wok, writes=[("ps", bk)], sig=(k == 7))
                        P.op("dve", lambda e, bk=bk, oc=oc: e.tensor_tensor(xt[x3][:, oc, :], PS[bk][:, 0:T], xt[x3][:, oc, :], ALU.add),
                             reads=[("ps", bk), ("xt", x3)], writes=[("xn", x3, oc)])
                    xk = [("xn", x3, oc) for oc in range(8)] + [("xt", x3)]
                    P.dma("sp", xT[:, t0:t0 + T].rearrange("(c p) t -> p c t", p=128), xt[x3][:], reads=xk)
                    pend[0] = lambda: fused_norm(nb, xt[x3][:], xk, T, l, 8, t0, b)
                run_tiles(S // T, ld, body)
                if pend[0] is not None:
                    pend[0]()
                P.barrier()

        def phase_f1(l, S):
            with ExitStack() as st:
                W, wk = load_w(st, "wUp", w_up[l], 128, 8, 2 * FFN, 0)
                hi = [sb(st, "fhi%d" % i, (128, 8, 512), BF16) for i in range(2)]
                ta = [sb(st, "fta%d" % i, (128, 512), F32) for i in range(3)]
                tv = [sb(st, "ftv%d" % i, (128, 512), F32) for i in range(3)]
                go = [sb(st, "fgo%d" % i, (128, 22, 512), BF16) for i in range(2)]
                tl = [(t, min(510, S - t)) for t in range(0, S, 510)]

                def ld(it):
                    t0, ntok = tl[it]
                    P.dma("sp", hi[it % 2][:, :, 0:ntok + 2],
                          hT[:, PAD + t0 - 1:PAD + t0 + 1 + ntok].rearrange("(c p) t -> p c t", p=128), writes=[("hi", it % 2)])

                def body(it):
                    t0, ntok = tl[it]
                    ncol = ntok + 2
                    b = it % 2
                    pend = None
                    for fc in range(22):
                        u = fc % 3
                        for part, (tt_, key) in enumerate(((ta[u], "ta"), (tv[u], "tv"))):
                            ch = part * 22 + fc
                            bk = bank()
                            for k in range(8):
                                P.op("pe", lambda e, bk=bk, k=k, ch=ch: e.matmul(
                                    PS[bk][:, 0:ncol], W[:, k, ch * 128:(ch + 1) * 128], hi[b][:, k, 0:ncol], start=(k == 0), stop=(k == 7)),
                                    reads=[("hi", b)] + wk, writes=[("ps", bk)], sig=(k == 7))
                            w0, w1, w2 = (cA[:, l, 26 + jj * 44 + ch:26 + jj * 44 + ch + 1] for jj in range(3))
                            bb = cA[:, l, 158 + ch:158 + ch + 1]
                            P.op("act", lambda e, bk=bk, tt_=tt_, w1=w1, bb=bb: e.activation(
                                out=tt_[:, 0:ntok], in_=PS[bk][:, 1:1 + ntok], func=AF.Identity, scale=w1, bias=bb),
                                reads=[("ps", bk), "cA"], writes=[(key, u)])
                            if part == 1 and pend is not None:
                                pend()
                            P.op("dve", lambda e, bk=bk, tt_=tt_, w0=w0: e.scalar_tensor_tensor(
                                tt_[:, 0:ntok], PS[bk][:, 0:ntok], w0, tt_[:, 0:ntok], ALU.mult, ALU.add),
                                reads=[("ps", bk), (key, u), "cA"], writes=[(key, u)])
                            P.op("dve", lambda e, bk=bk, tt_=tt_, w2=w2: e.scalar_tensor_tensor(
                                tt_[:, 0:ntok], PS[bk][:, 2:2 + ntok], w2, tt_[:, 0:ntok], ALU.mult, ALU.add),
                                reads=[("ps", bk), (key, u), "cA"], writes=[(key, u)])

                        def pend(u=u, fc=fc):
                            P.op("act", lambda e: e.activation(out=ta[u][:, 0:ntok], in_=ta[u][:, 0:ntok], func=AF.Gelu_apprx_tanh),
                                 reads=[("ta", u)], writes=[("ta", u)])
                            P.op("pool", lambda e: e.tensor_tensor(go[b][:, fc, 0:ntok], ta[u][:, 0:ntok], tv[u][:, 0:ntok], ALU.mult),
                                 reads=[("ta", u), ("tv", u)], writes=[("go", b, fc)])
                    pend()
                    P.dma("sp", gT[:, t0:t0 + ntok].rearrange("(c p) t -> p c t", p=128), go[b][:, :, 0:ntok],
                          reads=[("go", b, fc) for fc in range(22)])
                run_tiles(len(tl), ld, body)
                P.barrier()

        def phase_f2(l, S, fuse_next):
            with ExitStack() as st:
                W, wk = load_w(st, "wDn", w_dn[l], 128, 22, D, 0)
                gi = [sb(st, "dgi%d" % i, (128, 22, 512), BF16) for i in range(2)]
                xt = [sb(st, "dxt%d" % i, (128, 8, 512), F32) for i in range(3)]
                nb = norm_bufs(st, 512) if fuse_next else None
                pend = [None]

                def ld(it):
                    b = it % 2
                    x3 = it % 3
                    t0 = it * 512
                    P.dma("sp", gi[b][:], gT[:, t0:t0 + 512].rearrange("(c p) t -> p c t", p=128), writes=[("gi", b)])
                    P.dma("sp", xt[x3][:], xT[:, t0:t0 + 512].rearrange("(c p) t -> p c t", p=128), writes=[("xt", x3)])

                def body(it):
                    b = it % 2
                    x3 = it % 3
                    t0 = it * 512
                    for oc in range(8):
                        bk = bank()
                        for k in range(22):
                            P.op("pe", lambda e, bk=bk, k=k, oc=oc: e.matmul(PS[bk][:, :], W[:, k, oc * 128:(oc + 1) * 128], gi[b][:, k, :], start=(k == 0), stop=(k == 21)),
                                 reads=[("gi", b)] + wk, writes=[("ps", bk)], sig=(k == 21))
                        P.op("dve", lambda e, bk=bk, oc=oc: e.tensor_tensor(xt[x3][:, oc, :], PS[bk][:, :], xt[x3][:, oc, :], ALU.add),
                             reads=[("ps", bk), ("xt", x3)], writes=[("xn", x3, oc)])
                        if oc == 1 and pend[0] is not None:
                            pend[0]()
                            pend[0] = None
                    xk = [("xn", x3, oc) for oc in range(8)] + [("xt", x3)]
                    P.dma("sp", xT[:, t0:t0 + 512].rearrange("(c p) t -> p c t", p=128), xt[x3][:], reads=xk)
                    if fuse_next:
                        pend[0] = lambda: fused_norm(nb, xt[x3][:], xk, 512, l + 1, 0, t0, b)
                run_tiles(S // 512, ld, body)
                if pend[0] is not None:
                    pend[0]()
                P.barrier()

        phases = []
        for si, S in enumerate(seq_lens):
            phases.append(("in", lambda si=si, S=S: phase_in(si, S)))
            for l in range(depth):
                if l == 0:
                    phases.append(("n1", lambda l=l, S=S: phase_norm(l, S, 0)))
                phases.append(("a1", lambda l=l, S=S: phase_a1(l, S)))
                phases.append(("a2", lambda l=l, S=S: phase_a2(l, S)))
                phases.append(("m1a", lambda l=l, S=S: phase_m1a(l, S)))
                phases.append(("m1b", lambda l=l, S=S: phase_m1b(l, S)))
                phases.append(("m2", lambda l=l, S=S: phase_scan(l, S, False)))
                phases.append(("m3", lambda l=l, S=S: phase_scan(l, S, True)))
                phases.append(("x1", lambda l=l, S=S, si=si: phase_x1(l, S, si)))
                phases.append(("g1", lambda l=l, S=S: phase_g1(l, S)))
                phases.append(("f1", lambda l=l, S=S: phase_f1(l, S)))
                phases.append(("f2", lambda l=l, S=S: phase_f2(l, S, l + 1 < depth)))
            phases.append(("out", lambda si=si, S=S: phase_out(si, S)))
        for name, fn in phases:
            fn()
            if upto is not None and name == upto:
                break
        P.barrier()
    return nc


def kernel(**inputs):
    f = lambda k: np.ascontiguousarray(np.asarray(inputs[k], dtype=np.float32))
    xp, xsm = f("x_prompt"), f("x_sample")
    mp, ms = f("mem_prompt"), f("mem_sample")
    nc = build([xp.shape[1], xsm.shape[1]], depth=NL)
    cA, cB, cC = pack_params(inputs)
    eb, cd = host_consts()
    common = {"w_in": f("w_in"), "w_mem_kv": f("w_mem_kv"),
              "w_branch": f("w_branch").reshape(NL, 2304, D), "w_out": f("w_out"),
              "w_up": f("w_up"), "w_down": f("w_down"),
              "cA": cA, "cB": cB, "cC": cC, "cEB": eb, "cD": cd, "cM": pack_cm(inputs)}
    n = 8
    nsm = xsm.shape[0]
    in_maps = []
    for c in range(n):
        m = dict(common)
        m["x0"] = xp[c]
        m["mem0"] = mp[c]
        m["x1"] = xsm[c % nsm]
        m["mem1"] = ms[c % nsm]
        in_maps.append(m)
    res = run_bass_kernel_spmd(nc, in_maps, core_ids=list(range(n)))
    y_prompt = np.stack([np.asarray(res.results[c]["y0"], dtype=np.float32) for c in range(n)])
    y_sample = np.stack([np.asarray(res.results[c]["y1"], dtype=np.float32) for c in range(nsm)])
    return (y_prompt, y_sample)
```
